# Optimizing a Trainium2 kernel written in Bass

```python
import math
import jax, jax.numpy as jnp
from jax import lax
import numpy as np

D_MODEL = 1024
BATCH = 32
SEQ = 2048
DEPTH = 1

CTX_LEN = 256
GRID_W = 64
D_INNER = 2 * D_MODEL
D_SSM = D_INNER // 2
D_CONV = D_INNER - D_SSM
SSM_HEAD_DIM = 64
SSM_HEADS = D_SSM // SSM_HEAD_DIM
SSM_GROUPS = 2
SSM_STATE = 128
SSM_CONV_W = 5
SSM_CHUNK = 128
CONF_KERNEL = 31
CONF_CH_GROUP = 64
D_FF = ((int(8 * D_MODEL / 3) + 255) // 256) * 256
GN = SSM_GROUPS * SSM_STATE
OFF_Z = 0
OFF_X = OFF_Z + D_SSM
OFF_B = OFF_X + D_SSM
OFF_C = OFF_B + GN
OFF_DT = OFF_C + GN
OFF_GLU = OFF_DT + 2 * SSM_HEADS
D_IN_PROJ = OFF_GLU + 2 * D_CONV
N_MOD = 9
EPS = 1e-6

kernel_name = "hybrid_ssd_conformer_macaron_dit_block"


def rmsnorm(x, w):
    xf = x.astype(jnp.float32)
    y = xf * lax.rsqrt(jnp.mean(xf * xf, axis=-1, keepdims=True) + EPS)
    return (y * w.astype(jnp.float32)).astype(x.dtype)


def group_rmsnorm(x, w, groups):
    xf = x.astype(jnp.float32).reshape(*x.shape[:-1], groups, x.shape[-1] // groups)
    y = xf * lax.rsqrt(jnp.mean(xf * xf, axis=-1, keepdims=True) + EPS)
    return (y.reshape(x.shape) * w.astype(jnp.float32)).astype(x.dtype)


def layernorm(x, w, b):
    xf = x.astype(jnp.float32)
    mu = jnp.mean(xf, axis=-1, keepdims=True)
    var = jnp.mean(jnp.square(xf - mu), axis=-1, keepdims=True)
    y = (xf - mu) * lax.rsqrt(var + EPS)
    return (y * w.astype(jnp.float32) + b.astype(jnp.float32)).astype(x.dtype)


def modulate(h, shift, scale):
    return h * (1.0 + scale) + shift


def swiglu(h, w_gate, w_up, w_down):
    return (jax.nn.silu(h @ w_gate) * (h @ w_up)) @ w_down


def _flip(t):
    return jnp.flip(t, axis=1)


def dwconv1d(x, w, b):
    k = w.shape[0]
    pad = k // 2
    y = lax.conv_general_dilated(x, w[:, None, :], (1,), [(pad, pad)],
                                 dimension_numbers=("NWC", "WIO", "NWC"),
                                 feature_group_count=x.shape[-1])
    return y + b


def axial_dwconv(u, w, b, rows):
    bsz, seqlen, ch = u.shape
    k = w.shape[0]
    pad = k // 2
    half = ch // 2
    g = u.reshape(bsz, rows, GRID_W, ch)
    kh = w[:, :half][None, :, None, :]
    kv = w[:, half:][:, None, None, :]
    yh = lax.conv_general_dilated(g[..., :half], kh, (1, 1), [(0, 0), (pad, pad)],
                                  dimension_numbers=("NHWC", "HWIO", "NHWC"),
                                  feature_group_count=half)
    yv = lax.conv_general_dilated(g[..., half:], kv, (1, 1), [(pad, pad), (0, 0)],
                                  dimension_numbers=("NHWC", "HWIO", "NHWC"),
                                  feature_group_count=ch - half)
    return jnp.concatenate([yh, yv], axis=-1).reshape(bsz, seqlen, ch) + b


def ssd_chunked(xh, dt, a, bm, cm, h0):
    bsz, seqlen, nh, hd = xh.shape
    ng, ns = bm.shape[2], bm.shape[3]
    ne = nh // ng
    nc = seqlen // SSM_CHUNK
    dtype = xh.dtype
    xs = (xh * dt[..., None]).reshape(bsz, nc, SSM_CHUNK, ng, ne, hd)
    da = (dt.astype(jnp.float32) * a.astype(jnp.float32)).reshape(bsz, nc, SSM_CHUNK, ng, ne)
    cs = jnp.cumsum(da, axis=2)
    bc = bm.reshape(bsz, nc, SSM_CHUNK, ng, ns)
    cc = cm.reshape(bsz, nc, SSM_CHUNK, ng, ns)
    seg = cs[:, :, :, None] - cs[:, :, None, :]
    scan_order = jnp.tril(jnp.ones((SSM_CHUNK, SSM_CHUNK), dtype=bool))[None, None, :, :, None, None]
    decay = jnp.exp(jnp.where(scan_order, seg, -jnp.inf)).astype(dtype)
    scores = jnp.einsum("bclgn,bcsgn->bclsg", cc, bc)
    y_diag = jnp.einsum("bclsge,bcsgep->bclgep", scores[..., None] * decay, xs)
    w_state = jnp.exp(cs[:, :, -1:] - cs).astype(dtype)
    chunk_states = jnp.einsum("bclgn,bclge,bclgep->bcgepn", bc, w_state, xs)
    chunk_decay = jnp.exp(cs[:, :, -1]).astype(dtype)

    def step(h, inp):
        s, d = inp
        return h * d[..., None, None] + s, h

    _, h_prev = lax.scan(step, h0.reshape(bsz, ng, ne, hd, ns).astype(dtype),
                         (jnp.moveaxis(chunk_states, 1, 0), jnp.moveaxis(chunk_decay, 1, 0)))
    y_off = jnp.einsum("bclgn,cbgepn,bclge->bclgep", cc, h_prev, jnp.exp(cs).astype(dtype))
    return (y_diag + y_off).reshape(bsz, seqlen, nh, hd)


def ssd_final_state(xh, dt, a, bm):
    bsz, seqlen, nh, hd = xh.shape
    ng, ns = bm.shape[2], bm.shape[3]
    ne = nh // ng
    cs = jnp.cumsum(dt.astype(jnp.float32) * a.astype(jnp.float32), axis=1)
    w = (jnp.exp(cs[:, -1:] - cs) * dt.astype(jnp.float32)).astype(xh.dtype).reshape(bsz, seqlen, ng, ne)
    st = jnp.einsum("blgn,blge,blgep->bgepn", bm, w, xh.reshape(bsz, seqlen, ng, ne, hd))
    return st.reshape(bsz, nh, hd, ns)


def ctx_ssd_states(hc, w_in, conv_w, conv_b, dtb_f, dtb_b, alog_f, alog_b):
    bsz, clen, _ = hc.shape
    xb = jax.nn.silu(dwconv1d(hc @ w_in[:, OFF_X:OFF_C], conv_w[:, :D_SSM + GN], conv_b[:D_SSM + GN]))
    dt_raw = hc @ w_in[:, OFF_DT:OFF_GLU]
    xh = xb[..., :D_SSM].reshape(bsz, clen, SSM_HEADS, SSM_HEAD_DIM)
    bm = xb[..., D_SSM:].reshape(bsz, clen, SSM_GROUPS, SSM_STATE)
    dt_f = jax.nn.softplus(dt_raw[..., :SSM_HEADS] + dtb_f)
    dt_b = jax.nn.softplus(dt_raw[..., SSM_HEADS:] + dtb_b)
    s_f = ssd_final_state(xh, dt_f, -jnp.exp(alog_f), bm)
    s_b = ssd_final_state(_flip(xh), _flip(dt_b), -jnp.exp(alog_b), _flip(bm))
    return s_f, s_b


def mixer(h, w_in, w_out, conv_w, conv_b, dtb_f, dtb_b, alog_f, alog_b, d_skip, norm_w,
          cw, cb, ln_w, ln_b, h0_f, h0_b, rows):
    bsz, seqlen, _ = h.shape
    proj = h @ w_in
    z = proj[..., OFF_Z:OFF_X]
    xbc = jax.nn.silu(dwconv1d(proj[..., OFF_X:OFF_DT], conv_w, conv_b))
    dt_raw = proj[..., OFF_DT:OFF_GLU]
    glu = proj[..., OFF_GLU:]
    xh = xbc[..., :D_SSM].reshape(bsz, seqlen, SSM_HEADS, SSM_HEAD_DIM)
    bm = xbc[..., D_SSM:D_SSM + GN].reshape(bsz, seqlen, SSM_GROUPS, SSM_STATE)
    cm = xbc[..., D_SSM + GN:].reshape(bsz, seqlen, SSM_GROUPS, SSM_STATE)
    dt_f = jax.nn.softplus(dt_raw[..., :SSM_HEADS] + dtb_f)
    dt_b = jax.nn.softplus(dt_raw[..., SSM_HEADS:] + dtb_b)
    y_f = ssd_chunked(xh, dt_f, -jnp.exp(alog_f), bm, cm, h0_f)
    y_b = _flip(ssd_chunked(_flip(xh), _flip(dt_b), -jnp.exp(alog_b), _flip(bm), _flip(cm), h0_b))
    y = (y_f + y_b + d_skip[:, None] * xh).reshape(bsz, seqlen, D_SSM)
    y = group_rmsnorm(y * jax.nn.silu(z), norm_w, SSM_GROUPS)
    u = glu[..., :D_CONV] * jax.nn.sigmoid(glu[..., D_CONV:])
    if rows is None:
        u = dwconv1d(u, cw, cb)
    else:
        u = axial_dwconv(u, cw, cb, rows)
    u = jax.nn.silu(layernorm(u, ln_w, ln_b))
    return jnp.concatenate([y, u], axis=-1) @ w_out


def setup_inputs(seed: int = 0) -> dict:
    key = jax.random.key(seed)
    ks = iter(jax.random.split(key, 40))

    def nrm(shape, scale):
        return jax.random.normal(next(ks), shape, jnp.float32) * scale

    L = DEPTH
    d_in_scale = D_MODEL ** -0.5
    u_dt = jax.random.uniform(next(ks), (2, L, SSM_HEADS), jnp.float32)
    dt0 = jnp.exp(u_dt * (math.log(0.1) - math.log(1e-3)) + math.log(1e-3))
    dt_bias = dt0 + jnp.log(-jnp.expm1(-dt0))
    a_log = jnp.log(jax.random.uniform(next(ks), (2, L, SSM_HEADS), jnp.float32, 1.0, 16.0))
    return {
        "x": nrm((BATCH, SEQ, D_MODEL), 1.0),
        "c": nrm((BATCH, D_MODEL), 1.0),
        "ctx": nrm((BATCH, CTX_LEN, D_MODEL), 1.0),
        "c_ctx": nrm((D_MODEL,), 1.0),
        "w_mod": nrm((L, D_MODEL, N_MOD * D_MODEL), 0.5 * d_in_scale),
        "b_mod": nrm((L, N_MOD * D_MODEL), 0.02),
        "norm_ffn1": 1.0 + nrm((L, D_MODEL), 0.02),
        "ffn1_gate": nrm((L, D_MODEL, D_FF), d_in_scale),
        "ffn1_up": nrm((L, D_MODEL, D_FF), d_in_scale),
        "ffn1_down": nrm((L, D_FF, D_MODEL), D_FF ** -0.5),
        "norm_mix": 1.0 + nrm((L, D_MODEL), 0.02),
        "w_in": nrm((L, D_MODEL, D_IN_PROJ), d_in_scale),
        "ssm_conv_w": nrm((L, SSM_CONV_W, D_SSM + 2 * GN), SSM_CONV_W ** -0.5),
        "ssm_conv_b": nrm((L, D_SSM + 2 * GN), 0.02),
        "dt_bias_fwd": dt_bias[0],
        "dt_bias_bwd": dt_bias[1],
        "a_log_fwd": a_log[0],
        "a_log_bwd": a_log[1],
        "ssm_d": 1.0 + nrm((L, SSM_HEADS), 0.02),
        "ssm_norm_w": 1.0 + nrm((L, D_SSM), 0.02),
        "cconv_w": nrm((L, CONF_KERNEL, D_CONV), CONF_KERNEL ** -0.5),
        "cconv_b": nrm((L, D_CONV), 0.02),
        "cconv_ln_w": 1.0 + nrm((L, D_CONV), 0.02),
        "cconv_ln_b": nrm((L, D_CONV), 0.02),
        "w_out": nrm((L, D_INNER, D_MODEL), D_INNER ** -0.5),
        "norm_ffn2": 1.0 + nrm((L, D_MODEL), 0.02),
        "ffn2_gate": nrm((L, D_MODEL, D_FF), d_in_scale),
        "ffn2_up": nrm((L, D_MODEL, D_FF), d_in_scale),
        "ffn2_down": nrm((L, D_FF, D_MODEL), D_FF ** -0.5),
        "final_norm": 1.0 + nrm((D_MODEL,), 0.02),
    }


def reference(x, c, ctx, c_ctx, w_mod, b_mod, norm_ffn1, ffn1_gate, ffn1_up, ffn1_down, norm_mix,
              w_in, ssm_conv_w, ssm_conv_b, dt_bias_fwd, dt_bias_bwd, a_log_fwd, a_log_bwd, ssm_d,
              ssm_norm_w, cconv_w, cconv_b, cconv_ln_w, cconv_ln_b, w_out, norm_ffn2, ffn2_gate,
              ffn2_up, ffn2_down, final_norm):
    bsz = x.shape[0]
    rows = x.shape[1] // GRID_W
    xc = ctx
    for i in range(DEPTH):
        last = i == DEPTH - 1
        mod = (jax.nn.silu(c) @ w_mod[i] + b_mod[i]).reshape(bsz, N_MOD, 1, D_MODEL)
        n_c = 5 if last else N_MOD
        mod_c = (jax.nn.silu(c_ctx) @ w_mod[i][:, :n_c * D_MODEL] + b_mod[i][:n_c * D_MODEL]).reshape(n_c, 1, D_MODEL)
        x = x + 0.5 * mod[:, 2] * swiglu(modulate(rmsnorm(x, norm_ffn1[i]), mod[:, 0], mod[:, 1]),
                                         ffn1_gate[i], ffn1_up[i], ffn1_down[i])
        xc = xc + 0.5 * mod_c[2] * swiglu(modulate(rmsnorm(xc, norm_ffn1[i]), mod_c[0], mod_c[1]),
                                          ffn1_gate[i], ffn1_up[i], ffn1_down[i])
        hx = modulate(rmsnorm(x, norm_mix[i]), mod[:, 3], mod[:, 4])
        hc = modulate(rmsnorm(xc, norm_mix[i]), mod_c[3], mod_c[4])
        s_f, s_b = ctx_ssd_states(hc, w_in[i], ssm_conv_w[i], ssm_conv_b[i], dt_bias_fwd[i],
                                  dt_bias_bwd[i], a_log_fwd[i], a_log_bwd[i])
        x = x + mod[:, 5] * mixer(hx, w_in[i], w_out[i], ssm_conv_w[i], ssm_conv_b[i], dt_bias_fwd[i],
                                  dt_bias_bwd[i], a_log_fwd[i], a_log_bwd[i], ssm_d[i], ssm_norm_w[i],
                                  cconv_w[i], cconv_b[i], cconv_ln_w[i], cconv_ln_b[i], s_f, s_b, rows)
        if not last:
            zero_state = jnp.zeros_like(s_f)
            xc = xc + mod_c[5] * mixer(hc, w_in[i], w_out[i], ssm_conv_w[i], ssm_conv_b[i], dt_bias_fwd[i],
                                       dt_bias_bwd[i], a_log_fwd[i], a_log_bwd[i], ssm_d[i], ssm_norm_w[i],
                                       cconv_w[i], cconv_b[i], cconv_ln_w[i], cconv_ln_b[i],
                                       zero_state, zero_state, None)
        x = x + 0.5 * mod[:, 8] * swiglu(modulate(rmsnorm(x, norm_ffn2[i]), mod[:, 6], mod[:, 7]),
                                         ffn2_gate[i], ffn2_up[i], ffn2_down[i])
        if not last:
            xc = xc + 0.5 * mod_c[8] * swiglu(modulate(rmsnorm(xc, norm_ffn2[i]), mod_c[6], mod_c[7]),
                                              ffn2_gate[i], ffn2_up[i], ffn2_down[i])
    return rmsnorm(x, final_norm)
```

```python
import numpy as np
import concourse.bass as bass
import concourse.mybir as mybir
from concourse.bass_utils import run_bass_kernel_spmd

F32, BF16 = mybir.dt.float32, mybir.dt.bfloat16
ALU = mybir.AluOpType
AF = mybir.ActivationFunctionType

PE, ACT, DVE, POOL, SP = "pe", "act", "dve", "pool", "sp"
COMPUTE = (PE, ACT, DVE, POOL)

D = 1024
L = 2048
CTXL = 256
NSEQ = 4
DFF = 2816
NFC = 22
EPS = 1e-6
OFF_X, OFF_B, OFF_C, OFF_DT, OFF_GLU = 1024, 2048, 2304, 2560, 2592
NEGV = -30000.0


class Op:
    __slots__ = ("eng", "fn", "reads", "writes", "is_dma", "deps", "signal", "sigval", "dsem", "dval",
                 "dprev", "barrier", "nobar")

    def __init__(self, eng, fn, reads, writes, is_dma, barrier=False):
        self.eng, self.fn, self.reads, self.writes, self.is_dma = eng, fn, reads, writes, is_dma
        self.deps = None
        self.signal = False
        self.sigval = 0
        self.dsem = None
        self.dval = 0
        self.dprev = None
        self.barrier = barrier
        self.nobar = False


class Prog:
    def __init__(self, nc):
        self.nc = nc
        self.ops = []
        self.n_dma_sems = {SP: 8, POOL: 4, ACT: 4}
        self._ps = 0
        self._ps_set = None

    def capture(self, fn, banks):
        saved, self.ops = self.ops, []
        sv_set, self._ps_set = self._ps_set, banks
        fn()
        out, self.ops = self.ops, saved
        self._ps_set = sv_set
        return out

    def merge(self, a, b):
        ia = ib = 0
        while ia < len(a) or ib < len(b):
            fa = ia / len(a) if a else 2.0
            fb = ib / len(b) if b else 2.0
            if ia < len(a) and (fa <= fb or ib >= len(b)):
                self.ops.append(a[ia])
                ia += 1
            else:
                self.ops.append(b[ib])
                ib += 1

    def op(self, eng, fn, reads=(), writes=()):
        self.ops.append(Op(eng, fn, tuple(reads), tuple(writes), False))

    def dma(self, eng, fn, reads=(), writes=(), nobar=False):
        o = Op(eng, fn, tuple(reads), tuple(writes), True)
        o.nobar = nobar
        self.ops.append(o)

    def barrier(self, fn):
        self.ops.append(Op(DVE, fn, (), (), False, barrier=True))

    def finalize(self):
        nc, ops = self.nc, self.ops
        last_writer, readers = {}, {}
        last_on = {}
        dmas_since = []
        cur_bar = None
        for o in ops:
            deps = set()
            if o.barrier:
                for e in COMPUTE:
                    if e in last_on:
                        deps.add(last_on[e])
                deps.update(dmas_since)
                dmas_since = []
                cur_bar = o
            else:
                if cur_bar is not None:
                    deps.add(cur_bar)
                for r in o.reads:
                    w = last_writer.get(r)
                    if w is not None:
                        deps.add(w)
                for w_ in o.writes:
                    w = last_writer.get(w_)
                    if w is not None:
                        deps.add(w)
                    deps.update(readers.get(w_, ()))
                for r in o.reads:
                    readers.setdefault(r, []).append(o)
                for w_ in o.writes:
                    last_writer[w_] = o
                    readers[w_] = []
            deps.discard(o)
            o.deps = deps
            if o.is_dma:
                if not o.nobar:
                    dmas_since.append(o)
            else:
                last_on[o.eng] = o
        dma_count, dma_hist = {}, {}
        for o in ops:
            if o.is_dma:
                n = self.n_dma_sems[o.eng]
                k = dma_count.get(o.eng, 0)
                dma_count[o.eng] = k + 1
                slot = k % n
                o.dsem = (o.eng, slot)
                o.dval = 16 * (k // n + 1)
                hist = dma_hist.setdefault((o.eng, slot), [])
                o.dprev = hist[-1] if hist else None
                hist.append(o)
        for o in ops:
            for d in o.deps:
                if d.is_dma:
                    continue
                if d.eng == PE and o.eng == PE and not o.is_dma:
                    continue
                d.signal = True
        cnt = {e: 0 for e in COMPUTE}
        for o in ops:
            if not o.is_dma and o.signal:
                cnt[o.eng] += 1
                o.sigval = cnt[o.eng]
        self.sig_totals = dict(cnt)
        sems = {e: nc.alloc_semaphore(name=f"s_{e}") for e in COMPUTE}
        dsems = {k: nc.alloc_semaphore(name=f"d_{k[0]}{k[1]}") for k in dma_hist}
        waited = {}
        nwaits = 0
        for o in ops:
            need = {}
            for d in o.deps:
                if d.is_dma:
                    key, val = ("d",) + d.dsem, d.dval
                else:
                    if d.eng == PE and o.eng == PE and not o.is_dma:
                        continue
                    key, val = ("c", d.eng), d.sigval
                if need.get(key, 0) < val:
                    need[key] = val
            if o.is_dma and o.dprev is not None:
                key, val = ("d",) + o.dsem, o.dprev.dval
                if need.get(key, 0) < val:
                    need[key] = val
            wl = []
            for key, val in need.items():
                wk = (o.eng, key)
                if waited.get(wk, 0) >= val:
                    continue
                waited[wk] = val
                wl.append((sems[key[1]] if key[0] == "c" else dsems[(key[1], key[2])], val))
                nwaits += 1
            o.deps = wl
        self.nwaits = nwaits
        final_d = [(dsems[k], h[-1].dval) for k, h in dma_hist.items()]

        def emit_engine(eng_name):
            def body(eng):
                for o in ops:
                    if o.eng != eng_name:
                        continue
                    for sem, val in o.deps:
                        eng.wait_ge(sem, val)
                    ins = o.fn(eng)
                    if o.is_dma:
                        ins.then_inc(dsems[o.dsem], 16)
                    elif o.signal:
                        ins.then_inc(sems[o.eng], 1)
                if eng_name == SP:
                    for sem, val in final_d:
                        eng.wait_ge(sem, val)
            return body

        with nc.Block() as block:
            block.sync(emit_engine(SP))
            block.tensor(emit_engine(PE))
            block.scalar(emit_engine(ACT))
            block.vector(emit_engine(DVE))
            block.gpsimd(emit_engine(POOL))


class Arena:
    def __init__(self, nc, P, base, top):
        self.nc, self.P = nc, P
        self.base = (base + 63) // 64 * 64
        self.top = top
        self.cur = self.base
        self.top_cur = top
        self.n = 0
        self.peak = 0

    def alloc(self, name, shape, dtype):
        esz = 4 if dtype == F32 else 2
        nbytes = esz
        for s in shape[1:]:
            nbytes *= s
        nbytes = (nbytes + 63) // 64 * 64
        off = self.cur
        self.cur += nbytes
        assert self.cur <= self.top_cur, f"SBUF overflow allocating {name}: {self.cur} > {self.top_cur}"
        self.peak = max(self.peak, self.cur)
        self.n += 1
        return self.nc.alloc_sbuf_tensor_at(f"{name}_{self.n}", list(shape), dtype, offset=off)

    def alloc_top(self, name, shape, dtype):
        esz = 4 if dtype == F32 else 2
        nbytes = esz
        for s_ in shape[1:]:
            nbytes *= s_
        nbytes = (nbytes + 63) // 64 * 64
        self.top_cur -= nbytes
        assert self.top_cur >= self.cur, f"SBUF overflow (top) allocating {name}"
        self.n += 1
        return self.nc.alloc_sbuf_tensor_at(f"{name}_{self.n}", list(shape), dtype, offset=self.top_cur)

    def reset_top(self):
        self.top_cur = self.top

    def mark(self):
        return self.cur

    def release(self, m, fence):
        self.cur = m
        self.P.barrier(lambda e: e.memset(fence[:], 0.0))


def build(nseq=NSEQ, debug=None):
    nc = bass.Bass("TRN2", target_bir_lowering=False)
    P = Prog(nc)
    dbg_out = {}

    def din(name, shape, dt=F32):
        return nc.dram_tensor(name, list(shape), dt, kind="ExternalInput").ap()

    def dscr(name, shape, dt=BF16):
        return nc.dram_tensor(name, list(shape), dt, kind="Internal").ap()

    x_d = din("x", [nseq, L, D])
    ctx_d = din("ctx", [nseq * CTXL, D])
    cin_d = din("cin", [128, 8, 5])
    wmod_d = din("w_mod", [D, 9 * D])
    bmod_d = din("bmodT", [128, 72])
    vec_d = din("vecs", [128, 8, 8])
    convw_d = din("convwT", [128, 12, 5])
    convb_d = din("convbT", [128, 12])
    cconvw_d = din("cconvwT", [128, 8, 31])
    col_d = din("cols", [128, 4])
    fin_d = din("final_bc", [128, D])
    ident_d = din("ident", [128, 128])
    oh_d = din("oh", [128, 32, 128])
    neg_d = din("neg", [128, 2, 256])
    wg_d = [din("ffn1_gate", [D, DFF]), din("ffn2_gate", [D, DFF])]
    wu_d = [din("ffn1_up", [D, DFF]), din("ffn2_up", [D, DFF])]
    wd_d = [din("ffn1_down", [DFF, D]), din("ffn2_down", [DFF, D])]
    win_d = din("w_in", [D, 4640])
    wout_d = din("w_out", [2 * D, D])
    out_d = nc.dram_tensor("out", [nseq, L, D], F32, kind="ExternalOutput").ap()

    wg_s = [dscr(f"wg_s{i}", [11, 128, 8, 256]) for i in range(2)]
    wu_s = [dscr(f"wu_s{i}", [11, 128, 8, 256]) for i in range(2)]
    wd_s = [dscr(f"wd_s{i}", [8, 128, NFC, 128]) for i in range(2)]
    win_cols = [128 * i for i in range(20)] + [OFF_GLU + 128 * i for i in range(16)]
    win_s = dscr("win_s", [36, 128, 8, 128])
    wout_s = dscr("wout_s", [8, 128, 16, 128])
    h0_s = dscr("h0_s", [nseq, 2, 128, D], F32)

    if debug:
        for nm, shp in debug.items():
            dbg_out[nm] = nc.dram_tensor("dbg_" + nm, list(shp), F32, kind="ExternalOutput").ap()

    A = Arena(nc, P, int(nc.sbuf_base) + 64, int(nc.sbuf_top))
    ident_f = A.alloc("ident_f", [128, 128], F32)
    ident_b = A.alloc("ident_b", [128, 128], BF16)
    ones_b = A.alloc("ones_b", [128, 128], BF16)
    ones_f = A.alloc("ones_f", [128, 128], F32)
    fence = A.alloc("fence", [128, 16], F32)
    modT = A.alloc("modT", [128, 72, 5], F32)
    vecs = A.alloc("vecs", [128, 8, 8], F32)
    convw = A.alloc("convw", [128, 12, 5], F32)
    convb = A.alloc("convb", [128, 12], F32)
    cconvw = A.alloc("cconvw", [128, 8, 31], F32)
    cols = A.alloc("cols", [128, 4], F32)
    a128 = A.alloc("a128", [128, 1], F32)
    epsc = A.alloc("epsc", [128, 1], F32)
    onec = A.alloc("onec", [128, 1], F32)
    wdt = A.alloc("wdt", [128, 8, 128], BF16)
    AS = A.alloc("AS", [128, 2, 8], F32)
    GV = A.alloc("GV", [128, 8], F32)
    xT = A.alloc("xT", [128, 8, L], F32)
    hT = A.alloc("hT", [128, 8, L], BF16)
    ps = [nc.alloc_psum_tensor(f"ps{i}", [128, 512], F32) for i in range(8)]

    def nps():
        st = P._ps_set
        if st is not None:
            i = st[P._ps % len(st)]
            P._ps += 1
            return i
        i = P._ps % 8
        P._ps = (i + 1) % 8
        return i

    def dump(name, src_ap, key):
        if debug and name in dbg_out:
            q = SP if src_ap.dtype == F32 else POOL
            P.dma(q, lambda e: e.dma_start(out=dbg_out[name], in_=src_ap), reads=[key], writes=["dbg_" + name])

    V_N1, V_NM, V_N2, V_SN, V_CB, V_LW, V_LB, V_D = range(8)

    P.dma(SP, lambda e: e.dma_start(out=ident_f[:], in_=ident_d), writes=["ident_f"])
    P.dma(SP, lambda e: e.dma_start(out=vecs[:], in_=vec_d), writes=["vecs"])
    P.dma(SP, lambda e: e.dma_start(out=convw[:], in_=convw_d), writes=["convw"])
    P.dma(SP, lambda e: e.dma_start(out=convb[:], in_=convb_d), writes=["convb"])
    P.dma(SP, lambda e: e.dma_start(out=cconvw[:], in_=cconvw_d), writes=["cconvw"])
    P.dma(SP, lambda e: e.dma_start(out=cols[:], in_=col_d), writes=["cols"])
    P.op(DVE, lambda e: e.memset(epsc[:], EPS), writes=["epsc"])
    P.op(DVE, lambda e: e.memset(onec[:], 1.0), writes=["onec"])
    P.op(DVE, lambda e: e.tensor_copy(out=ident_b[:], in_=ident_f[:]), reads=["ident_f"], writes=["ident_b"])
    P.op(DVE, lambda e: e.memset(ones_b[:], 1.0), writes=["ones_b"])
    P.op(DVE, lambda e: e.memset(ones_f[:], 1.0), writes=["ones_f"])
    P.op(DVE, lambda e: e.memset(fence[:], 0.0), writes=["fence"])
    P.op(ACT, lambda e: e.activation(out=a128[:], in_=cols[:, 1:2], func=AF.Exp), reads=["cols"], writes=["a128"])
    P.op(DVE, lambda e: e.tensor_scalar(out=a128[:], in0=a128[:], scalar1=-1.0, scalar2=None, op0=ALU.mult),
         reads=["a128"], writes=["a128"])
    P.op(POOL, lambda e: e.memset(wdt[:], 0.0), writes=["wdt"])
    win_v = win_d.rearrange("(kc kp) f -> kp kc f", kp=128)
    for rep in range(2):
        for dr in range(2):
            c0 = rep * 64 + dr * 32
            P.dma(POOL, lambda e, c0=c0, dr=dr: e.dma_start(out=wdt[:, :, c0:c0 + 16],
                                                           in_=win_v[:, :, OFF_DT + 16 * dr:OFF_DT + 16 * dr + 16]),
                  reads=["wdt"], writes=["wdt"])

    def cast_ffn(i):
        gv = wg_d[i].rearrange("(kc kp) (fq j) -> fq kp kc j", kp=128, j=256)
        uv = wu_d[i].rearrange("(kc kp) (fq j) -> fq kp kc j", kp=128, j=256)
        for fq in range(11):
            P.dma(POOL, lambda e, fq=fq: e.dma_start(out=wg_s[i][fq], in_=gv[fq]), writes=[("wg_s", i, fq)], nobar=True)
            P.dma(POOL, lambda e, fq=fq: e.dma_start(out=wu_s[i][fq], in_=uv[fq]), writes=[("wu_s", i, fq)], nobar=True)
        dv = wd_d[i].rearrange("(fc fp) (o j) -> o fp fc j", fp=128, j=128)
        for o in range(8):
            P.dma(POOL, lambda e, o=o: e.dma_start(out=wd_s[i][o], in_=dv[o]), writes=[("wd_s", i, o)], nobar=True)

    def cast_mixer():
        for t, c0 in enumerate(win_cols):
            P.dma(POOL, lambda e, t=t, c0=c0: e.dma_start(out=win_s[t], in_=win_v[:, :, c0:c0 + 128]),
                  writes=[("win_s", t)], nobar=True)
        ov = wout_d.rearrange("(kc kp) (o j) -> o kp kc j", kp=128, j=128)
        for o in range(8):
            P.dma(POOL, lambda e, o=o: e.dma_start(out=wout_s[o], in_=ov[o]), writes=[("wout_s", o)], nobar=True)


    m0 = A.mark()
    cin = A.alloc("cin", [128, 8, 5], F32)
    bmod = A.alloc("bmod", [128, 72], F32)
    wm = [A.alloc(f"wm{i}", [128, 8, 512], BF16) for i in range(4)]
    cinb = A.alloc("cinb", [128, 8, 5], BF16)
    P.dma(SP, lambda e: e.dma_start(out=cin[:], in_=cin_d), writes=["cin"])
    P.dma(SP, lambda e: e.dma_start(out=bmod[:], in_=bmod_d), writes=["bmod"])
    P.op(ACT, lambda e: e.activation(out=cinb[:], in_=cin[:], func=AF.Silu), reads=["cin"], writes=["cinb"])
    wmod_v = wmod_d.rearrange("(kc kp) f -> kp kc f", kp=128)
    pm = nps()

    def mod_piece(q):
        P.dma(POOL, lambda e, q=q: e.dma_start(out=wm[q % 4][:], in_=wmod_v[:, :, q * 512:(q + 1) * 512]),
              writes=[("wm", q % 4)])

        def mm(e, q=q):
            ins = None
            for j in range(4):
                col = (q * 4 + j) * 5
                for kc in range(8):
                    ins = e.matmul(ps[pm][:, col:col + 5], lhsT=wm[q % 4][:, kc, j * 128:(j + 1) * 128],
                                   rhs=cinb[:, kc, :], start=(kc == 0), stop=(kc == 7))
            return ins
        P.op(PE, mm, reads=[("wm", q % 4), "cinb"], writes=[("ps", pm)])
    for q in range(18):
        mod_piece(q)
    cast_ffn(0)
    cast_mixer()
    cast_ffn(1)
    P.op(DVE, lambda e: e.tensor_tensor(out=modT[:], in0=ps[pm][:, 0:360].rearrange("p (j b) -> p j b", b=5),
                                        in1=bmod[:].unsqueeze(2).to_broadcast([128, 72, 5]), op=ALU.add),
         reads=[("ps", pm), "bmod"], writes=["modT"])
    dump("modT", modT[:], "modT")
    A.release(m0, fence)

    def set_AS(norm_idx, j_shift, j_scale, col):
        def f(e):
            return e.scalar_tensor_tensor(out=AS[:, 0, :], in0=modT[:, 8 * j_scale:8 * j_scale + 8, col], scalar=1.0,
                                          in1=vecs[:, norm_idx, :], op0=ALU.add, op1=ALU.mult)
        P.op(DVE, f, reads=["modT", "vecs"], writes=["AS"])
        P.op(DVE, lambda e: e.tensor_copy(out=AS[:, 1, :], in_=modT[:, 8 * j_shift:8 * j_shift + 8, col]),
             reads=["modT"], writes=["AS"])

    def set_gate(j, col, mul):
        P.op(DVE, lambda e: e.tensor_scalar(out=GV[:], in0=modT[:, 8 * j:8 * j + 8, col], scalar1=mul, scalar2=None,
                                            op0=ALU.mult), reads=["modT"], writes=["GV"])

    def ws_load(top=False):
        al = A.alloc_top if top else A.alloc
        return dict(stg=[al(f"stg{i}", [128, D], F32) for i in range(2)])

    def load_tokens(ws, src, T, dstT, dkey, i0=0, i1=None, do_dma=True, do_tr=True):
        stg = ws["stg"]
        for i in range(i0, T // 128 if i1 is None else i1):
            s = stg[i % 2]
            if do_dma:
                P.dma(SP, lambda e, i=i, s=s: e.dma_start(out=s[:], in_=src[i * 128:(i + 1) * 128, :]),
                      writes=[("stg", i % 2)])
            if not do_tr:
                continue
            for half in range(2):
                pb = nps()

                def tr(e, s=s, half=half, pb=pb):
                    ins = None
                    for c in range(4):
                        cc = half * 4 + c
                        ins = e.transpose(out=ps[pb][:, c * 128:(c + 1) * 128], in_=s[:, cc * 128:(cc + 1) * 128],
                                          identity=ident_f[:])
                    return ins
                P.op(PE, tr, reads=[("stg", i % 2), "ident_f"], writes=[("ps", pb)])
                src_v = ps[pb][:].rearrange("p (c t) -> p c t", c=4)
                dst_v = dstT[:, half * 4:half * 4 + 4, i * 128:(i + 1) * 128]
                if half == 0:
                    P.op(ACT, lambda e, a=dst_v, b=src_v: e.activation(out=a, in_=b, func=AF.Copy),
                         reads=[("ps", pb)], writes=[(dkey, i // 4)])
                else:
                    P.op(DVE, lambda e, a=dst_v, b=src_v: e.tensor_copy(out=a, in_=b),
                         reads=[("ps", pb)], writes=[(dkey, i // 4)])

    def ws_norm(ntmp=2):
        tmp = [A.alloc(f"nt{i}", [128, 512], F32) for i in range(ntmp)]
        return dict(sq=[A.alloc(f"sq{i}", [128, 512], BF16) for i in range(2)], rs=A.alloc("rs", [128, 512], F32),
                    tmp=[tmp[i % ntmp] for i in range(2)], ntmp=ntmp)

    def norm_mod(ws, srcT, skey, T, dst, dkey, nfc=8, denom=1024.0, affine=True, vec_scale=None, t0=0, t1=None,
                 chunk_cb=None):
        sq, rs, tmp = ws["sq"], ws["rs"], ws["tmp"]
        for t in range(t0, T // 512 if t1 is None else t1):
            tb = slice(t * 512, (t + 1) * 512)
            pb = nps()
            for c in range(nfc):
                P.op(ACT, lambda e, c=c, tb=tb: e.activation(out=sq[c % 2][:], in_=srcT[:, c, tb], func=AF.Square),
                     reads=[(skey, t)], writes=[("sq", c % 2)])
                P.op(PE, lambda e, c=c, pb=pb: e.matmul(ps[pb][:], lhsT=ones_b[:], rhs=sq[c % 2][:], start=(c == 0),
                                                        stop=(c == nfc - 1)),
                     reads=[("sq", c % 2), "ones_b"], writes=[("ps", pb)])
            P.op(ACT, lambda e, pb=pb: e.activation(out=rs[:], in_=ps[pb][:], func=AF.Sqrt, bias=epsc[:, 0:1],
                                                    scale=1.0 / denom),
                 reads=[("ps", pb), "epsc"], writes=["rs"])
            P.op(DVE, lambda e: e.reciprocal(out=rs[:], in_=rs[:]), reads=["rs"], writes=["rs"])
            for c in range(nfc):
                P.op(DVE, lambda e, c=c, tb=tb: e.tensor_tensor(out=tmp[c % 2][:], in0=srcT[:, c, tb], in1=rs[:],
                                                                op=ALU.mult),
                     reads=[(skey, t), "rs"], writes=[("nt", c % ws["ntmp"])])
                if affine:
                    P.op(ACT, lambda e, c=c, tb=tb: e.activation(out=dst[:, c, tb], in_=tmp[c % 2][:], func=AF.Identity,
                                                                 scale=AS[:, 0, c:c + 1], bias=AS[:, 1, c:c + 1]),
                         reads=[("nt", c % ws["ntmp"]), "AS"], writes=[(dkey, t)])
                else:
                    P.op(ACT, lambda e, c=c, tb=tb: e.activation(out=dst[:, c, tb], in_=tmp[c % 2][:], func=AF.Identity,
                                                                 scale=vec_scale[:, c:c + 1]),
                         reads=[("nt", c % ws["ntmp"]), "vecs"], writes=[(dkey, t)])
                if chunk_cb is not None:
                    chunk_cb(t, c)

    def ws_ffn(TB):
        return dict(hid=A.alloc("hid", [128, NFC, TB], BF16),
                    wgb=[A.alloc(f"wgb{j}", [128, 8, 256], BF16) for j in range(2)],
                    wub=[A.alloc(f"wub{j}", [128, 8, 256], BF16) for j in range(2)],
                    wdb=[A.alloc(f"wdb{j}", [128, NFC, 128], BF16) for j in range(2)],
                    sg=[A.alloc(f"sg{j}", [128, 512], F32) for j in range(2)], k=[0])

    def ffn_block(ws, i, tok0, TB, resT, rkey, hooks=None):
        nt = TB // 512
        hid, wgb, wub, wdb, sg = ws["hid"], ws["wgb"], ws["wub"], ws["wdb"], ws["sg"]
        for fq in range(11):
            b = fq % 2
            P.dma(SP, lambda e, fq=fq, b=b: e.dma_start(out=wgb[b][:], in_=wg_s[i][fq]), reads=[("wg_s", i, fq)],
                  writes=[("wgb", b)])
            P.dma(SP, lambda e, fq=fq, b=b: e.dma_start(out=wub[b][:], in_=wu_s[i][fq]), reads=[("wu_s", i, fq)],
                  writes=[("wub", b)])
            for j in range(2):
                f = 2 * fq + j
                for t in range(nt):
                    tok = slice(tok0 + t * 512, tok0 + (t + 1) * 512)
                    pg, pu = nps(), nps()

                    def mm(e, b=b, j=j, tok=tok, pg=pg, pu=pu):
                        ins = None
                        for kc in range(8):
                            ins = e.matmul(ps[pg][:], lhsT=wgb[b][:, kc, j * 128:(j + 1) * 128], rhs=hT[:, kc, tok],
                                           start=(kc == 0), stop=(kc == 7))
                        for kc in range(8):
                            ins = e.matmul(ps[pu][:], lhsT=wub[b][:, kc, j * 128:(j + 1) * 128], rhs=hT[:, kc, tok],
                                           start=(kc == 0), stop=(kc == 7))
                        return ins
                    P.op(PE, mm, reads=[("wgb", b), ("wub", b), ("hT", (tok0 + t * 512) // 512)],
                         writes=[("ps", pg), ("ps", pu)])
                    kb = ws["k"][0] % 2
                    ws["k"][0] += 1
                    s = sg[kb]
                    P.op(ACT, lambda e, s=s, pg=pg: e.activation(out=s[:], in_=ps[pg][:], func=AF.Silu),
                         reads=[("ps", pg)], writes=[("sg", kb)])
                    P.op(DVE, lambda e, s=s, pu=pu, f=f, t=t: e.tensor_tensor(out=hid[:, f, t * 512:(t + 1) * 512],
                                                                              in0=ps[pu][:], in1=s[:], op=ALU.mult),
                         reads=[("ps", pu), ("sg", kb)], writes=[("hid", f, t)])
            if hooks and ("g", fq) in hooks:
                hooks[("g", fq)]()
        for o in range(8):
            b = o % 2
            P.dma(SP, lambda e, o=o, b=b: e.dma_start(out=wdb[b][:], in_=wd_s[i][o]), reads=[("wd_s", i, o)],
                  writes=[("wdb", b)])
            for t in range(nt):
                tok = slice(tok0 + t * 512, tok0 + (t + 1) * 512)
                pb = nps()

                def mm(e, b=b, t=t, pb=pb):
                    ins = None
                    for fc in range(NFC):
                        ins = e.matmul(ps[pb][:], lhsT=wdb[b][:, fc, :], rhs=hid[:, fc, t * 512:(t + 1) * 512],
                                       start=(fc == 0), stop=(fc == NFC - 1))
                    return ins
                P.op(PE, mm, reads=[("wdb", b)] + [("hid", fc, t) for fc in range(NFC)], writes=[("ps", pb)])
                rk = (rkey, (tok0 + t * 512) // 512)
                P.op(DVE, lambda e, o=o, tok=tok, pb=pb: e.scalar_tensor_tensor(
                    out=resT[:, o, tok], in0=ps[pb][:], scalar=GV[:, o:o + 1], in1=resT[:, o, tok], op0=ALU.mult,
                    op1=ALU.add), reads=[("ps", pb), "GV", rk], writes=[rk])
            if hooks and ("d", o) in hooks:
                hooks[("d", o)]()

    def ws_oproj(nkc):
        return dict(wt=A.alloc("wo", [128, 8, nkc, 128], BF16))

    def out_proj_load(ws, kc0, nkc):
        wt = ws["wt"]
        for o in range(8):
            P.dma(SP, lambda e, o=o: e.dma_start(out=wt[:, o], in_=wout_s[o][:, kc0:kc0 + nkc, :]),
                  reads=[("wout_s", o)], writes=[("wo", o)])

    def out_proj_t(ws, srcT, skey, nkc, t, banks=None, o0=0, o1=8):
        wt = ws["wt"]
        tok = slice(t * 512, (t + 1) * 512)
        for o in range(o0, o1):
            pb = nps() if banks is None else banks[o % len(banks)]

            def mm(e, o=o, pb=pb):
                ins = None
                for kc in range(nkc):
                    ins = e.matmul(ps[pb][:], lhsT=wt[:, o, kc, :], rhs=srcT[:, kc, tok], start=(kc == 0),
                                   stop=(kc == nkc - 1))
                return ins
            P.op(PE, mm, reads=[("wo", o), (skey, t)], writes=[("ps", pb)])
            P.op(DVE, lambda e, o=o, pb=pb: e.scalar_tensor_tensor(
                out=xT[:, o, tok], in0=ps[pb][:], scalar=GV[:, o:o + 1], in1=xT[:, o, tok], op0=ALU.mult,
                op1=ALU.add), reads=[("ps", pb), "GV", ("xT", t)], writes=[("xT", t)])

    def ws_proj(T, seglen):
        w = dict(wt=A.alloc("pw", [128, 8, 128], BF16), dg=A.alloc("pdg", [128, 5, 128], BF16),
                 xpad=A.alloc("xpad", [128, T // seglen, seglen + 4], BF16))
        P.op(DVE, lambda e: e.memset(w["xpad"][:], 0.0), writes=["xpad"])
        return w

    def proj_conv_silu(ws, wtile, cc, T, seglen, dst, dkey, hsrc, hkey):
        wt, dg, xpad = ws["wt"], ws["dg"], ws["xpad"]
        if ws.get("rezero"):
            P.op(DVE, lambda e: e.memset(xpad[:, :, 0:2], 0.0), reads=["xpad"], writes=["xpad"])
        P.dma(SP, lambda e: e.dma_start(out=wt[:], in_=win_s[wtile]), reads=[("win_s", wtile)], writes=["pw"])
        for k in range(5):
            P.op(DVE, lambda e, k=k: e.tensor_scalar(out=dg[:, k, :], in0=ident_b[:], scalar1=convw[:, cc, k:k + 1],
                                                     scalar2=None, op0=ALU.mult),
                 reads=["ident_b", "convw"], writes=[("pdg", k)])
        nb = T // 512
        spb = seglen // 512 if seglen >= 512 else 0
        for t in range(nb):
            pb = nps()

            def mm(e, t=t, pb=pb):
                ins = None
                for kc in range(8):
                    ins = e.matmul(ps[pb][:], lhsT=wt[:, kc, :], rhs=hsrc[:, kc, t * 512:(t + 1) * 512],
                                   start=(kc == 0), stop=(kc == 7))
                return ins
            P.op(PE, mm, reads=["pw", (hkey, t)], writes=[("ps", pb)])
            if seglen >= 512:
                sgi, off = t // spb, (t % spb) * 512
                dv = xpad[:, sgi, 2 + off:2 + off + 512]
                sv = ps[pb][:]
            else:
                ns = 512 // seglen
                dv = xpad[:, t * ns:(t + 1) * ns, 2:2 + seglen]
                sv = ps[pb][:].rearrange("p (s l) -> p s l", s=ns)
            P.op(ACT, lambda e, dv=dv, sv=sv: e.activation(out=dv, in_=sv, func=AF.Copy),
                 reads=[("ps", pb), "xpad"], writes=["xpad"])
        for t in range(nb):
            pb = nps()

            def cv(e, t=t, pb=pb):
                ins = None
                for k in range(5):
                    if seglen >= 512:
                        sgi, off = t // spb, (t % spb) * 512
                        rv = xpad[:, sgi, off + k:off + k + 512]
                    else:
                        ns = 512 // seglen
                        rv = xpad[:, t * ns:(t + 1) * ns, k:k + seglen]
                    ins = e.matmul(ps[pb][:], lhsT=dg[:, k, :], rhs=rv, start=(k == 0), stop=(k == 4))
                return ins
            P.op(PE, cv, reads=["xpad"] + [("pdg", k) for k in range(5)], writes=[("ps", pb)])
            P.op(ACT, lambda e, t=t, pb=pb: e.activation(out=dst[:, t * 512:(t + 1) * 512], in_=ps[pb][:], func=AF.Silu,
                                                         bias=convb[:, cc:cc + 1]),
                 reads=[("ps", pb), "convb"], writes=[(dkey, t)])

    def to_tok(srcT, skey, T, dst, dkey):
        for q in range(T // 512):
            pb = nps()
            pv = ps[pb][:].bitcast(BF16)

            def tr(e, q=q, pv=pv):
                ins = None
                for j in range(4):
                    i = q * 4 + j
                    ins = e.transpose(out=pv[:, j * 128:(j + 1) * 128], in_=srcT[:, i * 128:(i + 1) * 128],
                                      identity=ident_b[:])
                return ins
            P.op(PE, tr, reads=[(skey, q), "ident_b"], writes=[("ps", pb)])
            P.op(DVE, lambda e, q=q, pv=pv: e.tensor_copy(out=dst[:, q * 4:(q + 1) * 4, :],
                                                           in_=pv[:, 0:512].rearrange("p (j f) -> p j f", j=4)),
                 reads=[("ps", pb)], writes=[(dkey, q)])

    def ws_dt_out(T, CL, full):
        w = dict(w_tok=A.alloc("w_tok", [128, T // 128, 64], F32))
        if full:
            w.update(phiHL=A.alloc("phiHL", [128, T], BF16), npsiHL=A.alloc("npsiHL", [128, T], BF16),
                     dbc=A.alloc("dbc", [128, 64, T // CL], F32))
        return w

    def dt_prep(R, hsrc, hkey, T, CL, full, barrier=True):
        m = A.mark()
        dtT = A.alloc("dtT", [128, T], F32)
        G = A.alloc("G", [128, T], F32)
        phi = A.alloc("phi", [128, T], F32)
        tmp = A.alloc("dtmp", [128, T], F32)
        nch = T // CL
        Tt = A.alloc("Tt", [128, nch], F32)
        for t in range(T // 512):
            tb = slice(t * 512, (t + 1) * 512)
            pb = nps()

            def mm(e, tb=tb, pb=pb):
                ins = None
                for kc in range(8):
                    ins = e.matmul(ps[pb][:], lhsT=wdt[:, kc, :], rhs=hsrc[:, kc, tb], start=(kc == 0), stop=(kc == 7))
                return ins
            P.op(PE, mm, reads=["wdt", (hkey, t)], writes=[("ps", pb)])
            P.op(ACT, lambda e, tb=tb, pb=pb: e.activation(out=tmp[:, tb], in_=ps[pb][:], func=AF.Exp,
                                                           bias=cols[:, 0:1]),
                 reads=[("ps", pb), "cols"], writes=["dtmp"])
        P.op(ACT, lambda e: e.activation(out=dtT[:], in_=tmp[:], func=AF.Ln, bias=onec[:, 0:1]), reads=["dtmp", "onec"],
             writes=["dtT"])
        P.op(DVE, lambda e: e.tensor_scalar(out=tmp[:], in0=dtT[:], scalar1=a128[:, 0:1], scalar2=None, op0=ALU.mult),
             reads=["dtT", "a128", "dtmp"], writes=["dtmp"])
        for c in range(nch):
            for j in range(CL // 128):
                sl = slice(c * CL + j * 128, c * CL + (j + 1) * 128)
                init = 0.0 if j == 0 else G[:, c * CL + j * 128 - 1:c * CL + j * 128]
                P.op(DVE, lambda e, sl=sl, init=init: e.tensor_tensor_scan(out=G[:, sl], data0=ones_f[:], data1=tmp[:, sl],
                                                                            initial=init, op0=ALU.mult, op1=ALU.add),
                     reads=["dtmp", "ones_f", "G"], writes=["G"])
        Gv = G[:].rearrange("p (c l) -> p c l", l=CL)
        P.op(DVE, lambda e: e.tensor_copy(out=Tt[:], in_=Gv[:, :, CL - 1]), reads=["G"], writes=["Tt"])
        Tb = Tt[:].unsqueeze(2).to_broadcast([128, nch, CL])
        P.op(DVE, lambda e: e.tensor_tensor(out=phi[:].rearrange("p (c l) -> p c l", l=CL),
                                            in0=tmp[:].rearrange("p (c l) -> p c l", l=CL), in1=Tb, op=ALU.add),
             reads=["dtmp", "Tt"], writes=["phi"])
        P.op(DVE, lambda e: e.tensor_scalar(out=phi[:], in0=phi[:], scalar1=cols[:, 3:4], scalar2=None, op0=ALU.mult),
             reads=["phi", "cols"], writes=["phi"])
        P.op(DVE, lambda e: e.scalar_tensor_tensor(out=phi[:], in0=G[:], scalar=cols[:, 2:3], in1=phi[:], op0=ALU.mult,
                                                   op1=ALU.add), reads=["G", "phi", "cols"], writes=["phi"])
        P.op(DVE, lambda e: e.tensor_tensor(out=G[:].rearrange("p (c l) -> p c l", l=CL), in0=Tb,
                                            in1=phi[:].rearrange("p (c l) -> p c l", l=CL), op=ALU.subtract),
             reads=["Tt", "phi", "G"], writes=["G"])
        P.op(ACT, lambda e: e.activation(out=G[:], in_=G[:], func=AF.Exp), reads=["G"], writes=["G"])
        P.op(DVE, lambda e: e.tensor_tensor(out=G[:], in0=G[:], in1=dtT[:], op=ALU.mult), reads=["G", "dtT"],
             writes=["G"])
        w_tok = R["w_tok"]
        ntile = T // 128
        for q0 in range(0, ntile, 8):
            nq = min(8, ntile - q0)
            pb = nps()

            def tr(e, q0=q0, nq=nq, pb=pb):
                ins = None
                for j in range(nq):
                    i = q0 + j
                    ins = e.transpose(out=ps[pb][:, j * 64:(j + 1) * 64], in_=G[0:64, i * 128:(i + 1) * 128],
                                      identity=ident_f[0:64, 0:64])
                return ins
            P.op(PE, tr, reads=["G", "ident_f"], writes=[("ps", pb)])
            P.op(DVE, lambda e, q0=q0, nq=nq, pb=pb: e.tensor_copy(
                out=w_tok[:, q0:q0 + nq, :], in_=ps[pb][:, 0:nq * 64].rearrange("p (j f) -> p j f", j=nq)),
                reads=[("ps", pb)], writes=["w_tok"])
        if full:
            phiHL, npsiHL, dbc = R["phiHL"], R["npsiHL"], R["dbc"]
            hi = A.alloc("hi_all", [128, T], BF16)
            P.op(ACT, lambda e: e.activation(out=hi[:], in_=phi[:], func=AF.Copy), reads=["phi"], writes=["hi_all"])
            P.op(DVE, lambda e: e.tensor_copy(out=phiHL[0:64, :], in_=hi[0:64, :]), reads=["hi_all"], writes=["phiHL"])
            P.op(DVE, lambda e: e.tensor_tensor(out=phiHL[64:128, :], in0=phi[64:128, :], in1=hi[64:128, :],
                                                op=ALU.subtract), reads=["hi_all", "phi"], writes=["phiHL"])
            P.op(ACT, lambda e: e.activation(out=tmp[:], in_=dtT[:], func=AF.Ln), reads=["dtT", "dtmp"], writes=["dtmp"])
            P.op(DVE, lambda e: e.tensor_tensor(out=tmp[:], in0=tmp[:], in1=phi[:], op=ALU.subtract),
                 reads=["dtmp", "phi"], writes=["dtmp"])
            P.op(ACT, lambda e: e.activation(out=hi[:], in_=tmp[:], func=AF.Copy), reads=["dtmp", "phiHL"],
                 writes=["hi_all"])
            P.op(DVE, lambda e: e.tensor_copy(out=npsiHL[0:64, :], in_=hi[0:64, :]), reads=["hi_all"], writes=["npsiHL"])
            P.op(DVE, lambda e: e.tensor_tensor(out=npsiHL[64:128, :], in0=tmp[64:128, :], in1=hi[64:128, :],
                                                op=ALU.subtract), reads=["hi_all", "dtmp"], writes=["npsiHL"])
            eT = A.alloc("eT", [128, nch], F32)
            rd = A.alloc("rd", [64, 64, nch], F32)
            P.op(ACT, lambda e: e.activation(out=eT[:], in_=Tt[:], func=AF.Exp), reads=["Tt"], writes=["eT"])
            P.op(DVE, lambda e: e.tensor_tensor(out=rd[:], in0=eT[0:64, :].unsqueeze(1).to_broadcast([64, 64, nch]),
                                                in1=ident_f[0:64, 0:64].unsqueeze(2).to_broadcast([64, 64, nch]),
                                                op=ALU.mult), reads=["eT", "ident_f"], writes=["rd"])
            rdf = rd[:].rearrange("p r c -> p (r c)")
            dbf = dbc[:].rearrange("p r c -> p (r c)")
            tot = 64 * nch
            for h0 in range(0, tot, 512):
                pb = nps()
                P.op(PE, lambda e, h0=h0, pb=pb: e.matmul(ps[pb][:], lhsT=ones_f[0:64, :], rhs=rdf[:, h0:h0 + 512],
                                                          start=True, stop=True),
                     reads=["rd", "ones_f"], writes=[("ps", pb)])
                P.op(DVE, lambda e, h0=h0, pb=pb: e.tensor_copy(out=dbf[:, h0:h0 + 512], in_=ps[pb][:]),
                     reads=[("ps", pb)], writes=["dbc"])
        if barrier:
            A.release(m, fence)
        else:
            A.cur = m

    def ctx_phase():
        T = nseq * CTXL
        m = A.mark()
        wl = ws_load()
        wn = ws_norm()
        wf = ws_ffn(T)
        load_tokens(wl, ctx_d, T, xT, "xT")
        set_AS(V_N1, 0, 1, 4)
        norm_mod(wn, xT, "xT", T, hT, "hT")
        set_gate(2, 4, 0.5)
        ffn_block(wf, 0, 0, T, xT, "xT")
        set_AS(V_NM, 3, 4, 4)
        norm_mod(wn, xT, "xT", T, hT, "hT")
        dump("ctx_h", hT[:, 0, 0:T], ("hT", T // 512 - 1))
        A.release(m, fence)
        m = A.mark()
        R = ws_dt_out(T, CTXL, False)
        dt_prep(R, hT, "hT", T, CTXL, False)
        w_tok = R["w_tok"]
        wp = ws_proj(T, CTXL)
        BT = A.alloc("cBT", [128, 2, T], BF16)
        B_tok = A.alloc("cB_tok", [128, 2, T // 128, 128], BF16)
        for g in range(2):
            proj_conv_silu(wp, 16 + g, 8 + g, T, CTXL, BT[:, g, :], ("cBT", g), hT, "hT")
            to_tok(BT[:, g, :], ("cBT", g), T, B_tok[:, g], ("cB_tok", g))
        xcb = [A.alloc(f"cxh{i}", [128, T], BF16) for i in range(2)]
        xctb = [A.alloc(f"cxh_tok{i}", [128, T // 128, 128], BF16) for i in range(2)]
        xw = [A.alloc(f"cxw{i}", [128, 128], BF16) for i in range(2)]
        st = [A.alloc(f"cst{i}", [128, 512], F32) for i in range(2)]
        kk = 0

        def cprologue(hp_):
            bi = hp_ % 2
            proj_conv_silu(wp, 8 + hp_, hp_, T, CTXL, xcb[bi][:], ("cxh", bi), hT, "hT")
            to_tok(xcb[bi][:], ("cxh", bi), T, xctb[bi], ("cxh_tok", bi))
        cprologue(0)
        wl0 = ws_load()
        wn0 = ws_norm()
        set_AS(V_N1, 0, 1, 0)
        for hp in range(8):
            g = hp // 4
            xbi = hp % 2
            xc_tok = xctb[xbi]
            if hp + 1 < 8:
                cprologue(hp + 1)
            load_tokens(wl0, x_d[0], L, xT, "xT", 2 * hp, 2 * hp + 2)
            if hp in (6, 7):
                norm_mod(wn0, xT, "xT", L, hT, "hT", t0=hp - 4, t1=hp - 3)
            pb = None
            for s in range(nseq):
                if s % 2 == 0:
                    pb = nps()
                for dr in range(2):
                    col = (s % 2) * 256 + dr * 128
                    r0 = dr * 32 + 2 * hp
                    for j in range(2):
                        i = s * 2 + j
                        b = kk % 2
                        kk += 1
                        P.op(DVE, lambda e, i=i, r0=r0, b=b, xt_=xc_tok: e.tensor_tensor(
                            out=xw[b][:].rearrange("p (h q) -> p h q", h=2),
                            in0=xt_[:, i, :].rearrange("p (h q) -> p h q", h=2),
                            in1=w_tok[:, i, r0:r0 + 2].unsqueeze(2).to_broadcast([128, 2, 64]), op=ALU.mult),
                            reads=[(("cxh_tok", xbi), i // 4), "w_tok"], writes=[("cxw", b)])
                        P.op(PE, lambda e, i=i, g=g, b=b, j=j, pb=pb, col=col: e.matmul(
                            ps[pb][:, col:col + 128], lhsT=B_tok[:, g, i, :], rhs=xw[b][:], start=(j == 0), stop=(j == 1)),
                            reads=[("cB_tok", g), ("cxw", b)], writes=[("ps", pb)])
                if s % 2 == 1:
                    sb = st[(s // 2) % 2]
                    skey = ("cst", (s // 2) % 2)
                    P.op(DVE, lambda e, sb=sb, pb=pb: e.tensor_copy(out=sb[:], in_=ps[pb][:]), reads=[("ps", pb)],
                         writes=[skey])
                    for s2 in (s - 1, s):
                        for dr in range(2):
                            col = (s2 % 2) * 256 + dr * 128
                            P.dma(POOL, lambda e, s2=s2, dr=dr, col=col, sb=sb, hp=hp: e.dma_start(
                                out=h0_s[s2, dr, :, hp * 128:(hp + 1) * 128], in_=sb[:, col:col + 128]),
                                reads=[skey], writes=[("h0_s", s2)])
        A.release(m, fence)

    def conformer():
        m = A.mark()
        upad = [A.alloc(f"upad{i}", [128, 62 * 64], BF16) for i in range(2)]
        uconv = A.alloc("uconv", [128, 8, L], BF16)
        dg = [A.alloc(f"cdg{i}", [128, 31, 128], BF16) for i in range(2)]
        wv = [A.alloc(f"cwv{i}", [128, 8, 128], BF16) for i in range(2)]
        wg = [A.alloc(f"cwg{i}", [128, 8, 128], BF16) for i in range(2)]
        sgm = [A.alloc(f"csg{i}", [128, 512], F32) for i in range(2)]
        sq = [A.alloc(f"lsq{i}", [128, 512], BF16) for i in range(2)]
        mean1 = A.alloc("lmean", [128, 512], F32)
        rstd1 = A.alloc("lrstd", [128, 512], F32)
        mean, rstd = [mean1, mean1], [rstd1, rstd1]
        t1 = [A.alloc(f"lt{i}", [128, 512], F32) for i in range(3)]
        wo = ws_oproj(8)
        out_proj_load(wo, 8, 8)
        kk = 0
        for j in range(8):
            b = j % 2
            horiz = j < 4
            if j in (0, 1, 4, 5):
                P.op(DVE, lambda e, b=b: e.memset(upad[b][:], 0.0), reads=[("upad", b)], writes=[("upad", b)])
            P.dma(SP, lambda e, j=j, b=b: e.dma_start(out=wv[b][:], in_=win_s[20 + j]), reads=[("win_s", 20 + j)],
                  writes=[("cwv", b)])
            P.dma(SP, lambda e, j=j, b=b: e.dma_start(out=wg[b][:], in_=win_s[28 + j]), reads=[("win_s", 28 + j)],
                  writes=[("cwg", b)])
            for k in range(31):
                P.op(DVE, lambda e, k=k, j=j, b=b: e.tensor_scalar(out=dg[b][:, k, :], in0=ident_b[:],
                                                                  scalar1=cconvw[:, j, k:k + 1], scalar2=None,
                                                                  op0=ALU.mult),
                     reads=["ident_b", "cconvw"], writes=[("cdg", b)])
            if horiz:
                up3 = upad[b][:, 0:32 * 94].rearrange("p (r w) -> p r w", w=94)
            else:
                up3 = upad[b][:].rearrange("p (r w) -> p r w", w=64)
            for t in range(4):
                tb = slice(t * 512, (t + 1) * 512)
                pv, pg = nps(), nps()

                def mm(e, b=b, tb=tb, pv=pv, pg=pg):
                    ins = None
                    for kc in range(8):
                        ins = e.matmul(ps[pv][:], lhsT=wv[b][:, kc, :], rhs=hT[:, kc, tb], start=(kc == 0), stop=(kc == 7))
                    for kc in range(8):
                        ins = e.matmul(ps[pg][:], lhsT=wg[b][:, kc, :], rhs=hT[:, kc, tb], start=(kc == 0), stop=(kc == 7))
                    return ins
                P.op(PE, mm, reads=[("cwv", b), ("cwg", b), ("hT", t)], writes=[("ps", pv), ("ps", pg)])
                s = sgm[kk % 2]
                skey = ("csg", kk % 2)
                kk += 1
                P.op(ACT, lambda e, s=s, pg=pg: e.activation(out=s[:], in_=ps[pg][:], func=AF.Sigmoid),
                     reads=[("ps", pg)], writes=[skey])
                if horiz:
                    dv = up3[:, 8 * t:8 * t + 8, 15:79]
                else:
                    dv = up3[:, 15 + 8 * t:15 + 8 * t + 8, :]
                P.op(DVE, lambda e, s=s, pv=pv, dv=dv: e.tensor_tensor(
                    out=dv, in0=ps[pv][:].rearrange("p (r w) -> p r w", w=64),
                    in1=s[:].rearrange("p (r w) -> p r w", w=64), op=ALU.mult),
                    reads=[("ps", pv), skey, ("upad", b)], writes=[("upad", b)])
            for t in range(4):
                pb = nps()

                def cv(e, b=b, t=t, pb=pb, horiz=horiz, up3=up3):
                    ins = None
                    if horiz:
                        taps = list(range(31))
                    else:
                        taps = [k for k in range(31) if 8 * t + k + 7 >= 15 and 8 * t + k <= 46]
                    for n_, k in enumerate(taps):
                        if horiz:
                            rv = up3[:, 8 * t:8 * t + 8, k:k + 64]
                        else:
                            rv = up3[:, 8 * t + k:8 * t + k + 8, :]
                        ins = e.matmul(ps[pb][:], lhsT=dg[b][:, k, :], rhs=rv, start=(n_ == 0), stop=(n_ == len(taps) - 1))
                    return ins
                P.op(PE, cv, reads=[("upad", b), ("cdg", b)], writes=[("ps", pb)])
                P.op(ACT, lambda e, j=j, t=t, pb=pb: e.activation(out=uconv[:, j, t * 512:(t + 1) * 512], in_=ps[pb][:],
                                                                  func=AF.Identity, bias=vecs[:, V_CB, j:j + 1]),
                     reads=[("ps", pb), "vecs"], writes=[("uconv", t)])
        dump("uconv_pre", uconv[:, 0, :], ("uconv", 3))
        lnps = {}

        def ln_stats(t):
            tb = slice(t * 512, (t + 1) * 512)
            p_s, p_q = (4, 5) if t % 2 == 0 else (6, 7)
            lnps[t] = (p_s, p_q)
            for j in range(8):
                P.op(PE, lambda e, j=j, tb=tb, p_s=p_s: e.matmul(ps[p_s][:], lhsT=ones_b[:], rhs=uconv[:, j, tb],
                                                                 start=(j == 0), stop=(j == 7)),
                     reads=[("uconv", t), "ones_b"], writes=[("ps", p_s)])
                P.op(ACT, lambda e, j=j, tb=tb: e.activation(out=sq[j % 2][:], in_=uconv[:, j, tb], func=AF.Square),
                     reads=[("uconv", t)], writes=[("lsq", j % 2)])
                P.op(PE, lambda e, j=j, p_q=p_q: e.matmul(ps[p_q][:], lhsT=ones_b[:], rhs=sq[j % 2][:], start=(j == 0),
                                                          stop=(j == 7)),
                     reads=[("lsq", j % 2), "ones_b"], writes=[("ps", p_q)])

        def ln_norm(t):
            tb = slice(t * 512, (t + 1) * 512)
            p_s, p_q = lnps[t]
            mean_, rstd_ = mean[t % 2], rstd[t % 2]
            mk, rk = "lmean", "lrstd"
            P.op(DVE, lambda e: e.tensor_scalar(out=mean_[:], in0=ps[p_s][:], scalar1=1.0 / 1024, scalar2=None,
                                                op0=ALU.mult), reads=[("ps", p_s)], writes=[mk])
            P.op(DVE, lambda e: e.tensor_tensor(out=rstd_[:], in0=mean_[:], in1=mean_[:], op=ALU.mult), reads=[mk],
                 writes=[rk])
            P.op(DVE, lambda e: e.scalar_tensor_tensor(out=rstd_[:], in0=ps[p_q][:], scalar=1.0 / 1024,
                                                       in1=rstd_[:], op0=ALU.mult, op1=ALU.subtract),
                 reads=[("ps", p_q), rk], writes=[rk])
            P.op(ACT, lambda e: e.activation(out=rstd_[:], in_=rstd_[:], func=AF.Sqrt, bias=epsc[:, 0:1]),
                 reads=[rk, "epsc"], writes=[rk])
            P.op(DVE, lambda e: e.reciprocal(out=rstd_[:], in_=rstd_[:]), reads=[rk], writes=[rk])
            for j in range(8):
                tt = t1[j % 3]
                tk = ("lt", j % 3)
                P.op(DVE, lambda e, j=j, tt=tt: e.tensor_tensor(out=tt[:], in0=uconv[:, j, tb], in1=mean_[:],
                                                                op=ALU.subtract),
                     reads=[("uconv", t), mk], writes=[tk])
                P.op(DVE, lambda e, tt=tt: e.tensor_tensor(out=tt[:], in0=tt[:], in1=rstd_[:], op=ALU.mult),
                     reads=[tk, rk], writes=[tk])
                P.op(ACT, lambda e, j=j, tt=tt: e.activation(out=uconv[:, j, tb], in_=tt[:], func=AF.Silu,
                                                             scale=vecs[:, V_LW, j:j + 1],
                                                             bias=vecs[:, V_LB, j:j + 1]),
                     reads=[tk, "vecs"], writes=[("uconv", t)])
                if t >= 1:
                    out_proj_t(wo, uconv, "uconv", 8, t - 1, banks=[0, 1, 2, 3], o0=j, o1=j + 1)

        ln_stats(0)
        for t in range(4):
            if t + 1 < 4:
                ln_stats(t + 1)
            ln_norm(t)
        out_proj_t(wo, uconv, "uconv", 8, 3, banks=[0, 1, 2, 3])
        dump("uconv", uconv[:, 0, :], ("uconv", 3))
        A.release(m, fence)

    def ssd(s, after_xt=None):
        m = A.mark()
        OH = A.alloc("OH", [128, 32, 128], BF16)
        NEG = A.alloc("NEG", [128, 2, 256], BF16)
        P.dma(POOL, lambda e: e.dma_start(out=OH[:], in_=oh_d), writes=["OH"])
        P.dma(POOL, lambda e: e.dma_start(out=NEG[:], in_=neg_d), writes=["NEG"])
        R = ws_dt_out(L, 128, True)
        w_tok2, phiHL2, npsiHL2, dbc2 = R["w_tok"], R["phiHL"], R["npsiHL"], R["dbc"]
        m_ov = A.mark()
        yg = A.alloc("yg", [128, 4, L], BF16)
        hprev = A.alloc("hprev", [128, 2, 16, 128], BF16)
        S2 = [A.alloc(f"S{i}", [128, 2, 128], F32) for i in range(2)]
        xwb = [A.alloc(f"xwb{i}", [128, 8, 128], BF16)[:] for i in range(2)]
        eb = [A.alloc(f"eb{i}", [128, 512], BF16) for i in range(2)]
        GE = [A.alloc(f"GE{i}", [128, 512], BF16) for i in range(4)]
        szb = A.alloc("szb", [128, L], BF16)
        ytmp = A.alloc("ytmp", [128, 512], F32)
        m_ov_end = A.mark()
        wz = A.alloc("wz", [128, 8, 128], BF16)
        MB = A.alloc("MB", [128, 8, 512], BF16)
        B_tok = A.alloc("B_tok", [128, 16, 128], BF16)
        sv_cur, sv_peak = A.cur, A.peak
        A.cur = m_ov
        ops_dt = P.capture(lambda: dt_prep(R, hT, "hT", L, 128, True, barrier=False), [0, 1, 2, 3])
        assert A.peak <= max(sv_peak, m_ov_end), "dt temporaries exceed overlay region"
        A.cur = sv_cur
        wp = ws_proj(L, L)
        wp["rezero"] = True
        xpv = wp["xpad"]
        xwb = xwb + [xpv[:, 0, 0:1024].rearrange("p (c f) -> p c f", c=8),
                     xpv[:, 0, 1024:2048].rearrange("p (c f) -> p c f", c=8)]
        YB = [4, 5, 6, 7]
        EBANKS = [0, 1, 2, 3]
        ecnt = [0]
        for g in range(2):
            m2 = A.mark()
            BT = A.alloc("BT", [128, L], BF16)
            CT = A.alloc("CT", [128, L], BF16)

            def bc_stage(g=g, BT=BT, CT=CT):
                proj_conv_silu(wp, 16 + g, 8 + g, L, L, BT[:], "BT", hT, "hT")
                proj_conv_silu(wp, 18 + g, 10 + g, L, L, CT[:], "CT", hT, "hT")
                to_tok(BT[:], "BT", L, B_tok, "B_tok")
                for q in range(4):
                    pb = nps()

                    def bc(e, q=q, pb=pb):
                        ins = None
                        for j in range(4):
                            c = q * 4 + j
                            ins = e.matmul(ps[pb][:, j * 128:(j + 1) * 128], lhsT=BT[:, c * 128:(c + 1) * 128],
                                           rhs=CT[:, c * 128:(c + 1) * 128], start=True, stop=True)
                        return ins
                    P.op(PE, bc, reads=[("BT", q), ("CT", q)], writes=[("ps", pb)])
                    P.op(ACT, lambda e, q=q, pb=pb: e.activation(out=MB[:, 2 * q:2 * q + 2, 0:256],
                                                                 in_=ps[pb][:].rearrange("p (a x) -> p a x", a=2),
                                                                 func=AF.Copy),
                         reads=[("ps", pb)], writes=[("MB", q)])
                    P.op(DVE, lambda e, q=q: e.tensor_copy(out=MB[:, 2 * q:2 * q + 2, 256:512],
                                                           in_=CT[:, q * 512:(q + 1) * 512].rearrange("p (a x) -> p a x", a=2)),
                         reads=[("CT", q)], writes=[("MB", q)])
            if g == 0:
                ops_bc = P.capture(bc_stage, [4, 5, 6, 7])
                P.merge(ops_dt, ops_bc)
            else:
                bc_stage()
            if g == 0:
                dump("MB", MB[:].rearrange("p a b -> p (a b)"), ("MB", 3))
            A.release(m2, fence)
            m2 = A.mark()
            xhb = [A.alloc(f"xh{i}", [128, L], BF16) for i in range(2)]
            xtb = [A.alloc(f"xh_tok{i}", [128, 16, 128], BF16) for i in range(2)]

            def prologue(hq_):
                hp_ = 4 * g + hq_
                bi = hq_ % 2
                proj_conv_silu(wp, 8 + hp_, hp_, L, L, xhb[bi][:], ("xh", bi), hT, "hT")
                to_tok(xhb[bi][:], ("xh", bi), L, xtb[bi], ("xh_tok", bi))
            def rec_phase(hq, mid=None):
                hp = 4 * g + hq
                xbi = hq % 2
                xh, xh_tok = xhb[xbi], xtb[xbi]
                P.dma(SP, lambda e, hp=hp: e.dma_start(out=wz[:], in_=win_s[hp]), reads=[("win_s", hp)], writes=["wz"])
                for dr in range(2):
                    P.dma(SP, lambda e, dr=dr, hp=hp: e.dma_start(out=S2[0][:, dr, :], in_=h0_s[s, dr, :, hp * 128:(hp + 1) * 128]),
                          reads=[("h0_s", s)], writes=[("S", 0, dr, 0), ("S", 0, dr, 1)])
                orders = [list(range(16)), list(range(15, -1, -1))]

                def emit_cs(k):
                    for dr in range(2):
                        c = orders[dr][k]
                        i_ = 2 * k + dr
                        pb = i_ // 4
                        j = i_ % 4
                        buf = (0 if c < 8 else 2) if dr == 0 else (1 if c >= 8 else 3)
                        xv = xwb[buf]
                        rds = [("B_tok", c // 4), ("xwb", buf)] + (["xpad"] if buf >= 2 else [])
                        P.op(PE, lambda e, c=c, xv=xv, j=j, pb=pb: e.matmul(ps[pb][:, j * 128:(j + 1) * 128],
                                                                             lhsT=B_tok[:, c, :], rhs=xv[:, c % 8, :],
                                                                             start=True, stop=True),
                             reads=rds, writes=[("ps", pb)])

                for k in range(4):
                    emit_cs(k)
                if mid is not None:
                    mid()
                for k in range(4, 16):
                    emit_cs(k)
                for k in range(16):
                    Sc, Sn = S2[k % 2], S2[(k + 1) % 2]
                    P.op(ACT, lambda e, k=k, Sc=Sc: e.activation(out=hprev[:, :, k, :], in_=Sc[:, :, :], func=AF.Copy),
                         reads=[("S", k % 2, d_, h_) for d_ in range(2) for h_ in range(2)],
                         writes=[("hprev", 0, k), ("hprev", 1, 15 - k)])
                    for dr in range(2):
                        c = orders[dr][k]
                        i_ = 2 * k + dr
                        pb = i_ // 4
                        j = i_ % 4
                        r0 = dr * 32 + 2 * hp
                        if k < 15:
                            for hh in range(2):
                                hs = slice(hh * 64, (hh + 1) * 64)
                                P.op(DVE, lambda e, dr=dr, c=c, r0=r0, hh=hh, hs=hs, Sc=Sc, Sn=Sn, pb=pb, j=j:
                                     e.scalar_tensor_tensor(out=Sn[:, dr, hs], in0=Sc[:, dr, hs],
                                                            scalar=dbc2[:, r0 + hh, c:c + 1],
                                                            in1=ps[pb][:, j * 128 + hh * 64:j * 128 + (hh + 1) * 64],
                                                            op0=ALU.mult, op1=ALU.add),
                                     reads=[("S", k % 2, dr, hh), ("ps", pb), "dbc"], writes=[("S", (k + 1) % 2, dr, hh)])

            def main_loop(hq):
                hp = 4 * g + hq
                xbi = hq % 2
                xh_tok = xtb[xbi]
                def emit_ex_pair(b8, hh):
                    h = 2 * hp + hh
                    gs = []
                    for dr in range(2):
                        ridx = dr * 16 + h
                        pb = EBANKS[ecnt[0] % 4]
                        ebuf = ecnt[0] % 2
                        gbuf = ecnt[0] % 4
                        ecnt[0] += 1
                        cs = slice(b8 * 256, (b8 + 1) * 256)

                        def ex(e, ridx=ridx, dr=dr, pb=pb, cs=cs, b8=b8):
                            e.matmul(ps[pb][:, 0:256], lhsT=OH[:, ridx, :], rhs=phiHL2[:, cs], start=True, stop=False)
                            for j in range(2):
                                c = 2 * b8 + j
                                e.matmul(ps[pb][:, j * 128:(j + 1) * 128], lhsT=npsiHL2[:, c * 128:(c + 1) * 128],
                                         rhs=OH[:, ridx, :], start=False, stop=False)
                            e.matmul(ps[pb][:, 0:256], lhsT=ident_b[:], rhs=NEG[:, dr, :], start=False, stop=True)
                            return e.matmul(ps[pb][:, 256:512], lhsT=OH[:, ridx, :], rhs=phiHL2[:, cs], start=True,
                                            stop=True)
                        P.op(PE, ex, reads=["OH", "phiHL", "npsiHL", "NEG", "ident_b"], writes=[("ps", pb)])
                        P.op(ACT, lambda e, pb=pb, ebuf=ebuf: e.activation(out=eb[ebuf][:], in_=ps[pb][:], func=AF.Exp),
                             reads=[("ps", pb)], writes=[("eb", ebuf)])
                        P.op(DVE, lambda e, ebuf=ebuf, gbuf=gbuf, b8=b8: e.tensor_tensor(out=GE[gbuf][:], in0=eb[ebuf][:],
                                                                                          in1=MB[:, b8, :], op=ALU.mult),
                             reads=[("eb", ebuf), ("MB", b8 // 2)], writes=[("GE", gbuf)])
                        gs.append(gbuf)
                    return gs

                def emit_ym(b8, hh, gs):
                    ybank = YB[b8 // 2]
                    gf, gb = gs

                    def ym(e, hh=hh, b8=b8, ybank=ybank, gf=gf, gb=gb, xt_=xh_tok):
                        ins = None
                        for j in range(2):
                            c = 2 * b8 + j
                            col = ((b8 % 2) * 2 + j) * 128
                            o = ps[ybank][hh * 64:(hh + 1) * 64, col:col + 128]
                            xl = xt_[:, c, hh * 64:(hh + 1) * 64]
                            e.matmul(o, lhsT=xl, rhs=GE[gf][:, j * 128:(j + 1) * 128], start=True, stop=False)
                            e.matmul(o, lhsT=hprev[:, 0, c, hh * 64:(hh + 1) * 64],
                                     rhs=GE[gf][:, 256 + j * 128:256 + (j + 1) * 128], start=False, stop=False)
                            e.matmul(o, lhsT=xl, rhs=GE[gb][:, j * 128:(j + 1) * 128], start=False, stop=False)
                            ins = e.matmul(o, lhsT=hprev[:, 1, 15 - c, hh * 64:(hh + 1) * 64],
                                           rhs=GE[gb][:, 256 + j * 128:256 + (j + 1) * 128], start=False, stop=True)
                        return ins
                    P.op(PE, ym, reads=[("GE", gf), ("GE", gb), (("xh_tok", xbi), b8 // 2)] +
                         [("hprev", d_, 2 * b8 + j_) for d_ in range(2) for j_ in range(2)],
                         writes=[("ps", ybank)])

                units = [(b8, hh) for b8 in range(8) for hh in range(2)]
                cur = emit_ex_pair(*units[0])
                for ui, (b8, hh) in enumerate(units):
                    nxt_ = emit_ex_pair(*units[ui + 1]) if ui + 1 < len(units) else None
                    emit_ym(b8, hh, cur)
                    cur = nxt_

            def zproj(hq):
                for t in range(4):
                    tb = slice(t * 512, (t + 1) * 512)
                    pb = EBANKS[t % 4]

                    def mm(e, tb=tb, pb=pb):
                        ins = None
                        for kc in range(8):
                            ins = e.matmul(ps[pb][:], lhsT=wz[:, kc, :], rhs=hT[:, kc, tb], start=(kc == 0), stop=(kc == 7))
                        return ins
                    P.op(PE, mm, reads=["wz", ("hT", t)], writes=[("ps", pb)])
                    P.op(ACT, lambda e, tb=tb, pb=pb: e.activation(out=szb[:, tb], in_=ps[pb][:], func=AF.Silu),
                         reads=[("ps", pb)], writes=[("szb", t)])

            def gating(hq):
                hp = 4 * g + hq
                xbi = hq % 2
                xh = xhb[xbi]
                for t in range(4):
                    tb = slice(t * 512, (t + 1) * 512)
                    P.op(DVE, lambda e, t=t, tb=tb, hp=hp, xh_=xh: e.scalar_tensor_tensor(
                        out=ytmp[:], in0=xh_[:, tb], scalar=vecs[:, V_D, hp:hp + 1], in1=ps[YB[t]][:], op0=ALU.mult,
                        op1=ALU.add), reads=[("ps", YB[t]), (("xh", xbi), t), "vecs"], writes=["ytmp"])
                    P.op(DVE, lambda e, t=t, tb=tb, hq=hq: e.tensor_tensor(out=yg[:, hq, tb], in0=ytmp[:],
                                                                           in1=szb[:, tb], op=ALU.mult),
                         reads=["ytmp", ("szb", t)], writes=[("yg", t)])

            def xw_all(hq):
                hp = 4 * g + hq
                xbi = hq % 2
                xt_ = xtb[xbi]
                for buf, (dr, half) in enumerate([(0, 0), (1, 1), (0, 1), (1, 0)]):
                    c0 = half * 8
                    r0 = dr * 32 + 2 * hp
                    xv = xwb[buf]
                    P.op(DVE, lambda e, c0=c0, r0=r0, xv=xv, xt_=xt_: e.tensor_tensor(
                        out=xv.rearrange("p c (h q) -> p c h q", h=2),
                        in0=xt_[:, c0:c0 + 8, :].rearrange("p c (h q) -> p c h q", h=2),
                        in1=w_tok2[:, c0:c0 + 8, r0:r0 + 2].unsqueeze(3).to_broadcast([128, 8, 2, 64]), op=ALU.mult),
                        reads=[(("xh_tok", xbi), c0 // 4), (("xh_tok", xbi), c0 // 4 + 1), "w_tok"] +
                        (["xpad"] if buf >= 2 else []),
                        writes=[("xwb", buf)] + (["xpad"] if buf >= 2 else []))

            prologue(0)
            xw_all(0)
            rec_phase(0)
            prologue(1)
            zproj(0)
            for hq in range(4):
                if hq + 1 < 4:
                    xw_all(hq + 1)
                main_loop(hq)
                if hq + 1 < 4:
                    rec_phase(hq + 1, mid=lambda hq=hq: gating(hq))
                    if hq + 2 < 4:
                        prologue(hq + 2)
                    zproj(hq + 1)
                else:
                    gating(hq)
            if g == 0:
                dump("yg", yg[:, 0, :], ("yg", 3))
            A.release(m2, fence)
            m3 = A.mark()
            wn = ws_norm(2)
            wo = ws_oproj(4)
            out_proj_load(wo, 4 * g, 4)
            for t in range(4):
                def ccb(t_, c_, wo=wo):
                    if t_ >= 1:
                        out_proj_t(wo, yg, "yg", 4, t_ - 1, o0=2 * c_, o1=2 * c_ + 2)
                norm_mod(wn, yg, "yg", L, yg, "yg", nfc=4, denom=512.0, affine=False,
                         vec_scale=vecs[:, V_SN, 4 * g:4 * g + 4], t0=t, t1=t + 1, chunk_cb=ccb)
                if g == 1 and after_xt is not None and t >= 2:
                    after_xt(t - 2, wn)
            out_proj_t(wo, yg, "yg", 4, 3)
            if g == 1 and after_xt is not None:
                after_xt(2, wn)
                after_xt(3, wn)
            A.release(m3, fence)
        A.release(m, fence)

    def ws_final(top=False):
        al = A.alloc_top if top else A.alloc
        w = dict(fin=al("fin", [128, D], F32), ot=[al(f"ot{i}", [128, D], F32) for i in range(2)],
                 junk=al("junk", [128, 512], BF16), ss=[al(f"fss{i}", [128, 2], F32) for i in range(2)])
        P.dma(SP, lambda e: e.dma_start(out=w["fin"][:], in_=fin_d), writes=["fin"])
        return w

    def final_store(ws, s, i0=0, i1=L // 128):
        fin, ot, junk, ss = ws["fin"], ws["ot"], ws["junk"], ws["ss"]
        for i in range(i0, i1):
            o_ = ot[i % 2]
            okey = ("ot", i % 2)
            sk = ("fss", i % 2)
            s_ = ss[i % 2]
            for half in range(2):
                pb = nps()

                def tr(e, i=i, half=half, pb=pb):
                    ins = None
                    for c in range(4):
                        cc = half * 4 + c
                        ins = e.transpose(out=ps[pb][:, c * 128:(c + 1) * 128], in_=xT[:, cc, i * 128:(i + 1) * 128],
                                          identity=ident_f[:])
                    return ins
                P.op(PE, tr, reads=[("xT", i // 4), "ident_f"], writes=[("ps", pb)])
                if half == 0:
                    P.op(ACT, lambda e, o_=o_, pb=pb: e.activation(out=o_[:, 0:512], in_=ps[pb][:], func=AF.Copy),
                         reads=[("ps", pb)], writes=[okey])
                else:
                    P.op(DVE, lambda e, o_=o_, pb=pb: e.tensor_copy(out=o_[:, 512:1024], in_=ps[pb][:]),
                         reads=[("ps", pb)], writes=[okey])
            for hf in range(2):
                P.op(ACT, lambda e, o_=o_, s_=s_, hf=hf: e.activation(out=junk[:], in_=o_[:, hf * 512:(hf + 1) * 512],
                                                                      func=AF.Square, accum_out=s_[:, hf:hf + 1]),
                     reads=[okey], writes=["junk", sk])
            P.op(DVE, lambda e, s_=s_: e.tensor_tensor(out=s_[:, 0:1], in0=s_[:, 0:1], in1=s_[:, 1:2], op=ALU.add),
                 reads=[sk], writes=[sk])
            P.op(ACT, lambda e, s_=s_: e.activation(out=s_[:, 0:1], in_=s_[:, 0:1], func=AF.Sqrt, bias=epsc[:, 0:1],
                                                    scale=1.0 / 1024), reads=[sk, "epsc"], writes=[sk])
            P.op(DVE, lambda e, s_=s_: e.reciprocal(out=s_[:, 0:1], in_=s_[:, 0:1]), reads=[sk], writes=[sk])
            P.op(DVE, lambda e, o_=o_, s_=s_: e.scalar_tensor_tensor(out=o_[:], in0=o_[:], scalar=s_[:, 0:1], in1=fin[:],
                                                                     op0=ALU.mult, op1=ALU.mult),
                 reads=[okey, sk, "fin"], writes=[okey])
            P.dma(POOL, lambda e, i=i, o_=o_: e.dma_start(out=out_d[s, i * 128:(i + 1) * 128, :], in_=o_[:]),
                  reads=[okey], writes=[("out", s, i)])

    stages = build.stages
    if "ctx" in stages:
        ctx_phase()

    def alloc_ffn_phase():
        A.reset_top()
        return dict(wl=ws_load(top=True), wfin=ws_final(top=True), wn=ws_norm(), wf=ws_ffn(1024))

    m = A.mark()
    W = alloc_ffn_phase()
    if "ctx" not in stages:
        load_tokens(W["wl"], x_d[0], L, xT, "xT")
        set_AS(V_N1, 0, 1, 0)
        norm_mod(W["wn"], xT, "xT", L, hT, "hT")
    else:
        norm_mod(W["wn"], xT, "xT", L, hT, "hT", t0=0, t1=2)
    for s in range(nseq):
        wn, wf, wl, wfin = W["wn"], W["wf"], W["wl"], W["wfin"]
        set_gate(2, s, 0.5)
        hk = {}
        if s > 0:
            for i in range(8):
                def f(i=i, s=s, wl=wl, wfin=wfin):
                    final_store(wfin, s - 1, 8 + i, 9 + i)
                    if i == 0:
                        load_tokens(wl, x_d[s], L, xT, "xT", 8, 9, do_tr=False)
                    if i < 7:
                        load_tokens(wl, x_d[s], L, xT, "xT", 9 + i, 10 + i, do_tr=False)
                    load_tokens(wl, x_d[s], L, xT, "xT", 8 + i, 9 + i, do_dma=False)
                hk[("g", i)] = f
            hk[("g", 8)] = lambda wn=wn: norm_mod(wn, xT, "xT", L, hT, "hT", t0=2, t1=3)
            hk[("g", 9)] = lambda wn=wn: norm_mod(wn, xT, "xT", L, hT, "hT", t0=3, t1=4)
        ffn_block(wf, 0, 0, 1024, xT, "xT", hooks=hk)

        def hook_mix0(s=s, wn=wn):
            set_AS(V_NM, 3, 4, s)
            norm_mod(wn, xT, "xT", L, hT, "hT", t0=0, t1=1)
        hk = {("g", 2): hook_mix0, ("g", 3): lambda wn=wn: norm_mod(wn, xT, "xT", L, hT, "hT", t0=1, t1=2)}
        ffn_block(wf, 0, 1024, 1024, xT, "xT", hooks=hk)
        norm_mod(wn, xT, "xT", L, hT, "hT", t0=2, t1=4)
        dump("x1", xT[:, 0, :], ("xT", 3))
        set_gate(5, s, 1.0)
        A.release(m, fence)
        A.reset_top()
        if "conf" in stages:
            conformer()
            dump("x2a", xT[:, 0, :], ("xT", 3))
        set_AS(V_N2, 6, 7, s)

        def after_xt(t, wn_):
            norm_mod(wn_, xT, "xT", L, hT, "hT", t0=t, t1=t + 1)
        if "ssd" in stages:
            ssd(s, after_xt)
        else:
            mt = A.mark()
            wn_ = ws_norm()
            norm_mod(wn_, xT, "xT", L, hT, "hT")
            A.release(mt, fence)
        dump("x2", xT[:, 0, :], ("xT", 3))
        m = A.mark()
        W = alloc_ffn_phase()
        wn, wf, wl, wfin = W["wn"], W["wf"], W["wl"], W["wfin"]
        set_gate(8, s, 0.5)
        ffn_block(wf, 1, 0, 1024, xT, "xT")
        nxt = s + 1 < nseq
        hk = {}
        for i in range(8):
            def f(i=i, s=s, wl=wl, wfin=wfin, nxt=nxt):
                final_store(wfin, s, i, i + 1)
                if nxt:
                    if i == 0:
                        load_tokens(wl, x_d[s + 1], L, xT, "xT", 0, 1, do_tr=False)
                    if i < 7:
                        load_tokens(wl, x_d[s + 1], L, xT, "xT", i + 1, i + 2, do_tr=False)
                    load_tokens(wl, x_d[s + 1], L, xT, "xT", i, i + 1, do_dma=False)
            hk[("g", i)] = f
        if nxt:
            def hook_n1(s=s, wn=wn):
                set_AS(V_N1, 0, 1, s + 1)
                norm_mod(wn, xT, "xT", L, hT, "hT", t0=0, t1=1)
            hk[("g", 8)] = hook_n1
            hk[("g", 9)] = lambda wn=wn: norm_mod(wn, xT, "xT", L, hT, "hT", t0=1, t1=2)
        ffn_block(wf, 1, 1024, 1024, xT, "xT", hooks=hk)
        if not nxt:
            final_store(wfin, s, 8, 16)
    A.release(m, fence)
    P.finalize()
    build.info = dict(nops=len(P.ops), nwaits=P.nwaits, sig=P.sig_totals, peak=A.peak - A.base, cap=A.top - A.base)
    return nc


build.stages = ("ctx", "ffn1", "mix", "conf", "ssd", "ffn2")
build.info = {}


def host_consts():
    ident = np.eye(128, dtype=np.float32)
    oh = np.zeros((128, 32, 128), np.float32)
    for ridx in range(32):
        r = (ridx // 16) * 32 + ridx % 16
        oh[r, ridx, :] = 1.0
        oh[r + 64, ridx, :] = 1.0
    s_ = np.arange(128)[:, None]
    l_ = np.arange(128)[None, :]
    neg = np.zeros((128, 2, 256), np.float32)
    mf = np.where(s_ > l_, NEGV, 0.0).astype(np.float32)
    mb = np.where(s_ < l_, NEGV, 0.0).astype(np.float32)
    neg[:, 0, :] = np.concatenate([mf, mf], axis=1)
    neg[:, 1, :] = np.concatenate([mb, mb], axis=1)
    return ident, oh, neg


def fm(v, n):
    return np.ascontiguousarray(np.asarray(v, np.float32).reshape(n, 128).T)


def prep_inputs(inp, nseq=NSEQ, ncores=8):
    f32 = lambda a: np.ascontiguousarray(np.asarray(a, np.float32))
    ident, oh, neg = host_consts()
    vecs = np.zeros((128, 8, 8), np.float32)
    for i, k in enumerate(["norm_ffn1", "norm_mix", "norm_ffn2", "ssm_norm_w", "cconv_b", "cconv_ln_w", "cconv_ln_b"]):
        vecs[:, i, :] = fm(inp[k][0], 8)
    convwT = np.ascontiguousarray(np.asarray(inp["ssm_conv_w"][0], np.float32).T.reshape(12, 128, 5).transpose(1, 0, 2))
    convbT = fm(inp["ssm_conv_b"][0], 12)
    cconvwT = np.ascontiguousarray(np.asarray(inp["cconv_w"][0], np.float32).T.reshape(8, 128, 31).transpose(1, 0, 2))
    cols = np.zeros((128, 4), np.float32)
    for rep in range(2):
        for dr, (kb, ka) in enumerate([("dt_bias_fwd", "a_log_fwd"), ("dt_bias_bwd", "a_log_bwd")]):
            r0 = rep * 64 + dr * 32
            cols[r0:r0 + 16, 0] = np.asarray(inp[kb][0], np.float32)
            cols[r0:r0 + 16, 1] = np.asarray(inp[ka][0], np.float32)
            cols[r0:r0 + 32, 2] = 1.0 if dr == 0 else -1.0
            cols[r0:r0 + 32, 3] = 0.0 if dr == 0 else 1.0
    vecs[:, 7, :] = fm(np.repeat(np.asarray(inp["ssm_d"][0], np.float32), 64), 8)
    fin = np.ascontiguousarray(np.broadcast_to(np.asarray(inp["final_norm"], np.float32)[None, :], (128, D)))
    shared = {
        "w_mod": f32(inp["w_mod"][0]), "bmodT": fm(inp["b_mod"][0], 72), "vecs": vecs, "convwT": convwT,
        "convbT": convbT, "cconvwT": cconvwT, "cols": cols, "final_bc": fin, "ident": ident, "oh": oh,
        "neg": neg, "ffn1_gate": f32(inp["ffn1_gate"][0]), "ffn1_up": f32(inp["ffn1_up"][0]),
        "ffn1_down": f32(inp["ffn1_down"][0]), "ffn2_gate": f32(inp["ffn2_gate"][0]), "ffn2_up": f32(inp["ffn2_up"][0]),
        "ffn2_down": f32(inp["ffn2_down"][0]), "w_in": f32(inp["w_in"][0]), "w_out": f32(inp["w_out"][0]),
    }
    x = np.asarray(inp["x"], np.float32)
    ctx = np.asarray(inp["ctx"], np.float32)
    c = np.asarray(inp["c"], np.float32)
    c_ctx = np.asarray(inp["c_ctx"], np.float32)
    maps = []
    for k in range(ncores):
        b0 = k * nseq
        cin = np.zeros((128, 8, 5), np.float32)
        for j in range(nseq):
            cin[:, :, j] = fm(c[b0 + j], 8)
        cin[:, :, 4] = fm(c_ctx, 8)
        mp = dict(shared)
        mp["x"] = np.ascontiguousarray(x[b0:b0 + nseq])
        mp["ctx"] = np.ascontiguousarray(ctx[b0:b0 + nseq].reshape(nseq * CTXL, D))
        mp["cin"] = cin
        maps.append(mp)
    return maps


def kernel(**inputs):
    nc = build(NSEQ)
    maps = prep_inputs(inputs, NSEQ, 8)
    res = run_bass_kernel_spmd(nc, maps, core_ids=list(range(8)))
    out = np.concatenate([np.asarray(r["out"], np.float32) for r in res.results], axis=0)
    return out
```

```python
import numpy as np
import concourse.bass as bass
import concourse.mybir as mybir
from concourse.bass_utils import run_bass_kernel_spmd

F32, BF16 = mybir.dt.float32, mybir.dt.bfloat16
ALU = mybir.AluOpType
AF = mybir.ActivationFunctionType

PE, ACT, DVE, POOL, SP = "pe", "act", "dve", "pool", "sp"
COMPUTE = (PE, ACT, DVE, POOL)

D = 1024
L = 2048
CTXL = 256
NSEQ = 4
DFF = 2816
NFC = 22
EPS = 1e-6
OFF_X, OFF_B, OFF_C, OFF_DT, OFF_GLU = 1024, 2048, 2304, 2560, 2592
NEGV = -30000.0


class Op:
    __slots__ = ("eng", "fn", "reads", "writes", "is_dma", "deps", "signal", "sigval", "dsem", "dval",
                 "dprev", "barrier", "nobar")

    def __init__(self, eng, fn, reads, writes, is_dma, barrier=False):
        self.eng, self.fn, self.reads, self.writes, self.is_dma = eng, fn, reads, writes, is_dma
        self.deps = None
        self.signal = False
        self.sigval = 0
        self.dsem = None
        self.dval = 0
        self.dprev = None
        self.barrier = barrier
        self.nobar = False


class Prog:
    def __init__(self, nc):
        self.nc = nc
        self.ops = []
        self.n_dma_sems = {SP: 8, POOL: 4, ACT: 4}
        self._ps = 0
        self._ps_set = None

    def capture(self, fn, banks):
        saved, self.ops = self.ops, []
        sv_set, self._ps_set = self._ps_set, banks
        fn()
        out, self.ops = self.ops, saved
        self._ps_set = sv_set
        return out

    def merge(self, a, b):
        ia = ib = 0
        while ia < len(a) or ib < len(b):
            fa = ia / len(a) if a else 2.0
            fb = ib / len(b) if b else 2.0
            if ia < len(a) and (fa <= fb or ib >= len(b)):
                self.ops.append(a[ia])
                ia += 1
            else:
                self.ops.append(b[ib])
                ib += 1

    def op(self, eng, fn, reads=(), writes=()):
        self.ops.append(Op(eng, fn, tuple(reads), tuple(writes), False))

    def dma(self, eng, fn, reads=(), writes=(), nobar=False):
        o = Op(eng, fn, tuple(reads), tuple(writes), True)
        o.nobar = nobar
        self.ops.append(o)

    def barrier(self, fn):
        self.ops.append(Op(DVE, fn, (), (), False, barrier=True))

    def finalize(self):
        nc, ops = self.nc, self.ops
        last_writer, readers = {}, {}
        last_on = {}
        dmas_since = []
        cur_bar = None
        for o in ops:
            deps = set()
            if o.barrier:
                for e in COMPUTE:
                    if e in last_on:
                        deps.add(last_on[e])
                deps.update(dmas_since)
                dmas_since = []
                cur_bar = o
            else:
                if cur_bar is not None:
                    deps.add(cur_bar)
                for r in o.reads:
                    w = last_writer.get(r)
                    if w is not None:
                        deps.add(w)
                for w_ in o.writes:
                    w = last_writer.get(w_)
                    if w is not None:
                        deps.add(w)
                    deps.update(readers.get(w_, ()))
                for r in o.reads:
                    readers.setdefault(r, []).append(o)
                for w_ in o.writes:
                    last_writer[w_] = o
                    readers[w_] = []
            deps.discard(o)
            o.deps = deps
            if o.is_dma:
                if not o.nobar:
                    dmas_since.append(o)
            else:
                last_on[o.eng] = o
        dma_count, dma_hist = {}, {}
        for o in ops:
            if o.is_dma:
                n = self.n_dma_sems[o.eng]
                k = dma_count.get(o.eng, 0)
                dma_count[o.eng] = k + 1
                slot = k % n
                o.dsem = (o.eng, slot)
                o.dval = 16 * (k // n + 1)
                hist = dma_hist.setdefault((o.eng, slot), [])
                o.dprev = hist[-1] if hist else None
                hist.append(o)
        for o in ops:
            for d in o.deps:
                if d.is_dma:
                    continue
                if d.eng == PE and o.eng == PE and not o.is_dma:
                    continue
                d.signal = True
        cnt = {e: 0 for e in COMPUTE}
        for o in ops:
            if not o.is_dma and o.signal:
                cnt[o.eng] += 1
                o.sigval = cnt[o.eng]
        self.sig_totals = dict(cnt)
        sems = {e: nc.alloc_semaphore(name=f"s_{e}") for e in COMPUTE}
        dsems = {k: nc.alloc_semaphore(name=f"d_{k[0]}{k[1]}") for k in dma_hist}
        waited = {}
        nwaits = 0
        for o in ops:
            need = {}
            for d in o.deps:
                if d.is_dma:
                    key, val = ("d",) + d.dsem, d.dval
                else:
                    if d.eng == PE and o.eng == PE and not o.is_dma:
                        continue
                    key, val = ("c", d.eng), d.sigval
                if need.get(key, 0) < val:
                    need[key] = val
            if o.is_dma and o.dprev is not None:
                key, val = ("d",) + o.dsem, o.dprev.dval
                if need.get(key, 0) < val:
                    need[key] = val
            wl = []
            for key, val in need.items():
                wk = (o.eng, key)
                if waited.get(wk, 0) >= val:
                    continue
                waited[wk] = val
                wl.append((sems[key[1]] if key[0] == "c" else dsems[(key[1], key[2])], val))
                nwaits += 1
            o.deps = wl
        self.nwaits = nwaits
        final_d = [(dsems[k], h[-1].dval) for k, h in dma_hist.items()]

        def emit_engine(eng_name):
            def body(eng):
                for o in ops:
                    if o.eng != eng_name:
                        continue
                    for sem, val in o.deps:
                        eng.wait_ge(sem, val)
                    ins = o.fn(eng)
                    if o.is_dma:
                        ins.then_inc(dsems[o.dsem], 16)
                    elif o.signal:
                        ins.then_inc(sems[o.eng], 1)
                if eng_name == SP:
                    for sem, val in final_d:
                        eng.wait_ge(sem, val)
            return body

        with nc.Block() as block:
            block.sync(emit_engine(SP))
            block.tensor(emit_engine(PE))
            block.scalar(emit_engine(ACT))
            block.vector(emit_engine(DVE))
            block.gpsimd(emit_engine(POOL))


class Arena:
    def __init__(self, nc, P, base, top):
        self.nc, self.P = nc, P
        self.base = (base + 63) // 64 * 64
        self.top = top
        self.cur = self.base
        self.top_cur = top
        self.n = 0
        self.peak = 0

    def alloc(self, name, shape, dtype):
        esz = 4 if dtype == F32 else 2
        nbytes = esz
        for s in shape[1:]:
            nbytes *= s
        nbytes = (nbytes + 63) // 64 * 64
        off = self.cur
        self.cur += nbytes
        assert self.cur <= self.top_cur, f"SBUF overflow allocating {name}: {self.cur} > {self.top_cur}"
        self.peak = max(self.peak, self.cur)
        self.n += 1
        return self.nc.alloc_sbuf_tensor_at(f"{name}_{self.n}", list(shape), dtype, offset=off)

    def alloc_top(self, name, shape, dtype):
        esz = 4 if dtype == F32 else 2
        nbytes = esz
        for s_ in shape[1:]:
            nbytes *= s_
        nbytes = (nbytes + 63) // 64 * 64
        self.top_cur -= nbytes
        assert self.top_cur >= self.cur, f"SBUF overflow (top) allocating {name}"
        self.n += 1
        return self.nc.alloc_sbuf_tensor_at(f"{name}_{self.n}", list(shape), dtype, offset=self.top_cur)

    def reset_top(self):
        self.top_cur = self.top

    def mark(self):
        return self.cur

    def release(self, m, fence):
        self.cur = m
        self.P.barrier(lambda e: e.memset(fence[:], 0.0))


def build(nseq=NSEQ, debug=None):
    nc = bass.Bass("TRN2", target_bir_lowering=False)
    P = Prog(nc)
    dbg_out = {}

    def din(name, shape, dt=F32):
        return nc.dram_tensor(name, list(shape), dt, kind="ExternalInput").ap()

    def dscr(name, shape, dt=BF16):
        return nc.dram_tensor(name, list(shape), dt, kind="Internal").ap()

    x_d = din("x", [nseq, L, D])
    ctx_d = din("ctx", [nseq * CTXL, D])
    cin_d = din("cin", [128, 8, 5])
    wmod_d = din("w_mod", [D, 9 * D])
    bmod_d = din("bmodT", [128, 72])
    vec_d = din("vecs", [128, 8, 8])
    convw_d = din("convwT", [128, 12, 5])
    convb_d = din("convbT", [128, 12])
    cconvw_d = din("cconvwT", [128, 8, 31])
    col_d = din("cols", [128, 4])
    fin_d = din("final_bc", [128, D])
    ident_d = din("ident", [128, 128])
    oh_d = din("oh", [128, 32, 128])
    neg_d = din("neg", [128, 2, 256])
    wg_d = [din("ffn1_gate", [D, DFF]), din("ffn2_gate", [D, DFF])]
    wu_d = [din("ffn1_up", [D, DFF]), din("ffn2_up", [D, DFF])]
    wd_d = [din("ffn1_down", [DFF, D]), din("ffn2_down", [DFF, D])]
    win_d = din("w_in", [D, 4640])
    wout_d = din("w_out", [2 * D, D])
    out_d = nc.dram_tensor("out", [nseq, L, D], F32, kind="ExternalOutput").ap()

    wg_s = [dscr(f"wg_s{i}", [11, 128, 8, 256]) for i in range(2)]
    wu_s = [dscr(f"wu_s{i}", [11, 128, 8, 256]) for i in range(2)]
    wd_s = [dscr(f"wd_s{i}", [8, 128, NFC, 128]) for i in range(2)]
    win_cols = [128 * i for i in range(20)] + [OFF_GLU + 128 * i for i in range(16)]
    win_s = dscr("win_s", [36, 128, 8, 128])
    wout_s = dscr("wout_s", [8, 128, 16, 128])
    h0_s = dscr("h0_s", [nseq, 2, 128, D], F32)

    if debug:
        for nm, shp in debug.items():
            dbg_out[nm] = nc.dram_tensor("dbg_" + nm, list(shp), F32, kind="ExternalOutput").ap()

    A = Arena(nc, P, int(nc.sbuf_base) + 64, int(nc.sbuf_top))
    ident_f = A.alloc("ident_f", [128, 128], F32)
    ident_b = A.alloc("ident_b", [128, 128], BF16)
    ones_b = A.alloc("ones_b", [128, 128], BF16)
    ones_f = A.alloc("ones_f", [128, 128], F32)
    fence = A.alloc("fence", [128, 16], F32)
    modT = A.alloc("modT", [128, 72, 5], F32)
    vecs = A.alloc("vecs", [128, 8, 8], F32)
    convw = A.alloc("convw", [128, 12, 5], F32)
    convb = A.alloc("convb", [128, 12], F32)
    cconvw = A.alloc("cconvw", [128, 8, 31], F32)
    cols = A.alloc("cols", [128, 4], F32)
    a128 = A.alloc("a128", [128, 1], F32)
    epsc = A.alloc("epsc", [128, 1], F32)
    onec = A.alloc("onec", [128, 1], F32)
    wdt = A.alloc("wdt", [128, 8, 128], BF16)
    AS = A.alloc("AS", [128, 2, 8], F32)
    GV = A.alloc("GV", [128, 8], F32)
    xT = A.alloc("xT", [128, 8, L], F32)
    hT = A.alloc("hT", [128, 8, L], BF16)
    ps = [nc.alloc_psum_tensor(f"ps{i}", [128, 512], F32) for i in range(8)]

    def nps():
        st = P._ps_set
        if st is not None:
            i = st[P._ps % len(st)]
            P._ps += 1
            return i
        i = P._ps % 8
        P._ps = (i + 1) % 8
        return i

    def dump(name, src_ap, key):
        if debug and name in dbg_out:
            q = SP if src_ap.dtype == F32 else POOL
            P.dma(q, lambda e: e.dma_start(out=dbg_out[name], in_=src_ap), reads=[key], writes=["dbg_" + name])

    V_N1, V_NM, V_N2, V_SN, V_CB, V_LW, V_LB, V_D = range(8)

    P.dma(SP, lambda e: e.dma_start(out=ident_f[:], in_=ident_d), writes=["ident_f"])
    P.dma(SP, lambda e: e.dma_start(out=vecs[:], in_=vec_d), writes=["vecs"])
    P.dma(SP, lambda e: e.dma_start(out=convw[:], in_=convw_d), writes=["convw"])
    P.dma(SP, lambda e: e.dma_start(out=convb[:], in_=convb_d), writes=["convb"])
    P.dma(SP, lambda e: e.dma_start(out=cconvw[:], in_=cconvw_d), writes=["cconvw"])
    P.dma(SP, lambda e: e.dma_start(out=cols[:], in_=col_d), writes=["cols"])
    P.op(DVE, lambda e: e.memset(epsc[:], EPS), writes=["epsc"])
    P.op(DVE, lambda e: e.memset(onec[:], 1.0), writes=["onec"])
    P.op(DVE, lambda e: e.tensor_copy(out=ident_b[:], in_=ident_f[:]), reads=["ident_f"], writes=["ident_b"])
    P.op(DVE, lambda e: e.memset(ones_b[:], 1.0), writes=["ones_b"])
    P.op(DVE, lambda e: e.memset(ones_f[:], 1.0), writes=["ones_f"])
    P.op(DVE, lambda e: e.memset(fence[:], 0.0), writes=["fence"])
    P.op(ACT, lambda e: e.activation(out=a128[:], in_=cols[:, 1:2], func=AF.Exp), reads=["cols"], writes=["a128"])
    P.op(DVE, lambda e: e.tensor_scalar(out=a128[:], in0=a128[:], scalar1=-1.0, scalar2=None, op0=ALU.mult),
         reads=["a128"], writes=["a128"])
    P.op(POOL, lambda e: e.memset(wdt[:], 0.0), writes=["wdt"])
    win_v = win_d.rearrange("(kc kp) f -> kp kc f", kp=128)
    for rep in range(2):
        for dr in range(2):
            c0 = rep * 64 + dr * 32
            P.dma(POOL, lambda e, c0=c0, dr=dr: e.dma_start(out=wdt[:, :, c0:c0 + 16],
                                                           in_=win_v[:, :, OFF_DT + 16 * dr:OFF_DT + 16 * dr + 16]),
                  reads=["wdt"], writes=["wdt"])

    def cast_ffn(i):
        gv = wg_d[i].rearrange("(kc kp) (fq j) -> fq kp kc j", kp=128, j=256)
        uv = wu_d[i].rearrange("(kc kp) (fq j) -> fq kp kc j", kp=128, j=256)
        for fq in range(11):
            P.dma(POOL, lambda e, fq=fq: e.dma_start(out=wg_s[i][fq], in_=gv[fq]), writes=[("wg_s", i, fq)], nobar=True)
            P.dma(POOL, lambda e, fq=fq: e.dma_start(out=wu_s[i][fq], in_=uv[fq]), writes=[("wu_s", i, fq)], nobar=True)
        dv = wd_d[i].rearrange("(fc fp) (o j) -> o fp fc j", fp=128, j=128)
        for o in range(8):
            P.dma(POOL, lambda e, o=o: e.dma_start(out=wd_s[i][o], in_=dv[o]), writes=[("wd_s", i, o)], nobar=True)

    def cast_mixer():
        for t, c0 in enumerate(win_cols):
            P.dma(POOL, lambda e, t=t, c0=c0: e.dma_start(out=win_s[t], in_=win_v[:, :, c0:c0 + 128]),
                  writes=[("win_s", t)], nobar=True)
        ov = wout_d.rearrange("(kc kp) (o j) -> o kp kc j", kp=128, j=128)
        for o in range(8):
            P.dma(POOL, lambda e, o=o: e.dma_start(out=wout_s[o], in_=ov[o]), writes=[("wout_s", o)], nobar=True)


    m0 = A.mark()
    cin = A.alloc("cin", [128, 8, 5], F32)
    bmod = A.alloc("bmod", [128, 72], F32)
    wm = [A.alloc(f"wm{i}", [128, 8, 512], BF16) for i in range(4)]
    cinb = A.alloc("cinb", [128, 8, 5], BF16)
    P.dma(SP, lambda e: e.dma_start(out=cin[:], in_=cin_d), writes=["cin"])
    P.dma(SP, lambda e: e.dma_start(out=bmod[:], in_=bmod_d), writes=["bmod"])
    P.op(ACT, lambda e: e.activation(out=cinb[:], in_=cin[:], func=AF.Silu), reads=["cin"], writes=["cinb"])
    wmod_v = wmod_d.rearrange("(kc kp) f -> kp kc f", kp=128)
    pm = nps()

    def mod_piece(q):
        P.dma(POOL, lambda e, q=q: e.dma_start(out=wm[q % 4][:], in_=wmod_v[:, :, q * 512:(q + 1) * 512]),
              writes=[("wm", q % 4)])

        def mm(e, q=q):
            ins = None
            for j in range(4):
                col = (q * 4 + j) * 5
                for kc in range(8):
                    ins = e.matmul(ps[pm][:, col:col + 5], lhsT=wm[q % 4][:, kc, j * 128:(j + 1) * 128],
                                   rhs=cinb[:, kc, :], start=(kc == 0), stop=(kc == 7))
            return ins
        P.op(PE, mm, reads=[("wm", q % 4), "cinb"], writes=[("ps", pm)])
    for q in range(18):
        mod_piece(q)
    cast_ffn(0)
    cast_mixer()
    cast_ffn(1)
    P.op(DVE, lambda e: e.tensor_tensor(out=modT[:], in0=ps[pm][:, 0:360].rearrange("p (j b) -> p j b", b=5),
                                        in1=bmod[:].unsqueeze(2).to_broadcast([128, 72, 5]), op=ALU.add),
         reads=[("ps", pm), "bmod"], writes=["modT"])
    dump("modT", modT[:], "modT")
    A.release(m0, fence)

    def set_AS(norm_idx, j_shift, j_scale, col):
        def f(e):
            return e.scalar_tensor_tensor(out=AS[:, 0, :], in0=modT[:, 8 * j_scale:8 * j_scale + 8, col], scalar=1.0,
                                          in1=vecs[:, norm_idx, :], op0=ALU.add, op1=ALU.mult)
        P.op(DVE, f, reads=["modT", "vecs"], writes=["AS"])
        P.op(DVE, lambda e: e.tensor_copy(out=AS[:, 1, :], in_=modT[:, 8 * j_shift:8 * j_shift + 8, col]),
             reads=["modT"], writes=["AS"])

    def set_gate(j, col, mul):
        P.op(DVE, lambda e: e.tensor_scalar(out=GV[:], in0=modT[:, 8 * j:8 * j + 8, col], scalar1=mul, scalar2=None,
                                            op0=ALU.mult), reads=["modT"], writes=["GV"])

    def ws_load(top=False):
        al = A.alloc_top if top else A.alloc
        return dict(stg=[al(f"stg{i}", [128, D], F32) for i in range(2)])

    def load_tokens(ws, src, T, dstT, dkey, i0=0, i1=None, do_dma=True, do_tr=True):
        stg = ws["stg"]
        for i in range(i0, T // 128 if i1 is None else i1):
            s = stg[i % 2]
            if do_dma:
                P.dma(SP, lambda e, i=i, s=s: e.dma_start(out=s[:], in_=src[i * 128:(i + 1) * 128, :]),
                      writes=[("stg", i % 2)])
            if not do_tr:
                continue
            for half in range(2):
                pb = nps()

                def tr(e, s=s, half=half, pb=pb):
                    ins = None
                    for c in range(4):
                        cc = half * 4 + c
                        ins = e.transpose(out=ps[pb][:, c * 128:(c + 1) * 128], in_=s[:, cc * 128:(cc + 1) * 128],
                                          identity=ident_f[:])
                    return ins
                P.op(PE, tr, reads=[("stg", i % 2), "ident_f"], writes=[("ps", pb)])
                src_v = ps[pb][:].rearrange("p (c t) -> p c t", c=4)
                dst_v = dstT[:, half * 4:half * 4 + 4, i * 128:(i + 1) * 128]
                if half == 0:
                    P.op(ACT, lambda e, a=dst_v, b=src_v: e.activation(out=a, in_=b, func=AF.Copy),
                         reads=[("ps", pb)], writes=[(dkey, i // 4)])
                else:
                    P.op(DVE, lambda e, a=dst_v, b=src_v: e.tensor_copy(out=a, in_=b),
                         reads=[("ps", pb)], writes=[(dkey, i // 4)])

    def ws_norm(ntmp=2):
        tmp = [A.alloc(f"nt{i}", [128, 512], F32) for i in range(ntmp)]
        return dict(sq=[A.alloc(f"sq{i}", [128, 512], BF16) for i in range(2)], rs=A.alloc("rs", [128, 512], F32),
                    tmp=[tmp[i % ntmp] for i in range(2)], ntmp=ntmp)

    def norm_mod(ws, srcT, skey, T, dst, dkey, nfc=8, denom=1024.0, affine=True, vec_scale=None, t0=0, t1=None,
                 chunk_cb=None):
        sq, rs, tmp = ws["sq"], ws["rs"], ws["tmp"]
        for t in range(t0, T // 512 if t1 is None else t1):
            tb = slice(t * 512, (t + 1) * 512)
            pb = nps()
            for c in range(nfc):
                P.op(ACT, lambda e, c=c, tb=tb: e.activation(out=sq[c % 2][:], in_=srcT[:, c, tb], func=AF.Square),
                     reads=[(skey, t)], writes=[("sq", c % 2)])
                P.op(PE, lambda e, c=c, pb=pb: e.matmul(ps[pb][:], lhsT=ones_b[:], rhs=sq[c % 2][:], start=(c == 0),
                                                        stop=(c == nfc - 1)),
                     reads=[("sq", c % 2), "ones_b"], writes=[("ps", pb)])
            P.op(ACT, lambda e, pb=pb: e.activation(out=rs[:], in_=ps[pb][:], func=AF.Sqrt, bias=epsc[:, 0:1],
                                                    scale=1.0 / denom),
                 reads=[("ps", pb), "epsc"], writes=["rs"])
            P.op(DVE, lambda e: e.reciprocal(out=rs[:], in_=rs[:]), reads=["rs"], writes=["rs"])
            for c in range(nfc):
                P.op(DVE, lambda e, c=c, tb=tb: e.tensor_tensor(out=tmp[c % 2][:], in0=srcT[:, c, tb], in1=rs[:],
                                                                op=ALU.mult),
                     reads=[(skey, t), "rs"], writes=[("nt", c % ws["ntmp"])])
                if affine:
                    P.op(ACT, lambda e, c=c, tb=tb: e.activation(out=dst[:, c, tb], in_=tmp[c % 2][:], func=AF.Identity,
                                                                 scale=AS[:, 0, c:c + 1], bias=AS[:, 1, c:c + 1]),
                         reads=[("nt", c % ws["ntmp"]), "AS"], writes=[(dkey, t)])
                else:
                    P.op(ACT, lambda e, c=c, tb=tb: e.activation(out=dst[:, c, tb], in_=tmp[c % 2][:], func=AF.Identity,
                                                                 scale=vec_scale[:, c:c + 1]),
                         reads=[("nt", c % ws["ntmp"]), "vecs"], writes=[(dkey, t)])
                if chunk_cb is not None:
                    chunk_cb(t, c)

    def ws_ffn(TB):
        return dict(hid=A.alloc("hid", [128, NFC, TB], BF16),
                    wgb=[A.alloc(f"wgb{j}", [128, 8, 256], BF16) for j in range(2)],
                    wub=[A.alloc(f"wub{j}", [128, 8, 256], BF16) for j in range(2)],
                    wdb=[A.alloc(f"wdb{j}", [128, NFC, 128], BF16) for j in range(2)],
                    sg=[A.alloc(f"sg{j}", [128, 512], F32) for j in range(2)], k=[0])

    def ffn_block(ws, i, tok0, TB, resT, rkey, hooks=None):
        nt = TB // 512
        hid, wgb, wub, wdb, sg = ws["hid"], ws["wgb"], ws["wub"], ws["wdb"], ws["sg"]
        for fq in range(11):
            b = fq % 2
            P.dma(SP, lambda e, fq=fq, b=b: e.dma_start(out=wgb[b][:], in_=wg_s[i][fq]), reads=[("wg_s", i, fq)],
                  writes=[("wgb", b)])
            P.dma(SP, lambda e, fq=fq, b=b: e.dma_start(out=wub[b][:], in_=wu_s[i][fq]), reads=[("wu_s", i, fq)],
                  writes=[("wub", b)])
            for j in range(2):
                f = 2 * fq + j
                for t in range(nt):
                    tok = slice(tok0 + t * 512, tok0 + (t + 1) * 512)
                    pg, pu = nps(), nps()

                    def mm(e, b=b, j=j, tok=tok, pg=pg, pu=pu):
                        ins = None
                        for kc in range(8):
                            ins = e.matmul(ps[pg][:], lhsT=wgb[b][:, kc, j * 128:(j + 1) * 128], rhs=hT[:, kc, tok],
                                           start=(kc == 0), stop=(kc == 7))
                        for kc in range(8):
                            ins = e.matmul(ps[pu][:], lhsT=wub[b][:, kc, j * 128:(j + 1) * 128], rhs=hT[:, kc, tok],
                                           start=(kc == 0), stop=(kc == 7))
                        return ins
                    P.op(PE, mm, reads=[("wgb", b), ("wub", b), ("hT", (tok0 + t * 512) // 512)],
                         writes=[("ps", pg), ("ps", pu)])
                    kb = ws["k"][0] % 2
                    ws["k"][0] += 1
                    s = sg[kb]
                    P.op(ACT, lambda e, s=s, pg=pg: e.activation(out=s[:], in_=ps[pg][:], func=AF.Silu),
                         reads=[("ps", pg)], writes=[("sg", kb)])
                    P.op(DVE, lambda e, s=s, pu=pu, f=f, t=t: e.tensor_tensor(out=hid[:, f, t * 512:(t + 1) * 512],
                                                                              in0=ps[pu][:], in1=s[:], op=ALU.mult),
                         reads=[("ps", pu), ("sg", kb)], writes=[("hid", f, t)])
            if hooks and ("g", fq) in hooks:
                hooks[("g", fq)]()
        for o in range(8):
            b = o % 2
            P.dma(SP, lambda e, o=o, b=b: e.dma_start(out=wdb[b][:], in_=wd_s[i][o]), reads=[("wd_s", i, o)],
                  writes=[("wdb", b)])
            for t in range(nt):
                tok = slice(tok0 + t * 512, tok0 + (t + 1) * 512)
                pb = nps()

                def mm(e, b=b, t=t, pb=pb):
                    ins = None
                    for fc in range(NFC):
                        ins = e.matmul(ps[pb][:], lhsT=wdb[b][:, fc, :], rhs=hid[:, fc, t * 512:(t + 1) * 512],
                                       start=(fc == 0), stop=(fc == NFC - 1))
                    return ins
                P.op(PE, mm, reads=[("wdb", b)] + [("hid", fc, t) for fc in range(NFC)], writes=[("ps", pb)])
                rk = (rkey, (tok0 + t * 512) // 512)
                P.op(DVE, lambda e, o=o, tok=tok, pb=pb: e.scalar_tensor_tensor(
                    out=resT[:, o, tok], in0=ps[pb][:], scalar=GV[:, o:o + 1], in1=resT[:, o, tok], op0=ALU.mult,
                    op1=ALU.add), reads=[("ps", pb), "GV", rk], writes=[rk])
            if hooks and ("d", o) in hooks:
                hooks[("d", o)]()

    def ws_oproj(nkc):
        return dict(wt=A.alloc("wo", [128, 8, nkc, 128], BF16))

    def out_proj_load(ws, kc0, nkc):
        wt = ws["wt"]
        for o in range(8):
            P.dma(SP, lambda e, o=o: e.dma_start(out=wt[:, o], in_=wout_s[o][:, kc0:kc0 + nkc, :]),
                  reads=[("wout_s", o)], writes=[("wo", o)])

    def out_proj_t(ws, srcT, skey, nkc, t, banks=None, o0=0, o1=8):
        wt = ws["wt"]
        tok = slice(t * 512, (t + 1) * 512)
        for o in range(o0, o1):
            pb = nps() if banks is None else banks[o % len(banks)]

            def mm(e, o=o, pb=pb):
                ins = None
                for kc in range(nkc):
                    ins = e.matmul(ps[pb][:], lhsT=wt[:, o, kc, :], rhs=srcT[:, kc, tok], start=(kc == 0),
                                   stop=(kc == nkc - 1))
                return ins
            P.op(PE, mm, reads=[("wo", o), (skey, t)], writes=[("ps", pb)])
            P.op(DVE, lambda e, o=o, pb=pb: e.scalar_tensor_tensor(
                out=xT[:, o, tok], in0=ps[pb][:], scalar=GV[:, o:o + 1], in1=xT[:, o, tok], op0=ALU.mult,
                op1=ALU.add), reads=[("ps", pb), "GV", ("xT", t)], writes=[("xT", t)])

    def ws_proj(T, seglen):
        w = dict(wt=A.alloc("pw", [128, 8, 128], BF16), dg=A.alloc("pdg", [128, 5, 128], BF16),
                 xpad=A.alloc("xpad", [128, T // seglen, seglen + 4], BF16))
        P.op(DVE, lambda e: e.memset(w["xpad"][:], 0.0), writes=["xpad"])
        return w

    def proj_conv_silu(ws, wtile, cc, T, seglen, dst, dkey, hsrc, hkey):
        wt, dg, xpad = ws["wt"], ws["dg"], ws["xpad"]
        if ws.get("rezero"):
            P.op(DVE, lambda e: e.memset(xpad[:, :, 0:2], 0.0), reads=["xpad"], writes=["xpad"])
        P.dma(SP, lambda e: e.dma_start(out=wt[:], in_=win_s[wtile]), reads=[("win_s", wtile)], writes=["pw"])
        for k in range(5):
            P.op(DVE, lambda e, k=k: e.tensor_scalar(out=dg[:, k, :], in0=ident_b[:], scalar1=convw[:, cc, k:k + 1],
                                                     scalar2=None, op0=ALU.mult),
                 reads=["ident_b", "convw"], writes=[("pdg", k)])
        nb = T // 512
        spb = seglen // 512 if seglen >= 512 else 0
        for t in range(nb):
            pb = nps()

            def mm(e, t=t, pb=pb):
                ins = None
                for kc in range(8):
                    ins = e.matmul(ps[pb][:], lhsT=wt[:, kc, :], rhs=hsrc[:, kc, t * 512:(t + 1) * 512],
                                   start=(kc == 0), stop=(kc == 7))
                return ins
            P.op(PE, mm, reads=["pw", (hkey, t)], writes=[("ps", pb)])
            if seglen >= 512:
                sgi, off = t // spb, (t % spb) * 512
                dv = xpad[:, sgi, 2 + off:2 + off + 512]
                sv = ps[pb][:]
            else:
                ns = 512 // seglen
                dv = xpad[:, t * ns:(t + 1) * ns, 2:2 + seglen]
                sv = ps[pb][:].rearrange("p (s l) -> p s l", s=ns)
            P.op(ACT, lambda e, dv=dv, sv=sv: e.activation(out=dv, in_=sv, func=AF.Copy),
                 reads=[("ps", pb), "xpad"], writes=["xpad"])
        for t in range(nb):
            pb = nps()

            def cv(e, t=t, pb=pb):
                ins = None
                for k in range(5):
                    if seglen >= 512:
                        sgi, off = t // spb, (t % spb) * 512
                        rv = xpad[:, sgi, off + k:off + k + 512]
                    else:
                        ns = 512 // seglen
                        rv = xpad[:, t * ns:(t + 1) * ns, k:k + seglen]
                    ins = e.matmul(ps[pb][:], lhsT=dg[:, k, :], rhs=rv, start=(k == 0), stop=(k == 4))
                return ins
            P.op(PE, cv, reads=["xpad"] + [("pdg", k) for k in range(5)], writes=[("ps", pb)])
            P.op(ACT, lambda e, t=t, pb=pb: e.activation(out=dst[:, t * 512:(t + 1) * 512], in_=ps[pb][:], func=AF.Silu,
                                                         bias=convb[:, cc:cc + 1]),
                 reads=[("ps", pb), "convb"], writes=[(dkey, t)])

    def to_tok(srcT, skey, T, dst, dkey):
        for q in range(T // 512):
            pb = nps()
            pv = ps[pb][:].bitcast(BF16)

            def tr(e, q=q, pv=pv):
                ins = None
                for j in range(4):
                    i = q * 4 + j
                    ins = e.transpose(out=pv[:, j * 128:(j + 1) * 128], in_=srcT[:, i * 128:(i + 1) * 128],
                                      identity=ident_b[:])
                return ins
            P.op(PE, tr, reads=[(skey, q), "ident_b"], writes=[("ps", pb)])
            P.op(DVE, lambda e, q=q, pv=pv: e.tensor_copy(out=dst[:, q * 4:(q + 1) * 4, :],
                                                           in_=pv[:, 0:512].rearrange("p (j f) -> p j f", j=4)),
                 reads=[("ps", pb)], writes=[(dkey, q)])

    def ws_dt_out(T, CL, full):
        w = dict(w_tok=A.alloc("w_tok", [128, T // 128, 64], F32))
        if full:
            w.update(phiHL=A.alloc("phiHL", [128, T], BF16), npsiHL=A.alloc("npsiHL", [128, T], BF16),
                     dbc=A.alloc("dbc", [128, 64, T // CL], F32))
        return w

    def dt_prep(R, hsrc, hkey, T, CL, full, barrier=True):
        m = A.mark()
        dtT = A.alloc("dtT", [128, T], F32)
        G = A.alloc("G", [128, T], F32)
        phi = A.alloc("phi", [128, T], F32)
        tmp = A.alloc("dtmp", [128, T], F32)
        nch = T // CL
        Tt = A.alloc("Tt", [128, nch], F32)
        for t in range(T // 512):
            tb = slice(t * 512, (t + 1) * 512)
            pb = nps()

            def mm(e, tb=tb, pb=pb):
                ins = None
                for kc in range(8):
                    ins = e.matmul(ps[pb][:], lhsT=wdt[:, kc, :], rhs=hsrc[:, kc, tb], start=(kc == 0), stop=(kc == 7))
                return ins
            P.op(PE, mm, reads=["wdt", (hkey, t)], writes=[("ps", pb)])
            P.op(ACT, lambda e, tb=tb, pb=pb: e.activation(out=tmp[:, tb], in_=ps[pb][:], func=AF.Exp,
                                                           bias=cols[:, 0:1]),
                 reads=[("ps", pb), "cols"], writes=["dtmp"])
        P.op(ACT, lambda e: e.activation(out=dtT[:], in_=tmp[:], func=AF.Ln, bias=onec[:, 0:1]), reads=["dtmp", "onec"],
             writes=["dtT"])
        P.op(DVE, lambda e: e.tensor_scalar(out=tmp[:], in0=dtT[:], scalar1=a128[:, 0:1], scalar2=None, op0=ALU.mult),
             reads=["dtT", "a128", "dtmp"], writes=["dtmp"])
        for c in range(nch):
            for j in range(CL // 128):
                sl = slice(c * CL + j * 128, c * CL + (j + 1) * 128)
                init = 0.0 if j == 0 else G[:, c * CL + j * 128 - 1:c * CL + j * 128]
                P.op(DVE, lambda e, sl=sl, init=init: e.tensor_tensor_scan(out=G[:, sl], data0=ones_f[:], data1=tmp[:, sl],
                                                                            initial=init, op0=ALU.mult, op1=ALU.add),
                     reads=["dtmp", "ones_f", "G"], writes=["G"])
        Gv = G[:].rearrange("p (c l) -> p c l", l=CL)
        P.op(DVE, lambda e: e.tensor_copy(out=Tt[:], in_=Gv[:, :, CL - 1]), reads=["G"], writes=["Tt"])
        Tb = Tt[:].unsqueeze(2).to_broadcast([128, nch, CL])
        P.op(DVE, lambda e: e.tensor_tensor(out=phi[:].rearrange("p (c l) -> p c l", l=CL),
                                            in0=tmp[:].rearrange("p (c l) -> p c l", l=CL), in1=Tb, op=ALU.add),
             reads=["dtmp", "Tt"], writes=["phi"])
        P.op(DVE, lambda e: e.tensor_scalar(out=phi[:], in0=phi[:], scalar1=cols[:, 3:4], scalar2=None, op0=ALU.mult),
             reads=["phi", "cols"], writes=["phi"])
        P.op(DVE, lambda e: e.scalar_tensor_tensor(out=phi[:], in0=G[:], scalar=cols[:, 2:3], in1=phi[:], op0=ALU.mult,
                                                   op1=ALU.add), reads=["G", "phi", "cols"], writes=["phi"])
        P.op(DVE, lambda e: e.tensor_tensor(out=G[:].rearrange("p (c l) -> p c l", l=CL), in0=Tb,
                                            in1=phi[:].rearrange("p (c l) -> p c l", l=CL), op=ALU.subtract),
             reads=["Tt", "phi", "G"], writes=["G"])
        P.op(ACT, lambda e: e.activation(out=G[:], in_=G[:], func=AF.Exp), reads=["G"], writes=["G"])
        P.op(DVE, lambda e: e.tensor_tensor(out=G[:], in0=G[:], in1=dtT[:], op=ALU.mult), reads=["G", "dtT"],
             writes=["G"])
        w_tok = R["w_tok"]
        ntile = T // 128
        for q0 in range(0, ntile, 8):
            nq = min(8, ntile - q0)
            pb = nps()

            def tr(e, q0=q0, nq=nq, pb=pb):
                ins = None
                for j in range(nq):
                    i = q0 + j
                    ins = e.transpose(out=ps[pb][:, j * 64:(j + 1) * 64], in_=G[0:64, i * 128:(i + 1) * 128],
                                      identity=ident_f[0:64, 0:64])
                return ins
            P.op(PE, tr, reads=["G", "ident_f"], writes=[("ps", pb)])
            P.op(DVE, lambda e, q0=q0, nq=nq, pb=pb: e.tensor_copy(
                out=w_tok[:, q0:q0 + nq, :], in_=ps[pb][:, 0:nq * 64].rearrange("p (j f) -> p j f", j=nq)),
                reads=[("ps", pb)], writes=["w_tok"])
        if full:
            phiHL, npsiHL, dbc = R["phiHL"], R["npsiHL"], R["dbc"]
            hi = A.alloc("hi_all", [128, T], BF16)
            P.op(ACT, lambda e: e.activation(out=hi[:], in_=phi[:], func=AF.Copy), reads=["phi"], writes=["hi_all"])
            P.op(DVE, lambda e: e.tensor_copy(out=phiHL[0:64, :], in_=hi[0:64, :]), reads=["hi_all"], writes=["phiHL"])
            P.op(DVE, lambda e: e.tensor_tensor(out=phiHL[64:128, :], in0=phi[64:128, :], in1=hi[64:128, :],
                                                op=ALU.subtract), reads=["hi_all", "phi"], writes=["phiHL"])
            P.op(ACT, lambda e: e.activation(out=tmp[:], in_=dtT[:], func=AF.Ln), reads=["dtT", "dtmp"], writes=["dtmp"])
            P.op(DVE, lambda e: e.tensor_tensor(out=tmp[:], in0=tmp[:], in1=phi[:], op=ALU.subtract),
                 reads=["dtmp", "phi"], writes=["dtmp"])
            P.op(ACT, lambda e: e.activation(out=hi[:], in_=tmp[:], func=AF.Copy), reads=["dtmp", "phiHL"],
                 writes=["hi_all"])
            P.op(DVE, lambda e: e.tensor_copy(out=npsiHL[0:64, :], in_=hi[0:64, :]), reads=["hi_all"], writes=["npsiHL"])
            P.op(DVE, lambda e: e.tensor_tensor(out=npsiHL[64:128, :], in0=tmp[64:128, :], in1=hi[64:128, :],
                                                op=ALU.subtract), reads=["hi_all", "dtmp"], writes=["npsiHL"])
            eT = A.alloc("eT", [128, nch], F32)
            rd = A.alloc("rd", [64, 64, nch], F32)
            P.op(ACT, lambda e: e.activation(out=eT[:], in_=Tt[:], func=AF.Exp), reads=["Tt"], writes=["eT"])
            P.op(DVE, lambda e: e.tensor_tensor(out=rd[:], in0=eT[0:64, :].unsqueeze(1).to_broadcast([64, 64, nch]),
                                                in1=ident_f[0:64, 0:64].unsqueeze(2).to_broadcast([64, 64, nch]),
                                                op=ALU.mult), reads=["eT", "ident_f"], writes=["rd"])
            rdf = rd[:].rearrange("p r c -> p (r c)")
            dbf = dbc[:].rearrange("p r c -> p (r c)")
            tot = 64 * nch
            for h0 in range(0, tot, 512):
                pb = nps()
                P.op(PE, lambda e, h0=h0, pb=pb: e.matmul(ps[pb][:], lhsT=ones_f[0:64, :], rhs=rdf[:, h0:h0 + 512],
                                                          start=True, stop=True),
                     reads=["rd", "ones_f"], writes=[("ps", pb)])
                P.op(DVE, lambda e, h0=h0, pb=pb: e.tensor_copy(out=dbf[:, h0:h0 + 512], in_=ps[pb][:]),
                     reads=[("ps", pb)], writes=["dbc"])
        if barrier:
            A.release(m, fence)
        else:
            A.cur = m

    def ctx_phase():
        T = nseq * CTXL
        m = A.mark()
        wl = ws_load()
        wn = ws_norm()
        wf = ws_ffn(T)
        load_tokens(wl, ctx_d, T, xT, "xT")
        set_AS(V_N1, 0, 1, 4)
        norm_mod(wn, xT, "xT", T, hT, "hT")
        set_gate(2, 4, 0.5)
        ffn_block(wf, 0, 0, T, xT, "xT")
        set_AS(V_NM, 3, 4, 4)
        norm_mod(wn, xT, "xT", T, hT, "hT")
        dump("ctx_h", hT[:, 0, 0:T], ("hT", T // 512 - 1))
        A.release(m, fence)
        m = A.mark()
        R = ws_dt_out(T, CTXL, False)
        dt_prep(R, hT, "hT", T, CTXL, False)
        w_tok = R["w_tok"]
        wp = ws_proj(T, CTXL)
        BT = A.alloc("cBT", [128, 2, T], BF16)
        B_tok = A.alloc("cB_tok", [128, 2, T // 128, 128], BF16)
        for g in range(2):
            proj_conv_silu(wp, 16 + g, 8 + g, T, CTXL, BT[:, g, :], ("cBT", g), hT, "hT")
            to_tok(BT[:, g, :], ("cBT", g), T, B_tok[:, g], ("cB_tok", g))
        xcb = [A.alloc(f"cxh{i}", [128, T], BF16) for i in range(2)]
        xctb = [A.alloc(f"cxh_tok{i}", [128, T // 128, 128], BF16) for i in range(2)]
        xw = [A.alloc(f"cxw{i}", [128, 128], BF16) for i in range(2)]
        st = [A.alloc(f"cst{i}", [128, 512], F32) for i in range(2)]
        kk = 0

        def cprologue(hp_):
            bi = hp_ % 2
            proj_conv_silu(wp, 8 + hp_, hp_, T, CTXL, xcb[bi][:], ("cxh", bi), hT, "hT")
            to_tok(xcb[bi][:], ("cxh", bi), T, xctb[bi], ("cxh_tok", bi))
        cprologue(0)
        wl0 = ws_load()
        wn0 = ws_norm()
        set_AS(V_N1, 0, 1, 0)
        for hp in range(8):
            g = hp // 4
            xbi = hp % 2
            xc_tok = xctb[xbi]
            if hp + 1 < 8:
                cprologue(hp + 1)
            load_tokens(wl0, x_d[0], L, xT, "xT", 2 * hp, 2 * hp + 2)
            if hp in (6, 7):
                norm_mod(wn0, xT, "xT", L, hT, "hT", t0=hp - 4, t1=hp - 3)
            pb = None
            for s in range(nseq):
                if s % 2 == 0:
                    pb = nps()
                for dr in range(2):
                    col = (s % 2) * 256 + dr * 128
                    r0 = dr * 32 + 2 * hp
                    for j in range(2):
                        i = s * 2 + j
                        b = kk % 2
                        kk += 1
                        P.op(DVE, lambda e, i=i, r0=r0, b=b, xt_=xc_tok: e.tensor_tensor(
                            out=xw[b][:].rearrange("p (h q) -> p h q", h=2),
                            in0=xt_[:, i, :].rearrange("p (h q) -> p h q", h=2),
                            in1=w_tok[:, i, r0:r0 + 2].unsqueeze(2).to_broadcast([128, 2, 64]), op=ALU.mult),
                            reads=[(("cxh_tok", xbi), i // 4), "w_tok"], writes=[("cxw", b)])
                        P.op(PE, lambda e, i=i, g=g, b=b, j=j, pb=pb, col=col: e.matmul(
                            ps[pb][:, col:col + 128], lhsT=B_tok[:, g, i, :], rhs=xw[b][:], start=(j == 0), stop=(j == 1)),
                            reads=[("cB_tok", g), ("cxw", b)], writes=[("ps", pb)])
                if s % 2 == 1:
                    sb = st[(s // 2) % 2]
                    skey = ("cst", (s // 2) % 2)
                    P.op(DVE, lambda e, sb=sb, pb=pb: e.tensor_copy(out=sb[:], in_=ps[pb][:]), reads=[("ps", pb)],
                         writes=[skey])
                    for s2 in (s - 1, s):
                        for dr in range(2):
                            col = (s2 % 2) * 256 + dr * 128
                            P.dma(POOL, lambda e, s2=s2, dr=dr, col=col, sb=sb, hp=hp: e.dma_start(
                                out=h0_s[s2, dr, :, hp * 128:(hp + 1) * 128], in_=sb[:, col:col + 128]),
                                reads=[skey], writes=[("h0_s", s2)])
        A.release(m, fence)

    def conformer():
        m = A.mark()
        upad = [A.alloc(f"upad{i}", [128, 62 * 64], BF16) for i in range(2)]
        uconv = A.alloc("uconv", [128, 8, L], BF16)
        dg = [A.alloc(f"cdg{i}", [128, 31, 128], BF16) for i in range(2)]
        wv = [A.alloc(f"cwv{i}", [128, 8, 128], BF16) for i in range(2)]
        wg = [A.alloc(f"cwg{i}", [128, 8, 128], BF16) for i in range(2)]
        sgm = [A.alloc(f"csg{i}", [128, 512], F32) for i in range(2)]
        sq = [A.alloc(f"lsq{i}", [128, 512], BF16) for i in range(2)]
        mean1 = A.alloc("lmean", [128, 512], F32)
        rstd1 = A.alloc("lrstd", [128, 512], F32)
        mean, rstd = [mean1, mean1], [rstd1, rstd1]
        t1 = [A.alloc(f"lt{i}", [128, 512], F32) for i in range(3)]
        wo = ws_oproj(8)
        out_proj_load(wo, 8, 8)
        kk = 0
        for j in range(8):
            b = j % 2
            horiz = j < 4
            if j in (0, 1, 4, 5):
                P.op(DVE, lambda e, b=b: e.memset(upad[b][:], 0.0), reads=[("upad", b)], writes=[("upad", b)])
            P.dma(SP, lambda e, j=j, b=b: e.dma_start(out=wv[b][:], in_=win_s[20 + j]), reads=[("win_s", 20 + j)],
                  writes=[("cwv", b)])
            P.dma(SP, lambda e, j=j, b=b: e.dma_start(out=wg[b][:], in_=win_s[28 + j]), reads=[("win_s", 28 + j)],
                  writes=[("cwg", b)])
            for k in range(31):
                P.op(DVE, lambda e, k=k, j=j, b=b: e.tensor_scalar(out=dg[b][:, k, :], in0=ident_b[:],
                                                                  scalar1=cconvw[:, j, k:k + 1], scalar2=None,
                                                                  op0=ALU.mult),
                     reads=["ident_b", "cconvw"], writes=[("cdg", b)])
            if horiz:
                up3 = upad[b][:, 0:32 * 94].rearrange("p (r w) -> p r w", w=94)
            else:
                up3 = upad[b][:].rearrange("p (r w) -> p r w", w=64)
            for t in range(4):
                tb = slice(t * 512, (t + 1) * 512)
                pv, pg = nps(), nps()

                def mm(e, b=b, tb=tb, pv=pv, pg=pg):
                    ins = None
                    for kc in range(8):
                        ins = e.matmul(ps[pv][:], lhsT=wv[b][:, kc, :], rhs=hT[:, kc, tb], start=(kc == 0), stop=(kc == 7))
                    for kc in range(8):
                        ins = e.matmul(ps[pg][:], lhsT=wg[b][:, kc, :], rhs=hT[:, kc, tb], start=(kc == 0), stop=(kc == 7))
                    return ins
                P.op(PE, mm, reads=[("cwv", b), ("cwg", b), ("hT", t)], writes=[("ps", pv), ("ps", pg)])
                s = sgm[kk % 2]
                skey = ("csg", kk % 2)
                kk += 1
                P.op(ACT, lambda e, s=s, pg=pg: e.activation(out=s[:], in_=ps[pg][:], func=AF.Sigmoid),
                     reads=[("ps", pg)], writes=[skey])
                if horiz:
                    dv = up3[:, 8 * t:8 * t + 8, 15:79]
                else:
                    dv = up3[:, 15 + 8 * t:15 + 8 * t + 8, :]
                P.op(DVE, lambda e, s=s, pv=pv, dv=dv: e.tensor_tensor(
                    out=dv, in0=ps[pv][:].rearrange("p (r w) -> p r w", w=64),
                    in1=s[:].rearrange("p (r w) -> p r w", w=64), op=ALU.mult),
                    reads=[("ps", pv), skey, ("upad", b)], writes=[("upad", b)])
            for t in range(4):
                pb = nps()

                def cv(e, b=b, t=t, pb=pb, horiz=horiz, up3=up3):
                    ins = None
                    if horiz:
                        taps = list(range(31))
                    else:
                        taps = [k for k in range(31) if 8 * t + k + 7 >= 15 and 8 * t + k <= 46]
                    for n_, k in enumerate(taps):
                        if horiz:
                            rv = up3[:, 8 * t:8 * t + 8, k:k + 64]
                        else:
                            rv = up3[:, 8 * t + k:8 * t + k + 8, :]
                        ins = e.matmul(ps[pb][:], lhsT=dg[b][:, k, :], rhs=rv, start=(n_ == 0), stop=(n_ == len(taps) - 1))
                    return ins
                P.op(PE, cv, reads=[("upad", b), ("cdg", b)], writes=[("ps", pb)])
                P.op(ACT, lambda e, j=j, t=t, pb=pb: e.activation(out=uconv[:, j, t * 512:(t + 1) * 512], in_=ps[pb][:],
                                                                  func=AF.Identity, bias=vecs[:, V_CB, j:j + 1]),
                     reads=[("ps", pb), "vecs"], writes=[("uconv", t)])
        dump("uconv_pre", uconv[:, 0, :], ("uconv", 3))
        lnps = {}

        def ln_stats(t):
            tb = slice(t * 512, (t + 1) * 512)
            p_s, p_q = (4, 5) if t % 2 == 0 else (6, 7)
            lnps[t] = (p_s, p_q)
            for j in range(8):
                P.op(PE, lambda e, j=j, tb=tb, p_s=p_s: e.matmul(ps[p_s][:], lhsT=ones_b[:], rhs=uconv[:, j, tb],
                                                                 start=(j == 0), stop=(j == 7)),
                     reads=[("uconv", t), "ones_b"], writes=[("ps", p_s)])
                P.op(ACT, lambda e, j=j, tb=tb: e.activation(out=sq[j % 2][:], in_=uconv[:, j, tb], func=AF.Square),
                     reads=[("uconv", t)], writes=[("lsq", j % 2)])
                P.op(PE, lambda e, j=j, p_q=p_q: e.matmul(ps[p_q][:], lhsT=ones_b[:], rhs=sq[j % 2][:], start=(j == 0),
                                                          stop=(j == 7)),
                     reads=[("lsq", j % 2), "ones_b"], writes=[("ps", p_q)])

        def ln_norm(t):
            tb = slice(t * 512, (t + 1) * 512)
            p_s, p_q = lnps[t]
            mean_, rstd_ = mean[t % 2], rstd[t % 2]
            mk, rk = "lmean", "lrstd"
            P.op(DVE, lambda e: e.tensor_scalar(out=mean_[:], in0=ps[p_s][:], scalar1=1.0 / 1024, scalar2=None,
                                                op0=ALU.mult), reads=[("ps", p_s)], writes=[mk])
            P.op(DVE, lambda e: e.tensor_tensor(out=rstd_[:], in0=mean_[:], in1=mean_[:], op=ALU.mult), reads=[mk],
                 writes=[rk])
            P.op(DVE, lambda e: e.scalar_tensor_tensor(out=rstd_[:], in0=ps[p_q][:], scalar=1.0 / 1024,
                                                       in1=rstd_[:], op0=ALU.mult, op1=ALU.subtract),
                 reads=[("ps", p_q), rk], writes=[rk])
            P.op(ACT, lambda e: e.activation(out=rstd_[:], in_=rstd_[:], func=AF.Sqrt, bias=epsc[:, 0:1]),
                 reads=[rk, "epsc"], writes=[rk])
            P.op(DVE, lambda e: e.reciprocal(out=rstd_[:], in_=rstd_[:]), reads=[rk], writes=[rk])
            for j in range(8):
                tt = t1[j % 3]
                tk = ("lt", j % 3)
                P.op(DVE, lambda e, j=j, tt=tt: e.tensor_tensor(out=tt[:], in0=uconv[:, j, tb], in1=mean_[:],
                                                                op=ALU.subtract),
                     reads=[("uconv", t), mk], writes=[tk])
                P.op(DVE, lambda e, tt=tt: e.tensor_tensor(out=tt[:], in0=tt[:], in1=rstd_[:], op=ALU.mult),
                     reads=[tk, rk], writes=[tk])
                P.op(ACT, lambda e, j=j, tt=tt: e.activation(out=uconv[:, j, tb], in_=tt[:], func=AF.Silu,
                                                             scale=vecs[:, V_LW, j:j + 1],
                                                             bias=vecs[:, V_LB, j:j + 1]),
                     reads=[tk, "vecs"], writes=[("uconv", t)])
                if t >= 1:
                    out_proj_t(wo, uconv, "uconv", 8, t - 1, banks=[0, 1, 2, 3], o0=j, o1=j + 1)

        ln_stats(0)
        for t in range(4):
            if t + 1 < 4:
                ln_stats(t + 1)
            ln_norm(t)
        out_proj_t(wo, uconv, "uconv", 8, 3, banks=[0, 1, 2, 3])
        dump("uconv", uconv[:, 0, :], ("uconv", 3))
        A.release(m, fence)

    def ssd(s, after_xt=None):
        m = A.mark()
        OH = A.alloc("OH", [128, 32, 128], BF16)
        NEG = A.alloc("NEG", [128, 2, 256], BF16)
        P.dma(POOL, lambda e: e.dma_start(out=OH[:], in_=oh_d), writes=["OH"])
        P.dma(POOL, lambda e: e.dma_start(out=NEG[:], in_=neg_d), writes=["NEG"])
        R = ws_dt_out(L, 128, True)
        w_tok2, phiHL2, npsiHL2, dbc2 = R["w_tok"], R["phiHL"], R["npsiHL"], R["dbc"]
        m_ov = A.mark()
        yg = A.alloc("yg", [128, 4, L], BF16)
        hprev = A.alloc("hprev", [128, 2, 16, 128], BF16)
        S2 = [A.alloc(f"S{i}", [128, 2, 128], F32) for i in range(2)]
        xwb = [A.alloc(f"xwb{i}", [128, 8, 128], BF16)[:] for i in range(2)]
        eb = [A.alloc(f"eb{i}", [128, 512], BF16) for i in range(2)]
        GE = [A.alloc(f"GE{i}", [128, 512], BF16) for i in range(4)]
        szb = A.alloc("szb", [128, L], BF16)
        ytmp = A.alloc("ytmp", [128, 512], F32)
        m_ov_end = A.mark()
        wz = A.alloc("wz", [128, 8, 128], BF16)
        MB = A.alloc("MB", [128, 8, 512], BF16)
        B_tok = A.alloc("B_tok", [128, 16, 128], BF16)
        sv_cur, sv_peak = A.cur, A.peak
        A.cur = m_ov
        ops_dt = P.capture(lambda: dt_prep(R, hT, "hT", L, 128, True, barrier=False), [0, 1, 2, 3])
        assert A.peak <= max(sv_peak, m_ov_end), "dt temporaries exceed overlay region"
        A.cur = sv_cur
        wp = ws_proj(L, L)
        wp["rezero"] = True
        xpv = wp["xpad"]
        xwb = xwb + [xpv[:, 0, 0:1024].rearrange("p (c f) -> p c f", c=8),
                     xpv[:, 0, 1024:2048].rearrange("p (c f) -> p c f", c=8)]
        YB = [4, 5, 6, 7]
        EBANKS = [0, 1, 2, 3]
        ecnt = [0]
        for g in range(2):
            m2 = A.mark()
            BT = A.alloc("BT", [128, L], BF16)
            CT = A.alloc("CT", [128, L], BF16)

            def bc_stage(g=g, BT=BT, CT=CT):
                proj_conv_silu(wp, 16 + g, 8 + g, L, L, BT[:], "BT", hT, "hT")
                proj_conv_silu(wp, 18 + g, 10 + g, L, L, CT[:], "CT", hT, "hT")
                to_tok(BT[:], "BT", L, B_tok, "B_tok")
                for q in range(4):
                    pb = nps()

                    def bc(e, q=q, pb=pb):
                        ins = None
                        for j in range(4):
                            c = q * 4 + j
                            ins = e.matmul(ps[pb][:, j * 128:(j + 1) * 128], lhsT=BT[:, c * 128:(c + 1) * 128],
                                           rhs=CT[:, c * 128:(c + 1) * 128], start=True, stop=True)
                        return ins
                    P.op(PE, bc, reads=[("BT", q), ("CT", q)], writes=[("ps", pb)])
                    P.op(ACT, lambda e, q=q, pb=pb: e.activation(out=MB[:, 2 * q:2 * q + 2, 0:256],
                                                                 in_=ps[pb][:].rearrange("p (a x) -> p a x", a=2),
                                                                 func=AF.Copy),
                         reads=[("ps", pb)], writes=[("MB", q)])
                    P.op(DVE, lambda e, q=q: e.tensor_copy(out=MB[:, 2 * q:2 * q + 2, 256:512],
                                                           in_=CT[:, q * 512:(q + 1) * 512].rearrange("p (a x) -> p a x", a=2)),
                         reads=[("CT", q)], writes=[("MB", q)])
            if g == 0:
                ops_bc = P.capture(bc_stage, [4, 5, 6, 7])
                P.merge(ops_dt, ops_bc)
            else:
                bc_stage()
            if g == 0:
                dump("MB", MB[:].rearrange("p a b -> p (a b)"), ("MB", 3))
            A.release(m2, fence)
            m2 = A.mark()
            xhb = [A.alloc(f"xh{i}", [128, L], BF16) for i in range(2)]
            xtb = [A.alloc(f"xh_tok{i}", [128, 16, 128], BF16) for i in range(2)]

            def prologue(hq_):
                hp_ = 4 * g + hq_
                bi = hq_ % 2
                proj_conv_silu(wp, 8 + hp_, hp_, L, L, xhb[bi][:], ("xh", bi), hT, "hT")
                to_tok(xhb[bi][:], ("xh", bi), L, xtb[bi], ("xh_tok", bi))
            def rec_phase(hq, mid=None):
                hp = 4 * g + hq
                xbi = hq % 2
                xh, xh_tok = xhb[xbi], xtb[xbi]
                P.dma(SP, lambda e, hp=hp: e.dma_start(out=wz[:], in_=win_s[hp]), reads=[("win_s", hp)], writes=["wz"])
                for dr in range(2):
                    P.dma(SP, lambda e, dr=dr, hp=hp: e.dma_start(out=S2[0][:, dr, :], in_=h0_s[s, dr, :, hp * 128:(hp + 1) * 128]),
                          reads=[("h0_s", s)], writes=[("S", 0, dr, 0), ("S", 0, dr, 1)])
                orders = [list(range(16)), list(range(15, -1, -1))]

                def emit_cs(k):
                    for dr in range(2):
                        c = orders[dr][k]
                        i_ = 2 * k + dr
                        pb = i_ // 4
                        j = i_ % 4
                        buf = (0 if c < 8 else 2) if dr == 0 else (1 if c >= 8 else 3)
                        xv = xwb[buf]
                        rds = [("B_tok", c // 4), ("xwb", buf)] + (["xpad"] if buf >= 2 else [])
                        P.op(PE, lambda e, c=c, xv=xv, j=j, pb=pb: e.matmul(ps[pb][:, j * 128:(j + 1) * 128],
                                                                             lhsT=B_tok[:, c, :], rhs=xv[:, c % 8, :],
                                                                             start=True, stop=True),
                             reads=rds, writes=[("ps", pb)])

                for k in range(4):
                    emit_cs(k)
                if mid is not None:
                    mid()
                for k in range(4, 16):
                    emit_cs(k)
                for k in range(16):
                    Sc, Sn = S2[k % 2], S2[(k + 1) % 2]
                    P.op(ACT, lambda e, k=k, Sc=Sc: e.activation(out=hprev[:, :, k, :], in_=Sc[:, :, :], func=AF.Copy),
                         reads=[("S", k % 2, d_, h_) for d_ in range(2) for h_ in range(2)],
                         writes=[("hprev", 0, k), ("hprev", 1, 15 - k)])
                    for dr in range(2):
                        c = orders[dr][k]
                        i_ = 2 * k + dr
                        pb = i_ // 4
                        j = i_ % 4
                        r0 = dr * 32 + 2 * hp
                        if k < 15:
                            for hh in range(2):
                                hs = slice(hh * 64, (hh + 1) * 64)
                                P.op(DVE, lambda e, dr=dr, c=c, r0=r0, hh=hh, hs=hs, Sc=Sc, Sn=Sn, pb=pb, j=j:
                                     e.scalar_tensor_tensor(out=Sn[:, dr, hs], in0=Sc[:, dr, hs],
                                                            scalar=dbc2[:, r0 + hh, c:c + 1],
                                                            in1=ps[pb][:, j * 128 + hh * 64:j * 128 + (hh + 1) * 64],
                                                            op0=ALU.mult, op1=ALU.add),
                                     reads=[("S", k % 2, dr, hh), ("ps", pb), "dbc"], writes=[("S", (k + 1) % 2, dr, hh)])

            def main_loop(hq):
                hp = 4 * g + hq
                xbi = hq % 2
                xh_tok = xtb[xbi]
                def emit_ex_pair(b8, hh):
                    h = 2 * hp + hh
                    gs = []
                    for dr in range(2):
                        ridx = dr * 16 + h
                        pb = EBANKS[ecnt[0] % 4]
                        ebuf = ecnt[0] % 2
                        gbuf = ecnt[0] % 4
                        ecnt[0] += 1
                        cs = slice(b8 * 256, (b8 + 1) * 256)

                        def ex(e, ridx=ridx, dr=dr, pb=pb, cs=cs, b8=b8):
                            e.matmul(ps[pb][:, 0:256], lhsT=OH[:, ridx, :], rhs=phiHL2[:, cs], start=True, stop=False)
                            for j in range(2):
                                c = 2 * b8 + j
                                e.matmul(ps[pb][:, j * 128:(j + 1) * 128], lhsT=npsiHL2[:, c * 128:(c + 1) * 128],
                                         rhs=OH[:, ridx, :], start=False, stop=False)
                            e.matmul(ps[pb][:, 0:256], lhsT=ident_b[:], rhs=NEG[:, dr, :], start=False, stop=True)
                            return e.matmul(ps[pb][:, 256:512], lhsT=OH[:, ridx, :], rhs=phiHL2[:, cs], start=True,
                                            stop=True)
                        P.op(PE, ex, reads=["OH", "phiHL", "npsiHL", "NEG", "ident_b"], writes=[("ps", pb)])
                        P.op(ACT, lambda e, pb=pb, ebuf=ebuf: e.activation(out=eb[ebuf][:], in_=ps[pb][:], func=AF.Exp),
                             reads=[("ps", pb)], writes=[("eb", ebuf)])
                        P.op(DVE, lambda e, ebuf=ebuf, gbuf=gbuf, b8=b8: e.tensor_tensor(out=GE[gbuf][:], in0=eb[ebuf][:],
                                                                                          in1=MB[:, b8, :], op=ALU.mult),
                             reads=[("eb", ebuf), ("MB", b8 // 2)], writes=[("GE", gbuf)])
                        gs.append(gbuf)
                    P.op(DVE, lambda e, gf=gs[0], gb=gs[1]: e.tensor_tensor(out=GE[gf][:, 0:256], in0=GE[gf][:, 0:256],
                                                                            in1=GE[gb][:, 0:256], op=ALU.add),
                         reads=[("GE", gs[0]), ("GE", gs[1])], writes=[("GE", gs[0])])
                    return gs

                def emit_ym(b8, hh, gs):
                    ybank = YB[b8 // 2]
                    gf, gb = gs

                    def ym(e, hh=hh, b8=b8, ybank=ybank, gf=gf, gb=gb, xt_=xh_tok):
                        ins = None
                        for j in range(2):
                            c = 2 * b8 + j
                            col = ((b8 % 2) * 2 + j) * 128
                            o = ps[ybank][hh * 64:(hh + 1) * 64, col:col + 128]
                            xl = xt_[:, c, hh * 64:(hh + 1) * 64]
                            e.matmul(o, lhsT=xl, rhs=GE[gf][:, j * 128:(j + 1) * 128], start=True, stop=False)
                            e.matmul(o, lhsT=hprev[:, 0, c, hh * 64:(hh + 1) * 64],
                                     rhs=GE[gf][:, 256 + j * 128:256 + (j + 1) * 128], start=False, stop=False)
                            ins = e.matmul(o, lhsT=hprev[:, 1, 15 - c, hh * 64:(hh + 1) * 64],
                                           rhs=GE[gb][:, 256 + j * 128:256 + (j + 1) * 128], start=False, stop=True)
                        return ins
                    P.op(PE, ym, reads=[("GE", gf), ("GE", gb), (("xh_tok", xbi), b8 // 2)] +
                         [("hprev", d_, 2 * b8 + j_) for d_ in range(2) for j_ in range(2)],
                         writes=[("ps", ybank)])

                units = [(b8, hh) for b8 in range(8) for hh in range(2)]
                cur = emit_ex_pair(*units[0])
                for ui, (b8, hh) in enumerate(units):
                    nxt_ = emit_ex_pair(*units[ui + 1]) if ui + 1 < len(units) else None
                    emit_ym(b8, hh, cur)
                    cur = nxt_

            def zproj(hq):
                for t in range(4):
                    tb = slice(t * 512, (t + 1) * 512)
                    pb = EBANKS[t % 4]

                    def mm(e, tb=tb, pb=pb):
                        ins = None
                        for kc in range(8):
                            ins = e.matmul(ps[pb][:], lhsT=wz[:, kc, :], rhs=hT[:, kc, tb], start=(kc == 0), stop=(kc == 7))
                        return ins
                    P.op(PE, mm, reads=["wz", ("hT", t)], writes=[("ps", pb)])
                    P.op(ACT, lambda e, tb=tb, pb=pb: e.activation(out=szb[:, tb], in_=ps[pb][:], func=AF.Silu),
                         reads=[("ps", pb)], writes=[("szb", t)])

            def gating(hq):
                hp = 4 * g + hq
                xbi = hq % 2
                xh = xhb[xbi]
                for t in range(4):
                    tb = slice(t * 512, (t + 1) * 512)
                    P.op(DVE, lambda e, t=t, tb=tb, hp=hp, xh_=xh: e.scalar_tensor_tensor(
                        out=ytmp[:], in0=xh_[:, tb], scalar=vecs[:, V_D, hp:hp + 1], in1=ps[YB[t]][:], op0=ALU.mult,
                        op1=ALU.add), reads=[("ps", YB[t]), (("xh", xbi), t), "vecs"], writes=["ytmp"])
                    P.op(DVE, lambda e, t=t, tb=tb, hq=hq: e.tensor_tensor(out=yg[:, hq, tb], in0=ytmp[:],
                                                                           in1=szb[:, tb], op=ALU.mult),
                         reads=["ytmp", ("szb", t)], writes=[("yg", t)])

            def xw_all(hq):
                hp = 4 * g + hq
                xbi = hq % 2
                xt_ = xtb[xbi]
                for buf, (dr, half) in enumerate([(0, 0), (1, 1), (0, 1), (1, 0)]):
                    c0 = half * 8
                    r0 = dr * 32 + 2 * hp
                    xv = xwb[buf]
                    P.op(DVE, lambda e, c0=c0, r0=r0, xv=xv, xt_=xt_: e.tensor_tensor(
                        out=xv.rearrange("p c (h q) -> p c h q", h=2),
                        in0=xt_[:, c0:c0 + 8, :].rearrange("p c (h q) -> p c h q", h=2),
                        in1=w_tok2[:, c0:c0 + 8, r0:r0 + 2].unsqueeze(3).to_broadcast([128, 8, 2, 64]), op=ALU.mult),
                        reads=[(("xh_tok", xbi), c0 // 4), (("xh_tok", xbi), c0 // 4 + 1), "w_tok"] +
                        (["xpad"] if buf >= 2 else []),
                        writes=[("xwb", buf)] + (["xpad"] if buf >= 2 else []))

            prologue(0)
            xw_all(0)
            rec_phase(0)
            prologue(1)
            zproj(0)
            for hq in range(4):
                if hq + 1 < 4:
                    xw_all(hq + 1)
                main_loop(hq)
                if hq + 1 < 4:
                    rec_phase(hq + 1, mid=lambda hq=hq: gating(hq))
                    if hq + 2 < 4:
                        prologue(hq + 2)
                    zproj(hq + 1)
                else:
                    gating(hq)
            if g == 0:
                dump("yg", yg[:, 0, :], ("yg", 3))
            A.release(m2, fence)
            m3 = A.mark()
            wn = ws_norm(2)
            wo = ws_oproj(4)
            out_proj_load(wo, 4 * g, 4)
            for t in range(4):
                def ccb(t_, c_, wo=wo):
                    if t_ >= 1:
                        out_proj_t(wo, yg, "yg", 4, t_ - 1, o0=2 * c_, o1=2 * c_ + 2)
                norm_mod(wn, yg, "yg", L, yg, "yg", nfc=4, denom=512.0, affine=False,
                         vec_scale=vecs[:, V_SN, 4 * g:4 * g + 4], t0=t, t1=t + 1, chunk_cb=ccb)
                if g == 1 and after_xt is not None and t >= 2:
                    after_xt(t - 2, wn)
            out_proj_t(wo, yg, "yg", 4, 3)
            if g == 1 and after_xt is not None:
                after_xt(2, wn)
                after_xt(3, wn)
            A.release(m3, fence)
        A.release(m, fence)

    def ws_final(top=False):
        al = A.alloc_top if top else A.alloc
        w = dict(fin=al("fin", [128, D], F32), ot=[al(f"ot{i}", [128, D], F32) for i in range(2)],
                 junk=al("junk", [128, 512], BF16), ss=[al(f"fss{i}", [128, 2], F32) for i in range(2)])
        P.dma(SP, lambda e: e.dma_start(out=w["fin"][:], in_=fin_d), writes=["fin"])
        return w

    def final_store(ws, s, i0=0, i1=L // 128):
        fin, ot, junk, ss = ws["fin"], ws["ot"], ws["junk"], ws["ss"]
        for i in range(i0, i1):
            o_ = ot[i % 2]
            okey = ("ot", i % 2)
            sk = ("fss", i % 2)
            s_ = ss[i % 2]
            for half in range(2):
                pb = nps()

                def tr(e, i=i, half=half, pb=pb):
                    ins = None
                    for c in range(4):
                        cc = half * 4 + c
                        ins = e.transpose(out=ps[pb][:, c * 128:(c + 1) * 128], in_=xT[:, cc, i * 128:(i + 1) * 128],
                                          identity=ident_f[:])
                    return ins
                P.op(PE, tr, reads=[("xT", i // 4), "ident_f"], writes=[("ps", pb)])
                if half == 0:
                    P.op(ACT, lambda e, o_=o_, pb=pb: e.activation(out=o_[:, 0:512], in_=ps[pb][:], func=AF.Copy),
                         reads=[("ps", pb)], writes=[okey])
                else:
                    P.op(DVE, lambda e, o_=o_, pb=pb: e.tensor_copy(out=o_[:, 512:1024], in_=ps[pb][:]),
                         reads=[("ps", pb)], writes=[okey])
            for hf in range(2):
                P.op(ACT, lambda e, o_=o_, s_=s_, hf=hf: e.activation(out=junk[:], in_=o_[:, hf * 512:(hf + 1) * 512],
                                                                      func=AF.Square, accum_out=s_[:, hf:hf + 1]),
                     reads=[okey], writes=["junk", sk])
            P.op(DVE, lambda e, s_=s_: e.tensor_tensor(out=s_[:, 0:1], in0=s_[:, 0:1], in1=s_[:, 1:2], op=ALU.add),
                 reads=[sk], writes=[sk])
            P.op(ACT, lambda e, s_=s_: e.activation(out=s_[:, 0:1], in_=s_[:, 0:1], func=AF.Sqrt, bias=epsc[:, 0:1],
                                                    scale=1.0 / 1024), reads=[sk, "epsc"], writes=[sk])
            P.op(DVE, lambda e, s_=s_: e.reciprocal(out=s_[:, 0:1], in_=s_[:, 0:1]), reads=[sk], writes=[sk])
            P.op(DVE, lambda e, o_=o_, s_=s_: e.scalar_tensor_tensor(out=o_[:], in0=o_[:], scalar=s_[:, 0:1], in1=fin[:],
                                                                     op0=ALU.mult, op1=ALU.mult),
                 reads=[okey, sk, "fin"], writes=[okey])
            P.dma(POOL, lambda e, i=i, o_=o_: e.dma_start(out=out_d[s, i * 128:(i + 1) * 128, :], in_=o_[:]),
                  reads=[okey], writes=[("out", s, i)])

    stages = build.stages
    if "ctx" in stages:
        ctx_phase()

    def alloc_ffn_phase():
        A.reset_top()
        return dict(wl=ws_load(top=True), wfin=ws_final(top=True), wn=ws_norm(), wf=ws_ffn(1024))

    m = A.mark()
    W = alloc_ffn_phase()
    if "ctx" not in stages:
        load_tokens(W["wl"], x_d[0], L, xT, "xT")
        set_AS(V_N1, 0, 1, 0)
        norm_mod(W["wn"], xT, "xT", L, hT, "hT")
    else:
        norm_mod(W["wn"], xT, "xT", L, hT, "hT", t0=0, t1=2)
    for s in range(nseq):
        wn, wf, wl, wfin = W["wn"], W["wf"], W["wl"], W["wfin"]
        set_gate(2, s, 0.5)
        hk = {}
        if s > 0:
            for i in range(8):
                def f(i=i, s=s, wl=wl, wfin=wfin):
                    final_store(wfin, s - 1, 8 + i, 9 + i)
                    if i == 0:
                        load_tokens(wl, x_d[s], L, xT, "xT", 8, 9, do_tr=False)
                    if i < 7:
                        load_tokens(wl, x_d[s], L, xT, "xT", 9 + i, 10 + i, do_tr=False)
                    load_tokens(wl, x_d[s], L, xT, "xT", 8 + i, 9 + i, do_dma=False)
                hk[("g", i)] = f
            hk[("g", 8)] = lambda wn=wn: norm_mod(wn, xT, "xT", L, hT, "hT", t0=2, t1=3)
            hk[("g", 9)] = lambda wn=wn: norm_mod(wn, xT, "xT", L, hT, "hT", t0=3, t1=4)
        ffn_block(wf, 0, 0, 1024, xT, "xT", hooks=hk)

        def hook_mix0(s=s, wn=wn):
            set_AS(V_NM, 3, 4, s)
            norm_mod(wn, xT, "xT", L, hT, "hT", t0=0, t1=1)
        hk = {("g", 2): hook_mix0, ("g", 3): lambda wn=wn: norm_mod(wn, xT, "xT", L, hT, "hT", t0=1, t1=2)}
        ffn_block(wf, 0, 1024, 1024, xT, "xT", hooks=hk)
        norm_mod(wn, xT, "xT", L, hT, "hT", t0=2, t1=4)
        dump("x1", xT[:, 0, :], ("xT", 3))
        set_gate(5, s, 1.0)
        A.release(m, fence)
        A.reset_top()
        if "conf" in stages:
            conformer()
            dump("x2a", xT[:, 0, :], ("xT", 3))
        set_AS(V_N2, 6, 7, s)

        def after_xt(t, wn_):
            norm_mod(wn_, xT, "xT", L, hT, "hT", t0=t, t1=t + 1)
        if "ssd" in stages:
            ssd(s, after_xt)
        else:
            mt = A.mark()
            wn_ = ws_norm()
            norm_mod(wn_, xT, "xT", L, hT, "hT")
            A.release(mt, fence)
        dump("x2", xT[:, 0, :], ("xT", 3))
        m = A.mark()
        W = alloc_ffn_phase()
        wn, wf, wl, wfin = W["wn"], W["wf"], W["wl"], W["wfin"]
        set_gate(8, s, 0.5)
        ffn_block(wf, 1, 0, 1024, xT, "xT")
        nxt = s + 1 < nseq
        hk = {}
        for i in range(8):
            def f(i=i, s=s, wl=wl, wfin=wfin, nxt=nxt):
                final_store(wfin, s, i, i + 1)
                if nxt:
                    if i == 0:
                        load_tokens(wl, x_d[s + 1], L, xT, "xT", 0, 1, do_tr=False)
                    if i < 7:
                        load_tokens(wl, x_d[s + 1], L, xT, "xT", i + 1, i + 2, do_tr=False)
                    load_tokens(wl, x_d[s + 1], L, xT, "xT", i, i + 1, do_dma=False)
            hk[("g", i)] = f
        if nxt:
            def hook_n1(s=s, wn=wn):
                set_AS(V_N1, 0, 1, s + 1)
                norm_mod(wn, xT, "xT", L, hT, "hT", t0=0, t1=1)
            hk[("g", 8)] = hook_n1
            hk[("g", 9)] = lambda wn=wn: norm_mod(wn, xT, "xT", L, hT, "hT", t0=1, t1=2)
        ffn_block(wf, 1, 1024, 1024, xT, "xT", hooks=hk)
        if not nxt:
            final_store(wfin, s, 8, 16)
    A.release(m, fence)
    P.finalize()
    build.info = dict(nops=len(P.ops), nwaits=P.nwaits, sig=P.sig_totals, peak=A.peak - A.base, cap=A.top - A.base)
    return nc


build.stages = ("ctx", "ffn1", "mix", "conf", "ssd", "ffn2")
build.info = {}


def host_consts():
    ident = np.eye(128, dtype=np.float32)
    oh = np.zeros((128, 32, 128), np.float32)
    for ridx in range(32):
        r = (ridx // 16) * 32 + ridx % 16
        oh[r, ridx, :] = 1.0
        oh[r + 64, ridx, :] = 1.0
    s_ = np.arange(128)[:, None]
    l_ = np.arange(128)[None, :]
    neg = np.zeros((128, 2, 256), np.float32)
    mf = np.where(s_ > l_, NEGV, 0.0).astype(np.float32)
    mb = np.where(s_ < l_, NEGV, 0.0).astype(np.float32)
    neg[:, 0, :] = np.concatenate([mf, mf], axis=1)
    neg[:, 1, :] = np.concatenate([mb, mb], axis=1)
    return ident, oh, neg


def fm(v, n):
    return np.ascontiguousarray(np.asarray(v, np.float32).reshape(n, 128).T)


def prep_inputs(inp, nseq=NSEQ, ncores=8):
    f32 = lambda a: np.ascontiguousarray(np.asarray(a, np.float32))
    ident, oh, neg = host_consts()
    vecs = np.zeros((128, 8, 8), np.float32)
    for i, k in enumerate(["norm_ffn1", "norm_mix", "norm_ffn2", "ssm_norm_w", "cconv_b", "cconv_ln_w", "cconv_ln_b"]):
        vecs[:, i, :] = fm(inp[k][0], 8)
    convwT = np.ascontiguousarray(np.asarray(inp["ssm_conv_w"][0], np.float32).T.reshape(12, 128, 5).transpose(1, 0, 2))
    convbT = fm(inp["ssm_conv_b"][0], 12)
    cconvwT = np.ascontiguousarray(np.asarray(inp["cconv_w"][0], np.float32).T.reshape(8, 128, 31).transpose(1, 0, 2))
    cols = np.zeros((128, 4), np.float32)
    for rep in range(2):
        for dr, (kb, ka) in enumerate([("dt_bias_fwd", "a_log_fwd"), ("dt_bias_bwd", "a_log_bwd")]):
            r0 = rep * 64 + dr * 32
            cols[r0:r0 + 16, 0] = np.asarray(inp[kb][0], np.float32)
            cols[r0:r0 + 16, 1] = np.asarray(inp[ka][0], np.float32)
            cols[r0:r0 + 32, 2] = 1.0 if dr == 0 else -1.0
            cols[r0:r0 + 32, 3] = 0.0 if dr == 0 else 1.0
    vecs[:, 7, :] = fm(np.repeat(np.asarray(inp["ssm_d"][0], np.float32), 64), 8)
    fin = np.ascontiguousarray(np.broadcast_to(np.asarray(inp["final_norm"], np.float32)[None, :], (128, D)))
    shared = {
        "w_mod": f32(inp["w_mod"][0]), "bmodT": fm(inp["b_mod"][0], 72), "vecs": vecs, "convwT": convwT,
        "convbT": convbT, "cconvwT": cconvwT, "cols": cols, "final_bc": fin, "ident": ident, "oh": oh,
        "neg": neg, "ffn1_gate": f32(inp["ffn1_gate"][0]), "ffn1_up": f32(inp["ffn1_up"][0]),
        "ffn1_down": f32(inp["ffn1_down"][0]), "ffn2_gate": f32(inp["ffn2_gate"][0]), "ffn2_up": f32(inp["ffn2_up"][0]),
        "ffn2_down": f32(inp["ffn2_down"][0]), "w_in": f32(inp["w_in"][0]), "w_out": f32(inp["w_out"][0]),
    }
    x = np.asarray(inp["x"], np.float32)
    ctx = np.asarray(inp["ctx"], np.float32)
    c = np.asarray(inp["c"], np.float32)
    c_ctx = np.asarray(inp["c_ctx"], np.float32)
    maps = []
    for k in range(ncores):
        b0 = k * nseq
        cin = np.zeros((128, 8, 5), np.float32)
        for j in range(nseq):
            cin[:, :, j] = fm(c[b0 + j], 8)
        cin[:, :, 4] = fm(c_ctx, 8)
        mp = dict(shared)
        mp["x"] = np.ascontiguousarray(x[b0:b0 + nseq])
        mp["ctx"] = np.ascontiguousarray(ctx[b0:b0 + nseq].reshape(nseq * CTXL, D))
        mp["cin"] = cin
        maps.append(mp)
    return maps


def kernel(**inputs):
    nc = build(NSEQ)
    maps = prep_inputs(inputs, NSEQ, 8)
    res = run_bass_kernel_spmd(nc, maps, core_ids=list(range(8)))
    out = np.concatenate([np.asarray(r["out"], np.float32) for r in res.results], axis=0)
    return out
```

```python
import numpy as np
import concourse.bass as bass
import concourse.mybir as mybir
from concourse.bass_utils import run_bass_kernel_spmd

F32, BF16 = mybir.dt.float32, mybir.dt.bfloat16
ALU = mybir.AluOpType
AF = mybir.ActivationFunctionType

PE, ACT, DVE, POOL, SP = "pe", "act", "dve", "pool", "sp"
COMPUTE = (PE, ACT, DVE, POOL)

D = 1024
L = 2048
CTXL = 256
NSEQ = 4
DFF = 2816
NFC = 22
EPS = 1e-6
OFF_X, OFF_B, OFF_C, OFF_DT, OFF_GLU = 1024, 2048, 2304, 2560, 2592
NEGV = -30000.0


class Op:
    __slots__ = ("eng", "fn", "reads", "writes", "is_dma", "deps", "signal", "sigval", "dsem", "dval",
                 "dprev", "barrier", "nobar")

    def __init__(self, eng, fn, reads, writes, is_dma, barrier=False):
        self.eng, self.fn, self.reads, self.writes, self.is_dma = eng, fn, reads, writes, is_dma
        self.deps = None
        self.signal = False
        self.sigval = 0
        self.dsem = None
        self.dval = 0
        self.dprev = None
        self.barrier = barrier
        self.nobar = False


class Prog:
    def __init__(self, nc):
        self.nc = nc
        self.ops = []
        self.n_dma_sems = {SP: 12, POOL: 8, ACT: 4}
        self._ps = 0
        self._ps_set = None

    def capture(self, fn, banks):
        saved, self.ops = self.ops, []
        sv_set, self._ps_set = self._ps_set, banks
        fn()
        out, self.ops = self.ops, saved
        self._ps_set = sv_set
        return out

    def merge(self, a, b):
        ia = ib = 0
        while ia < len(a) or ib < len(b):
            fa = ia / len(a) if a else 2.0
            fb = ib / len(b) if b else 2.0
            if ia < len(a) and (fa <= fb or ib >= len(b)):
                self.ops.append(a[ia])
                ia += 1
            else:
                self.ops.append(b[ib])
                ib += 1

    def op(self, eng, fn, reads=(), writes=()):
        self.ops.append(Op(eng, fn, tuple(reads), tuple(writes), False))

    def dma(self, eng, fn, reads=(), writes=(), nobar=False):
        o = Op(eng, fn, tuple(reads), tuple(writes), True)
        o.nobar = nobar
        self.ops.append(o)

    def barrier(self, fn):
        self.ops.append(Op(DVE, fn, (), (), False, barrier=True))

    def finalize(self):
        nc, ops = self.nc, self.ops
        last_writer, readers = {}, {}
        last_on = {}
        dmas_since = []
        cur_bar = None
        for o in ops:
            deps = set()
            if o.barrier:
                for e in COMPUTE:
                    if e in last_on:
                        deps.add(last_on[e])
                deps.update(dmas_since)
                dmas_since = []
                cur_bar = o
            else:
                if cur_bar is not None:
                    deps.add(cur_bar)
                for r in o.reads:
                    w = last_writer.get(r)
                    if w is not None:
                        deps.add(w)
                for w_ in o.writes:
                    w = last_writer.get(w_)
                    if w is not None:
                        deps.add(w)
                    deps.update(readers.get(w_, ()))
                for r in o.reads:
                    readers.setdefault(r, []).append(o)
                for w_ in o.writes:
                    last_writer[w_] = o
                    readers[w_] = []
            deps.discard(o)
            o.deps = deps
            if o.is_dma:
                if not o.nobar:
                    dmas_since.append(o)
            else:
                last_on[o.eng] = o
        dma_count, dma_hist = {}, {}
        for o in ops:
            if o.is_dma:
                n = self.n_dma_sems[o.eng]
                k = dma_count.get(o.eng, 0)
                dma_count[o.eng] = k + 1
                slot = k % n
                o.dsem = (o.eng, slot)
                o.dval = 16 * (k // n + 1)
                hist = dma_hist.setdefault((o.eng, slot), [])
                o.dprev = hist[-1] if hist else None
                hist.append(o)
        for o in ops:
            for d in o.deps:
                if d.is_dma:
                    continue
                if d.eng == PE and o.eng == PE and not o.is_dma:
                    continue
                d.signal = True
        cnt = {e: 0 for e in COMPUTE}
        for o in ops:
            if not o.is_dma and o.signal:
                cnt[o.eng] += 1
                o.sigval = cnt[o.eng]
        self.sig_totals = dict(cnt)
        sems = {e: nc.alloc_semaphore(name=f"s_{e}") for e in COMPUTE}
        dsems = {k: nc.alloc_semaphore(name=f"d_{k[0]}{k[1]}") for k in dma_hist}
        waited = {}
        nwaits = 0
        for o in ops:
            need = {}
            for d in o.deps:
                if d.is_dma:
                    key, val = ("d",) + d.dsem, d.dval
                else:
                    if d.eng == PE and o.eng == PE and not o.is_dma:
                        continue
                    key, val = ("c", d.eng), d.sigval
                if need.get(key, 0) < val:
                    need[key] = val
            if o.is_dma and o.dprev is not None:
                key, val = ("d",) + o.dsem, o.dprev.dval
                if need.get(key, 0) < val:
                    need[key] = val
            wl = []
            for key, val in need.items():
                wk = (o.eng, key)
                if waited.get(wk, 0) >= val:
                    continue
                waited[wk] = val
                wl.append((sems[key[1]] if key[0] == "c" else dsems[(key[1], key[2])], val))
                nwaits += 1
            o.deps = wl
        self.nwaits = nwaits
        final_d = [(dsems[k], h[-1].dval) for k, h in dma_hist.items()]

        def emit_engine(eng_name):
            def body(eng):
                for o in ops:
                    if o.eng != eng_name:
                        continue
                    for sem, val in o.deps:
                        eng.wait_ge(sem, val)
                    ins = o.fn(eng)
                    if o.is_dma:
                        ins.then_inc(dsems[o.dsem], 16)
                    elif o.signal:
                        ins.then_inc(sems[o.eng], 1)
                if eng_name == SP:
                    for sem, val in final_d:
                        eng.wait_ge(sem, val)
            return body

        with nc.Block() as block:
            block.sync(emit_engine(SP))
            block.tensor(emit_engine(PE))
            block.scalar(emit_engine(ACT))
            block.vector(emit_engine(DVE))
            block.gpsimd(emit_engine(POOL))


class Arena:
    def __init__(self, nc, P, base, top):
        self.nc, self.P = nc, P
        self.base = (base + 63) // 64 * 64
        self.top = top
        self.cur = self.base
        self.top_cur = top
        self.n = 0
        self.peak = 0

    def alloc(self, name, shape, dtype):
        esz = 4 if dtype == F32 else 2
        nbytes = esz
        for s in shape[1:]:
            nbytes *= s
        nbytes = (nbytes + 63) // 64 * 64
        off = self.cur
        self.cur += nbytes
        assert self.cur <= self.top_cur, f"SBUF overflow allocating {name}: {self.cur} > {self.top_cur}"
        self.peak = max(self.peak, self.cur)
        self.n += 1
        return self.nc.alloc_sbuf_tensor_at(f"{name}_{self.n}", list(shape), dtype, offset=off)

    def alloc_top(self, name, shape, dtype):
        esz = 4 if dtype == F32 else 2
        nbytes = esz
        for s_ in shape[1:]:
            nbytes *= s_
        nbytes = (nbytes + 63) // 64 * 64
        self.top_cur -= nbytes
        assert self.top_cur >= self.cur, f"SBUF overflow (top) allocating {name}"
        self.n += 1
        return self.nc.alloc_sbuf_tensor_at(f"{name}_{self.n}", list(shape), dtype, offset=self.top_cur)

    def reset_top(self):
        self.top_cur = self.top

    def mark(self):
        return self.cur

    def release(self, m, fence):
        self.cur = m
        self.P.barrier(lambda e: e.memset(fence[:], 0.0))


def build(nseq=NSEQ, debug=None):
    nc = bass.Bass("TRN2", target_bir_lowering=False)
    P = Prog(nc)
    dbg_out = {}

    def din(name, shape, dt=F32):
        return nc.dram_tensor(name, list(shape), dt, kind="ExternalInput").ap()

    def dscr(name, shape, dt=BF16):
        return nc.dram_tensor(name, list(shape), dt, kind="Internal").ap()

    x_d = din("x", [nseq, L, D])
    ctx_d = din("ctx", [nseq * CTXL, D])
    cin_d = din("cin", [128, 8, 5])
    wmod_d = din("w_mod", [D, 9 * D])
    bmod_d = din("bmodT", [128, 72])
    vec_d = din("vecs", [128, 8, 8])
    convw_d = din("convwT", [128, 12, 5])
    convb_d = din("convbT", [128, 12])
    cconvw_d = din("cconvwT", [128, 8, 31])
    col_d = din("cols", [128, 4])
    fin_d = din("final_bc", [128, D])
    ident_d = din("ident", [128, 128])
    oh_d = din("oh", [128, 32, 128])
    neg_d = din("neg", [128, 2, 256])
    wg_d = [din("ffn1_gate", [D, DFF]), din("ffn2_gate", [D, DFF])]
    wu_d = [din("ffn1_up", [D, DFF]), din("ffn2_up", [D, DFF])]
    wd_d = [din("ffn1_down", [DFF, D]), din("ffn2_down", [DFF, D])]
    win_d = din("w_in", [D, 4640])
    wout_d = din("w_out", [2 * D, D])
    out_d = nc.dram_tensor("out", [nseq, L, D], F32, kind="ExternalOutput").ap()

    wg_s = [dscr(f"wg_s{i}", [11, 128, 8, 256]) for i in range(2)]
    wu_s = [dscr(f"wu_s{i}", [11, 128, 8, 256]) for i in range(2)]
    wd_s = [dscr(f"wd_s{i}", [8, 128, NFC, 128]) for i in range(2)]
    win_cols = [128 * i for i in range(20)] + [OFF_GLU + 128 * i for i in range(16)]
    win_s = dscr("win_s", [36, 128, 8, 128])
    wout_s = dscr("wout_s", [8, 128, 16, 128])
    h0_s = dscr("h0_s", [nseq, 2, 128, D], F32)

    if debug:
        for nm, shp in debug.items():
            dbg_out[nm] = nc.dram_tensor("dbg_" + nm, list(shp), F32, kind="ExternalOutput").ap()

    A = Arena(nc, P, int(nc.sbuf_base) + 64, int(nc.sbuf_top))
    ident_f = A.alloc("ident_f", [128, 128], F32)
    ident_b = A.alloc("ident_b", [128, 128], BF16)
    ones_b = A.alloc("ones_b", [128, 128], BF16)
    ones_f = A.alloc("ones_f", [128, 128], F32)
    fence = A.alloc("fence", [128, 16], F32)
    modT = A.alloc("modT", [128, 72, 5], F32)
    vecs = A.alloc("vecs", [128, 8, 8], F32)
    convw = A.alloc("convw", [128, 12, 5], F32)
    convb = A.alloc("convb", [128, 12], F32)
    cconvw = A.alloc("cconvw", [128, 8, 31], F32)
    cols = A.alloc("cols", [128, 4], F32)
    a128 = A.alloc("a128", [128, 1], F32)
    epsc = A.alloc("epsc", [128, 1], F32)
    onec = A.alloc("onec", [128, 1], F32)
    wdt = A.alloc("wdt", [128, 8, 128], BF16)
    AS = A.alloc("AS", [128, 2, 8], F32)
    GV = A.alloc("GV", [128, 8], F32)
    xT = A.alloc("xT", [128, 8, L], F32)
    hT = A.alloc("hT", [128, 8, L], BF16)
    ps = [nc.alloc_psum_tensor(f"ps{i}", [128, 512], F32) for i in range(8)]

    def nps():
        st = P._ps_set
        if st is not None:
            i = st[P._ps % len(st)]
            P._ps += 1
            return i
        i = P._ps % 8
        P._ps = (i + 1) % 8
        return i

    def dump(name, src_ap, key):
        if debug and name in dbg_out:
            q = SP if src_ap.dtype == F32 else POOL
            P.dma(q, lambda e: e.dma_start(out=dbg_out[name], in_=src_ap), reads=[key], writes=["dbg_" + name])

    V_N1, V_NM, V_N2, V_SN, V_CB, V_LW, V_LB, V_D = range(8)

    P.dma(SP, lambda e: e.dma_start(out=ident_f[:], in_=ident_d), writes=["ident_f"])
    P.dma(SP, lambda e: e.dma_start(out=vecs[:], in_=vec_d), writes=["vecs"])
    P.dma(SP, lambda e: e.dma_start(out=convw[:], in_=convw_d), writes=["convw"])
    P.dma(SP, lambda e: e.dma_start(out=convb[:], in_=convb_d), writes=["convb"])
    P.dma(SP, lambda e: e.dma_start(out=cconvw[:], in_=cconvw_d), writes=["cconvw"])
    P.dma(SP, lambda e: e.dma_start(out=cols[:], in_=col_d), writes=["cols"])
    P.op(DVE, lambda e: e.memset(epsc[:], EPS), writes=["epsc"])
    P.op(DVE, lambda e: e.memset(onec[:], 1.0), writes=["onec"])
    P.op(DVE, lambda e: e.tensor_copy(out=ident_b[:], in_=ident_f[:]), reads=["ident_f"], writes=["ident_b"])
    P.op(DVE, lambda e: e.memset(ones_b[:], 1.0), writes=["ones_b"])
    P.op(DVE, lambda e: e.memset(ones_f[:], 1.0), writes=["ones_f"])
    P.op(DVE, lambda e: e.memset(fence[:], 0.0), writes=["fence"])
    P.op(ACT, lambda e: e.activation(out=a128[:], in_=cols[:, 1:2], func=AF.Exp), reads=["cols"], writes=["a128"])
    P.op(DVE, lambda e: e.tensor_scalar(out=a128[:], in0=a128[:], scalar1=-1.0, scalar2=None, op0=ALU.mult),
         reads=["a128"], writes=["a128"])
    P.op(POOL, lambda e: e.memset(wdt[:], 0.0), writes=["wdt"])
    win_v = win_d.rearrange("(kc kp) f -> kp kc f", kp=128)
    for rep in range(2):
        for dr in range(2):
            c0 = rep * 64 + dr * 32
            P.dma(POOL, lambda e, c0=c0, dr=dr: e.dma_start(out=wdt[:, :, c0:c0 + 16],
                                                           in_=win_v[:, :, OFF_DT + 16 * dr:OFF_DT + 16 * dr + 16]),
                  reads=["wdt"], writes=["wdt"])

    def cast_ffn(i):
        gv = wg_d[i].rearrange("(kc kp) (fq j) -> fq kp kc j", kp=128, j=256)
        uv = wu_d[i].rearrange("(kc kp) (fq j) -> fq kp kc j", kp=128, j=256)
        for fq in range(11):
            P.dma(POOL, lambda e, fq=fq: e.dma_start(out=wg_s[i][fq], in_=gv[fq]), writes=[("wg_s", i, fq)], nobar=True)
            P.dma(POOL, lambda e, fq=fq: e.dma_start(out=wu_s[i][fq], in_=uv[fq]), writes=[("wu_s", i, fq)], nobar=True)
        dv = wd_d[i].rearrange("(fc fp) (o j) -> o fp fc j", fp=128, j=128)
        for o in range(8):
            P.dma(POOL, lambda e, o=o: e.dma_start(out=wd_s[i][o], in_=dv[o]), writes=[("wd_s", i, o)], nobar=True)

    def cast_mixer():
        for t, c0 in enumerate(win_cols):
            P.dma(POOL, lambda e, t=t, c0=c0: e.dma_start(out=win_s[t], in_=win_v[:, :, c0:c0 + 128]),
                  writes=[("win_s", t)], nobar=True)
        ov = wout_d.rearrange("(kc kp) (o j) -> o kp kc j", kp=128, j=128)
        for o in range(8):
            P.dma(POOL, lambda e, o=o: e.dma_start(out=wout_s[o], in_=ov[o]), writes=[("wout_s", o)], nobar=True)


    m0 = A.mark()
    cin = A.alloc("cin", [128, 8, 5], F32)
    bmod = A.alloc("bmod", [128, 72], F32)
    wm = [A.alloc(f"wm{i}", [128, 8, 512], BF16) for i in range(4)]
    cinb = A.alloc("cinb", [128, 8, 5], BF16)
    P.dma(SP, lambda e: e.dma_start(out=cin[:], in_=cin_d), writes=["cin"])
    P.dma(SP, lambda e: e.dma_start(out=bmod[:], in_=bmod_d), writes=["bmod"])
    P.op(ACT, lambda e: e.activation(out=cinb[:], in_=cin[:], func=AF.Silu), reads=["cin"], writes=["cinb"])
    wmod_v = wmod_d.rearrange("(kc kp) f -> kp kc f", kp=128)
    pm = nps()

    def mod_piece(q):
        P.dma(POOL, lambda e, q=q: e.dma_start(out=wm[q % 4][:], in_=wmod_v[:, :, q * 512:(q + 1) * 512]),
              writes=[("wm", q % 4)])

        def mm(e, q=q):
            ins = None
            for j in range(4):
                col = (q * 4 + j) * 5
                for kc in range(8):
                    ins = e.matmul(ps[pm][:, col:col + 5], lhsT=wm[q % 4][:, kc, j * 128:(j + 1) * 128],
                                   rhs=cinb[:, kc, :], start=(kc == 0), stop=(kc == 7))
            return ins
        P.op(PE, mm, reads=[("wm", q % 4), "cinb"], writes=[("ps", pm)])
    for q in range(18):
        mod_piece(q)
    cast_ffn(0)
    cast_mixer()
    cast_ffn(1)
    P.op(DVE, lambda e: e.tensor_tensor(out=modT[:], in0=ps[pm][:, 0:360].rearrange("p (j b) -> p j b", b=5),
                                        in1=bmod[:].unsqueeze(2).to_broadcast([128, 72, 5]), op=ALU.add),
         reads=[("ps", pm), "bmod"], writes=["modT"])
    dump("modT", modT[:], "modT")
    A.release(m0, fence)

    def set_AS(norm_idx, j_shift, j_scale, col):
        def f(e):
            return e.scalar_tensor_tensor(out=AS[:, 0, :], in0=modT[:, 8 * j_scale:8 * j_scale + 8, col], scalar=1.0,
                                          in1=vecs[:, norm_idx, :], op0=ALU.add, op1=ALU.mult)
        P.op(DVE, f, reads=["modT", "vecs"], writes=["AS"])
        P.op(DVE, lambda e: e.tensor_copy(out=AS[:, 1, :], in_=modT[:, 8 * j_shift:8 * j_shift + 8, col]),
             reads=["modT"], writes=["AS"])

    def set_gate(j, col, mul):
        P.op(DVE, lambda e: e.tensor_scalar(out=GV[:], in0=modT[:, 8 * j:8 * j + 8, col], scalar1=mul, scalar2=None,
                                            op0=ALU.mult), reads=["modT"], writes=["GV"])

    def ws_load(top=False):
        al = A.alloc_top if top else A.alloc
        return dict(stg=[al(f"stg{i}", [128, D], F32) for i in range(2)])

    def load_tokens(ws, src, T, dstT, dkey, i0=0, i1=None, do_dma=True, do_tr=True):
        stg = ws["stg"]
        for i in range(i0, T // 128 if i1 is None else i1):
            s = stg[i % 2]
            if do_dma:
                P.dma(SP, lambda e, i=i, s=s: e.dma_start(out=s[:], in_=src[i * 128:(i + 1) * 128, :]),
                      writes=[("stg", i % 2)])
            if not do_tr:
                continue
            for half in range(2):
                pb = nps()

                def tr(e, s=s, half=half, pb=pb):
                    ins = None
                    for c in range(4):
                        cc = half * 4 + c
                        ins = e.transpose(out=ps[pb][:, c * 128:(c + 1) * 128], in_=s[:, cc * 128:(cc + 1) * 128],
                                          identity=ident_f[:])
                    return ins
                P.op(PE, tr, reads=[("stg", i % 2), "ident_f"], writes=[("ps", pb)])
                src_v = ps[pb][:].rearrange("p (c t) -> p c t", c=4)
                dst_v = dstT[:, half * 4:half * 4 + 4, i * 128:(i + 1) * 128]
                if half == 0:
                    P.op(ACT, lambda e, a=dst_v, b=src_v: e.activation(out=a, in_=b, func=AF.Copy),
                         reads=[("ps", pb)], writes=[(dkey, i // 4)])
                else:
                    P.op(DVE, lambda e, a=dst_v, b=src_v: e.tensor_copy(out=a, in_=b),
                         reads=[("ps", pb)], writes=[(dkey, i // 4)])

    def ws_norm(ntmp=2):
        tmp = [A.alloc(f"nt{i}", [128, 512], F32) for i in range(ntmp)]
        return dict(sq=[A.alloc(f"sq{i}", [128, 512], BF16) for i in range(2)], rs=A.alloc("rs", [128, 512], F32),
                    tmp=[tmp[i % ntmp] for i in range(2)], ntmp=ntmp)

    def norm_mod(ws, srcT, skey, T, dst, dkey, nfc=8, denom=1024.0, affine=True, vec_scale=None, t0=0, t1=None,
                 chunk_cb=None):
        sq, rs, tmp = ws["sq"], ws["rs"], ws["tmp"]
        for t in range(t0, T // 512 if t1 is None else t1):
            tb = slice(t * 512, (t + 1) * 512)
            pb = nps()
            for c in range(nfc):
                P.op(ACT, lambda e, c=c, tb=tb: e.activation(out=sq[c % 2][:], in_=srcT[:, c, tb], func=AF.Square),
                     reads=[(skey, t)], writes=[("sq", c % 2)])
                P.op(PE, lambda e, c=c, pb=pb: e.matmul(ps[pb][:], lhsT=ones_b[:], rhs=sq[c % 2][:], start=(c == 0),
                                                        stop=(c == nfc - 1)),
                     reads=[("sq", c % 2), "ones_b"], writes=[("ps", pb)])
            P.op(ACT, lambda e, pb=pb: e.activation(out=rs[:], in_=ps[pb][:], func=AF.Sqrt, bias=epsc[:, 0:1],
                                                    scale=1.0 / denom),
                 reads=[("ps", pb), "epsc"], writes=["rs"])
            P.op(DVE, lambda e: e.reciprocal(out=rs[:], in_=rs[:]), reads=["rs"], writes=["rs"])
            for c in range(nfc):
                P.op(DVE, lambda e, c=c, tb=tb: e.tensor_tensor(out=tmp[c % 2][:], in0=srcT[:, c, tb], in1=rs[:],
                                                                op=ALU.mult),
                     reads=[(skey, t), "rs"], writes=[("nt", c % ws["ntmp"])])
                if affine:
                    P.op(ACT, lambda e, c=c, tb=tb: e.activation(out=dst[:, c, tb], in_=tmp[c % 2][:], func=AF.Identity,
                                                                 scale=AS[:, 0, c:c + 1], bias=AS[:, 1, c:c + 1]),
                         reads=[("nt", c % ws["ntmp"]), "AS"], writes=[(dkey, t)])
                else:
                    P.op(ACT, lambda e, c=c, tb=tb: e.activation(out=dst[:, c, tb], in_=tmp[c % 2][:], func=AF.Identity,
                                                                 scale=vec_scale[:, c:c + 1]),
                         reads=[("nt", c % ws["ntmp"]), "vecs"], writes=[(dkey, t)])
                if chunk_cb is not None:
                    chunk_cb(t, c)

    def ws_ffn(TB):
        return dict(hid=A.alloc("hid", [128, NFC, TB], BF16),
                    wgb=[A.alloc(f"wgb{j}", [128, 8, 256], BF16) for j in range(2)],
                    wub=[A.alloc(f"wub{j}", [128, 8, 256], BF16) for j in range(2)],
                    wdb=[A.alloc(f"wdb{j}", [128, NFC, 128], BF16) for j in range(2)],
                    sg=[A.alloc(f"sg{j}", [128, 512], F32) for j in range(2)], k=[0])

    def ffn_block(ws, i, tok0, TB, resT, rkey, hooks=None):
        nt = TB // 512
        hid, wgb, wub, wdb, sg = ws["hid"], ws["wgb"], ws["wub"], ws["wdb"], ws["sg"]
        for fq in range(11):
            b = fq % 2
            P.dma(SP, lambda e, fq=fq, b=b: e.dma_start(out=wgb[b][:], in_=wg_s[i][fq]), reads=[("wg_s", i, fq)],
                  writes=[("wgb", b)])
            P.dma(SP, lambda e, fq=fq, b=b: e.dma_start(out=wub[b][:], in_=wu_s[i][fq]), reads=[("wu_s", i, fq)],
                  writes=[("wub", b)])
            for j in range(2):
                f = 2 * fq + j
                for t in range(nt):
                    tok = slice(tok0 + t * 512, tok0 + (t + 1) * 512)
                    pg, pu = nps(), nps()

                    def mm(e, b=b, j=j, tok=tok, pg=pg, pu=pu):
                        ins = None
                        for kc in range(8):
                            ins = e.matmul(ps[pg][:], lhsT=wgb[b][:, kc, j * 128:(j + 1) * 128], rhs=hT[:, kc, tok],
                                           start=(kc == 0), stop=(kc == 7))
                        for kc in range(8):
                            ins = e.matmul(ps[pu][:], lhsT=wub[b][:, kc, j * 128:(j + 1) * 128], rhs=hT[:, kc, tok],
                                           start=(kc == 0), stop=(kc == 7))
                        return ins
                    P.op(PE, mm, reads=[("wgb", b), ("wub", b), ("hT", (tok0 + t * 512) // 512)],
                         writes=[("ps", pg), ("ps", pu)])
                    kb = ws["k"][0] % 2
                    ws["k"][0] += 1
                    s = sg[kb]
                    P.op(ACT, lambda e, s=s, pg=pg: e.activation(out=s[:], in_=ps[pg][:], func=AF.Silu),
                         reads=[("ps", pg)], writes=[("sg", kb)])
                    P.op(DVE, lambda e, s=s, pu=pu, f=f, t=t: e.tensor_tensor(out=hid[:, f, t * 512:(t + 1) * 512],
                                                                              in0=ps[pu][:], in1=s[:], op=ALU.mult),
                         reads=[("ps", pu), ("sg", kb)], writes=[("hid", f, t)])
            if hooks and ("g", fq) in hooks:
                hooks[("g", fq)]()
        for o in range(8):
            b = o % 2
            P.dma(SP, lambda e, o=o, b=b: e.dma_start(out=wdb[b][:], in_=wd_s[i][o]), reads=[("wd_s", i, o)],
                  writes=[("wdb", b)])
            for t in range(nt):
                tok = slice(tok0 + t * 512, tok0 + (t + 1) * 512)
                pb = nps()

                def mm(e, b=b, t=t, pb=pb):
                    ins = None
                    for fc in range(NFC):
                        ins = e.matmul(ps[pb][:], lhsT=wdb[b][:, fc, :], rhs=hid[:, fc, t * 512:(t + 1) * 512],
                                       start=(fc == 0), stop=(fc == NFC - 1))
                    return ins
                P.op(PE, mm, reads=[("wdb", b)] + [("hid", fc, t) for fc in range(NFC)], writes=[("ps", pb)])
                rk = (rkey, (tok0 + t * 512) // 512)
                P.op(DVE, lambda e, o=o, tok=tok, pb=pb: e.scalar_tensor_tensor(
                    out=resT[:, o, tok], in0=ps[pb][:], scalar=GV[:, o:o + 1], in1=resT[:, o, tok], op0=ALU.mult,
                    op1=ALU.add), reads=[("ps", pb), "GV", rk], writes=[rk])
            if hooks and ("d", o) in hooks:
                hooks[("d", o)]()

    def ws_oproj(nkc):
        return dict(wt=A.alloc("wo", [128, 8, nkc, 128], BF16))

    def out_proj_load(ws, kc0, nkc):
        wt = ws["wt"]
        for o in range(8):
            P.dma(SP, lambda e, o=o: e.dma_start(out=wt[:, o], in_=wout_s[o][:, kc0:kc0 + nkc, :]),
                  reads=[("wout_s", o)], writes=[("wo", o)])

    def out_proj_t(ws, srcT, skey, nkc, t, banks=None, o0=0, o1=8):
        wt = ws["wt"]
        tok = slice(t * 512, (t + 1) * 512)
        for o in range(o0, o1):
            pb = nps() if banks is None else banks[o % len(banks)]

            def mm(e, o=o, pb=pb):
                ins = None
                for kc in range(nkc):
                    ins = e.matmul(ps[pb][:], lhsT=wt[:, o, kc, :], rhs=srcT[:, kc, tok], start=(kc == 0),
                                   stop=(kc == nkc - 1))
                return ins
            P.op(PE, mm, reads=[("wo", o), (skey, t)], writes=[("ps", pb)])
            P.op(DVE, lambda e, o=o, pb=pb: e.scalar_tensor_tensor(
                out=xT[:, o, tok], in0=ps[pb][:], scalar=GV[:, o:o + 1], in1=xT[:, o, tok], op0=ALU.mult,
                op1=ALU.add), reads=[("ps", pb), "GV", ("xT", t)], writes=[("xT", t)])

    def ws_proj(T, seglen):
        w = dict(wt=A.alloc("pw", [128, 8, 128], BF16), dg=A.alloc("pdg", [128, 5, 128], BF16),
                 xpad=A.alloc("xpad", [128, T // seglen, seglen + 4], BF16))
        P.op(DVE, lambda e: e.memset(w["xpad"][:], 0.0), writes=["xpad"])
        return w

    def proj_conv_silu(ws, wtile, cc, T, seglen, dst, dkey, hsrc, hkey):
        wt, dg, xpad = ws["wt"], ws["dg"], ws["xpad"]
        if ws.get("rezero"):
            P.op(DVE, lambda e: e.memset(xpad[:, :, 0:2], 0.0), reads=["xpad"], writes=["xpad"])
        P.dma(SP, lambda e: e.dma_start(out=wt[:], in_=win_s[wtile]), reads=[("win_s", wtile)], writes=["pw"])
        for k in range(5):
            P.op(DVE, lambda e, k=k: e.tensor_scalar(out=dg[:, k, :], in0=ident_b[:], scalar1=convw[:, cc, k:k + 1],
                                                     scalar2=None, op0=ALU.mult),
                 reads=["ident_b", "convw"], writes=[("pdg", k)])
        nb = T // 512
        spb = seglen // 512 if seglen >= 512 else 0
        for t in range(nb):
            pb = nps()

            def mm(e, t=t, pb=pb):
                ins = None
                for kc in range(8):
                    ins = e.matmul(ps[pb][:], lhsT=wt[:, kc, :], rhs=hsrc[:, kc, t * 512:(t + 1) * 512],
                                   start=(kc == 0), stop=(kc == 7))
                return ins
            P.op(PE, mm, reads=["pw", (hkey, t)], writes=[("ps", pb)])
            if seglen >= 512:
                sgi, off = t // spb, (t % spb) * 512
                dv = xpad[:, sgi, 2 + off:2 + off + 512]
                sv = ps[pb][:]
            else:
                ns = 512 // seglen
                dv = xpad[:, t * ns:(t + 1) * ns, 2:2 + seglen]
                sv = ps[pb][:].rearrange("p (s l) -> p s l", s=ns)
            P.op(ACT, lambda e, dv=dv, sv=sv: e.activation(out=dv, in_=sv, func=AF.Copy),
                 reads=[("ps", pb), "xpad"], writes=["xpad"])
        for t in range(nb):
            pb = nps()

            def cv(e, t=t, pb=pb):
                ins = None
                for k in range(5):
                    if seglen >= 512:
                        sgi, off = t // spb, (t % spb) * 512
                        rv = xpad[:, sgi, off + k:off + k + 512]
                    else:
                        ns = 512 // seglen
                        rv = xpad[:, t * ns:(t + 1) * ns, k:k + seglen]
                    ins = e.matmul(ps[pb][:], lhsT=dg[:, k, :], rhs=rv, start=(k == 0), stop=(k == 4))
                return ins
            P.op(PE, cv, reads=["xpad"] + [("pdg", k) for k in range(5)], writes=[("ps", pb)])
            P.op(ACT, lambda e, t=t, pb=pb: e.activation(out=dst[:, t * 512:(t + 1) * 512], in_=ps[pb][:], func=AF.Silu,
                                                         bias=convb[:, cc:cc + 1]),
                 reads=[("ps", pb), "convb"], writes=[(dkey, t)])

    def to_tok(srcT, skey, T, dst, dkey):
        for q in range(T // 512):
            pb = nps()
            pv = ps[pb][:].bitcast(BF16)

            def tr(e, q=q, pv=pv):
                ins = None
                for j in range(4):
                    i = q * 4 + j
                    ins = e.transpose(out=pv[:, j * 128:(j + 1) * 128], in_=srcT[:, i * 128:(i + 1) * 128],
                                      identity=ident_b[:])
                return ins
            P.op(PE, tr, reads=[(skey, q), "ident_b"], writes=[("ps", pb)])
            P.op(DVE, lambda e, q=q, pv=pv: e.tensor_copy(out=dst[:, q * 4:(q + 1) * 4, :],
                                                           in_=pv[:, 0:512].rearrange("p (j f) -> p j f", j=4)),
                 reads=[("ps", pb)], writes=[(dkey, q)])

    def ws_dt_out(T, CL, full):
        w = dict(w_tok=A.alloc("w_tok", [128, T // 128, 64], F32))
        if full:
            w.update(phiHL=A.alloc("phiHL", [128, T], BF16), npsiHL=A.alloc("npsiHL", [128, T], BF16),
                     dbc=A.alloc("dbc", [128, 64, T // CL], F32))
        return w

    def dt_prep(R, hsrc, hkey, T, CL, full, barrier=True):
        m = A.mark()
        dtT = A.alloc("dtT", [128, T], F32)
        G = A.alloc("G", [128, T], F32)
        phi = A.alloc("phi", [128, T], F32)
        tmp = A.alloc("dtmp", [128, T], F32)
        nch = T // CL
        Tt = A.alloc("Tt", [128, nch], F32)
        for t in range(T // 512):
            tb = slice(t * 512, (t + 1) * 512)
            pb = nps()

            def mm(e, tb=tb, pb=pb):
                ins = None
                for kc in range(8):
                    ins = e.matmul(ps[pb][:], lhsT=wdt[:, kc, :], rhs=hsrc[:, kc, tb], start=(kc == 0), stop=(kc == 7))
                return ins
            P.op(PE, mm, reads=["wdt", (hkey, t)], writes=[("ps", pb)])
            P.op(ACT, lambda e, tb=tb, pb=pb: e.activation(out=tmp[:, tb], in_=ps[pb][:], func=AF.Exp,
                                                           bias=cols[:, 0:1]),
                 reads=[("ps", pb), "cols"], writes=["dtmp"])
        P.op(ACT, lambda e: e.activation(out=dtT[:], in_=tmp[:], func=AF.Ln, bias=onec[:, 0:1]), reads=["dtmp", "onec"],
             writes=["dtT"])
        P.op(DVE, lambda e: e.tensor_scalar(out=tmp[:], in0=dtT[:], scalar1=a128[:, 0:1], scalar2=None, op0=ALU.mult),
             reads=["dtT", "a128", "dtmp"], writes=["dtmp"])
        for c in range(nch):
            for j in range(CL // 128):
                sl = slice(c * CL + j * 128, c * CL + (j + 1) * 128)
                init = 0.0 if j == 0 else G[:, c * CL + j * 128 - 1:c * CL + j * 128]
                P.op(DVE, lambda e, sl=sl, init=init: e.tensor_tensor_scan(out=G[:, sl], data0=ones_f[:], data1=tmp[:, sl],
                                                                            initial=init, op0=ALU.mult, op1=ALU.add),
                     reads=["dtmp", "ones_f", "G"], writes=["G"])
        Gv = G[:].rearrange("p (c l) -> p c l", l=CL)
        P.op(DVE, lambda e: e.tensor_copy(out=Tt[:], in_=Gv[:, :, CL - 1]), reads=["G"], writes=["Tt"])
        Tb = Tt[:].unsqueeze(2).to_broadcast([128, nch, CL])
        P.op(DVE, lambda e: e.tensor_tensor(out=phi[:].rearrange("p (c l) -> p c l", l=CL),
                                            in0=tmp[:].rearrange("p (c l) -> p c l", l=CL), in1=Tb, op=ALU.add),
             reads=["dtmp", "Tt"], writes=["phi"])
        P.op(DVE, lambda e: e.tensor_scalar(out=phi[:], in0=phi[:], scalar1=cols[:, 3:4], scalar2=None, op0=ALU.mult),
             reads=["phi", "cols"], writes=["phi"])
        P.op(DVE, lambda e: e.scalar_tensor_tensor(out=phi[:], in0=G[:], scalar=cols[:, 2:3], in1=phi[:], op0=ALU.mult,
                                                   op1=ALU.add), reads=["G", "phi", "cols"], writes=["phi"])
        P.op(DVE, lambda e: e.tensor_tensor(out=G[:].rearrange("p (c l) -> p c l", l=CL), in0=Tb,
                                            in1=phi[:].rearrange("p (c l) -> p c l", l=CL), op=ALU.subtract),
             reads=["Tt", "phi", "G"], writes=["G"])
        P.op(ACT, lambda e: e.activation(out=G[:], in_=G[:], func=AF.Exp), reads=["G"], writes=["G"])
        P.op(DVE, lambda e: e.tensor_tensor(out=G[:], in0=G[:], in1=dtT[:], op=ALU.mult), reads=["G", "dtT"],
             writes=["G"])
        w_tok = R["w_tok"]
        ntile = T // 128
        for q0 in range(0, ntile, 8):
            nq = min(8, ntile - q0)
            pb = nps()

            def tr(e, q0=q0, nq=nq, pb=pb):
                ins = None
                for j in range(nq):
                    i = q0 + j
                    ins = e.transpose(out=ps[pb][:, j * 64:(j + 1) * 64], in_=G[0:64, i * 128:(i + 1) * 128],
                                      identity=ident_f[0:64, 0:64])
                return ins
            P.op(PE, tr, reads=["G", "ident_f"], writes=[("ps", pb)])
            P.op(DVE, lambda e, q0=q0, nq=nq, pb=pb: e.tensor_copy(
                out=w_tok[:, q0:q0 + nq, :], in_=ps[pb][:, 0:nq * 64].rearrange("p (j f) -> p j f", j=nq)),
                reads=[("ps", pb)], writes=["w_tok"])
        if full:
            phiHL, npsiHL, dbc = R["phiHL"], R["npsiHL"], R["dbc"]
            hi = A.alloc("hi_all", [128, T], BF16)
            P.op(ACT, lambda e: e.activation(out=hi[:], in_=phi[:], func=AF.Copy), reads=["phi"], writes=["hi_all"])
            P.op(DVE, lambda e: e.tensor_copy(out=phiHL[0:64, :], in_=hi[0:64, :]), reads=["hi_all"], writes=["phiHL"])
            P.op(DVE, lambda e: e.tensor_tensor(out=phiHL[64:128, :], in0=phi[64:128, :], in1=hi[64:128, :],
                                                op=ALU.subtract), reads=["hi_all", "phi"], writes=["phiHL"])
            P.op(ACT, lambda e: e.activation(out=tmp[:], in_=dtT[:], func=AF.Ln), reads=["dtT", "dtmp"], writes=["dtmp"])
            P.op(DVE, lambda e: e.tensor_tensor(out=tmp[:], in0=tmp[:], in1=phi[:], op=ALU.subtract),
                 reads=["dtmp", "phi"], writes=["dtmp"])
            P.op(ACT, lambda e: e.activation(out=hi[:], in_=tmp[:], func=AF.Copy), reads=["dtmp", "phiHL"],
                 writes=["hi_all"])
            P.op(DVE, lambda e: e.tensor_copy(out=npsiHL[0:64, :], in_=hi[0:64, :]), reads=["hi_all"], writes=["npsiHL"])
            P.op(DVE, lambda e: e.tensor_tensor(out=npsiHL[64:128, :], in0=tmp[64:128, :], in1=hi[64:128, :],
                                                op=ALU.subtract), reads=["hi_all", "dtmp"], writes=["npsiHL"])
            eT = A.alloc("eT", [128, nch], F32)
            rd = A.alloc("rd", [64, 64, nch], F32)
            P.op(ACT, lambda e: e.activation(out=eT[:], in_=Tt[:], func=AF.Exp), reads=["Tt"], writes=["eT"])
            P.op(DVE, lambda e: e.tensor_tensor(out=rd[:], in0=eT[0:64, :].unsqueeze(1).to_broadcast([64, 64, nch]),
                                                in1=ident_f[0:64, 0:64].unsqueeze(2).to_broadcast([64, 64, nch]),
                                                op=ALU.mult), reads=["eT", "ident_f"], writes=["rd"])
            rdf = rd[:].rearrange("p r c -> p (r c)")
            dbf = dbc[:].rearrange("p r c -> p (r c)")
            tot = 64 * nch
            for h0 in range(0, tot, 512):
                pb = nps()
                P.op(PE, lambda e, h0=h0, pb=pb: e.matmul(ps[pb][:], lhsT=ones_f[0:64, :], rhs=rdf[:, h0:h0 + 512],
                                                          start=True, stop=True),
                     reads=["rd", "ones_f"], writes=[("ps", pb)])
                P.op(DVE, lambda e, h0=h0, pb=pb: e.tensor_copy(out=dbf[:, h0:h0 + 512], in_=ps[pb][:]),
                     reads=[("ps", pb)], writes=["dbc"])
        if barrier:
            A.release(m, fence)
        else:
            A.cur = m

    def ctx_phase():
        T = nseq * CTXL
        m = A.mark()
        wl = ws_load()
        wn = ws_norm()
        wf = ws_ffn(T)
        load_tokens(wl, ctx_d, T, xT, "xT")
        set_AS(V_N1, 0, 1, 4)
        norm_mod(wn, xT, "xT", T, hT, "hT")
        set_gate(2, 4, 0.5)
        ffn_block(wf, 0, 0, T, xT, "xT")
        set_AS(V_NM, 3, 4, 4)
        norm_mod(wn, xT, "xT", T, hT, "hT")
        dump("ctx_h", hT[:, 0, 0:T], ("hT", T // 512 - 1))
        A.release(m, fence)
        m = A.mark()
        R = ws_dt_out(T, CTXL, False)
        dt_prep(R, hT, "hT", T, CTXL, False)
        w_tok = R["w_tok"]
        wp = ws_proj(T, CTXL)
        BT = A.alloc("cBT", [128, 2, T], BF16)
        B_tok = A.alloc("cB_tok", [128, 2, T // 128, 128], BF16)
        for g in range(2):
            proj_conv_silu(wp, 16 + g, 8 + g, T, CTXL, BT[:, g, :], ("cBT", g), hT, "hT")
            to_tok(BT[:, g, :], ("cBT", g), T, B_tok[:, g], ("cB_tok", g))
        xcb = [A.alloc(f"cxh{i}", [128, T], BF16) for i in range(2)]
        xctb = [A.alloc(f"cxh_tok{i}", [128, T // 128, 128], BF16) for i in range(2)]
        xw = [A.alloc(f"cxw{i}", [128, 128], BF16) for i in range(2)]
        st = [A.alloc(f"cst{i}", [128, 512], F32) for i in range(2)]
        kk = 0

        def cprologue(hp_):
            bi = hp_ % 2
            proj_conv_silu(wp, 8 + hp_, hp_, T, CTXL, xcb[bi][:], ("cxh", bi), hT, "hT")
            to_tok(xcb[bi][:], ("cxh", bi), T, xctb[bi], ("cxh_tok", bi))
        cprologue(0)
        wl0 = ws_load()
        wn0 = ws_norm()
        set_AS(V_N1, 0, 1, 0)
        for hp in range(8):
            g = hp // 4
            xbi = hp % 2
            xc_tok = xctb[xbi]
            if hp + 1 < 8:
                cprologue(hp + 1)
            load_tokens(wl0, x_d[0], L, xT, "xT", 2 * hp, 2 * hp + 2)
            if hp in (6, 7):
                norm_mod(wn0, xT, "xT", L, hT, "hT", t0=hp - 4, t1=hp - 3)
            pb = None
            for s in range(nseq):
                if s % 2 == 0:
                    pb = nps()
                for dr in range(2):
                    col = (s % 2) * 256 + dr * 128
                    r0 = dr * 32 + 2 * hp
                    for j in range(2):
                        i = s * 2 + j
                        b = kk % 2
                        kk += 1
                        P.op(DVE, lambda e, i=i, r0=r0, b=b, xt_=xc_tok: e.tensor_tensor(
                            out=xw[b][:].rearrange("p (h q) -> p h q", h=2),
                            in0=xt_[:, i, :].rearrange("p (h q) -> p h q", h=2),
                            in1=w_tok[:, i, r0:r0 + 2].unsqueeze(2).to_broadcast([128, 2, 64]), op=ALU.mult),
                            reads=[(("cxh_tok", xbi), i // 4), "w_tok"], writes=[("cxw", b)])
                        P.op(PE, lambda e, i=i, g=g, b=b, j=j, pb=pb, col=col: e.matmul(
                            ps[pb][:, col:col + 128], lhsT=B_tok[:, g, i, :], rhs=xw[b][:], start=(j == 0), stop=(j == 1)),
                            reads=[("cB_tok", g), ("cxw", b)], writes=[("ps", pb)])
                if s % 2 == 1:
                    sb = st[(s // 2) % 2]
                    skey = ("cst", (s // 2) % 2)
                    P.op(DVE, lambda e, sb=sb, pb=pb: e.tensor_copy(out=sb[:], in_=ps[pb][:]), reads=[("ps", pb)],
                         writes=[skey])
                    for s2 in (s - 1, s):
                        for dr in range(2):
                            col = (s2 % 2) * 256 + dr * 128
                            P.dma(POOL, lambda e, s2=s2, dr=dr, col=col, sb=sb, hp=hp: e.dma_start(
                                out=h0_s[s2, dr, :, hp * 128:(hp + 1) * 128], in_=sb[:, col:col + 128]),
                                reads=[skey], writes=[("h0_s", s2)])
        A.release(m, fence)

    def conformer():
        m = A.mark()
        upad = [A.alloc(f"upad{i}", [128, 62 * 64], BF16) for i in range(2)]
        uconv = A.alloc("uconv", [128, 8, L], BF16)
        dg = [A.alloc(f"cdg{i}", [128, 31, 128], BF16) for i in range(2)]
        wv = [A.alloc(f"cwv{i}", [128, 8, 128], BF16) for i in range(2)]
        wg = [A.alloc(f"cwg{i}", [128, 8, 128], BF16) for i in range(2)]
        sgm = [A.alloc(f"csg{i}", [128, 512], F32) for i in range(2)]
        sq = [A.alloc(f"lsq{i}", [128, 512], BF16) for i in range(2)]
        mean1 = A.alloc("lmean", [128, 512], F32)
        rstd1 = A.alloc("lrstd", [128, 512], F32)
        mean, rstd = [mean1, mean1], [rstd1, rstd1]
        t1 = [A.alloc(f"lt{i}", [128, 512], F32) for i in range(3)]
        wo = ws_oproj(8)
        out_proj_load(wo, 8, 8)
        kk = 0
        for j in range(8):
            b = j % 2
            horiz = j < 4
            if j in (0, 1, 4, 5):
                P.op(DVE, lambda e, b=b: e.memset(upad[b][:], 0.0), reads=[("upad", b)], writes=[("upad", b)])
            P.dma(SP, lambda e, j=j, b=b: e.dma_start(out=wv[b][:], in_=win_s[20 + j]), reads=[("win_s", 20 + j)],
                  writes=[("cwv", b)])
            P.dma(SP, lambda e, j=j, b=b: e.dma_start(out=wg[b][:], in_=win_s[28 + j]), reads=[("win_s", 28 + j)],
                  writes=[("cwg", b)])
            for k in range(31):
                P.op(DVE, lambda e, k=k, j=j, b=b: e.tensor_scalar(out=dg[b][:, k, :], in0=ident_b[:],
                                                                  scalar1=cconvw[:, j, k:k + 1], scalar2=None,
                                                                  op0=ALU.mult),
                     reads=["ident_b", "cconvw"], writes=[("cdg", b)])
            if horiz:
                up3 = upad[b][:, 0:32 * 94].rearrange("p (r w) -> p r w", w=94)
            else:
                up3 = upad[b][:].rearrange("p (r w) -> p r w", w=64)
            for t in range(4):
                tb = slice(t * 512, (t + 1) * 512)
                pv, pg = nps(), nps()

                def mm(e, b=b, tb=tb, pv=pv, pg=pg):
                    ins = None
                    for kc in range(8):
                        ins = e.matmul(ps[pv][:], lhsT=wv[b][:, kc, :], rhs=hT[:, kc, tb], start=(kc == 0), stop=(kc == 7))
                    for kc in range(8):
                        ins = e.matmul(ps[pg][:], lhsT=wg[b][:, kc, :], rhs=hT[:, kc, tb], start=(kc == 0), stop=(kc == 7))
                    return ins
                P.op(PE, mm, reads=[("cwv", b), ("cwg", b), ("hT", t)], writes=[("ps", pv), ("ps", pg)])
                s = sgm[kk % 2]
                skey = ("csg", kk % 2)
                kk += 1
                P.op(ACT, lambda e, s=s, pg=pg: e.activation(out=s[:], in_=ps[pg][:], func=AF.Sigmoid),
                     reads=[("ps", pg)], writes=[skey])
                if horiz:
                    dv = up3[:, 8 * t:8 * t + 8, 15:79]
                else:
                    dv = up3[:, 15 + 8 * t:15 + 8 * t + 8, :]
                P.op(DVE, lambda e, s=s, pv=pv, dv=dv: e.tensor_tensor(
                    out=dv, in0=ps[pv][:].rearrange("p (r w) -> p r w", w=64),
                    in1=s[:].rearrange("p (r w) -> p r w", w=64), op=ALU.mult),
                    reads=[("ps", pv), skey, ("upad", b)], writes=[("upad", b)])
            for t in range(4):
                pb = nps()

                def cv(e, b=b, t=t, pb=pb, horiz=horiz, up3=up3):
                    ins = None
                    if horiz:
                        taps = list(range(31))
                    else:
                        taps = [k for k in range(31) if 8 * t + k + 7 >= 15 and 8 * t + k <= 46]
                    for n_, k in enumerate(taps):
                        if horiz:
                            rv = up3[:, 8 * t:8 * t + 8, k:k + 64]
                        else:
                            rv = up3[:, 8 * t + k:8 * t + k + 8, :]
                        ins = e.matmul(ps[pb][:], lhsT=dg[b][:, k, :], rhs=rv, start=(n_ == 0), stop=(n_ == len(taps) - 1))
                    return ins
                P.op(PE, cv, reads=[("upad", b), ("cdg", b)], writes=[("ps", pb)])
                P.op(ACT, lambda e, j=j, t=t, pb=pb: e.activation(out=uconv[:, j, t * 512:(t + 1) * 512], in_=ps[pb][:],
                                                                  func=AF.Identity, bias=vecs[:, V_CB, j:j + 1]),
                     reads=[("ps", pb), "vecs"], writes=[("uconv", t)])
        dump("uconv_pre", uconv[:, 0, :], ("uconv", 3))
        lnps = {}

        def ln_stats(t):
            tb = slice(t * 512, (t + 1) * 512)
            p_s, p_q = (4, 5) if t % 2 == 0 else (6, 7)
            lnps[t] = (p_s, p_q)
            for j in range(8):
                P.op(PE, lambda e, j=j, tb=tb, p_s=p_s: e.matmul(ps[p_s][:], lhsT=ones_b[:], rhs=uconv[:, j, tb],
                                                                 start=(j == 0), stop=(j == 7)),
                     reads=[("uconv", t), "ones_b"], writes=[("ps", p_s)])
                P.op(ACT, lambda e, j=j, tb=tb: e.activation(out=sq[j % 2][:], in_=uconv[:, j, tb], func=AF.Square),
                     reads=[("uconv", t)], writes=[("lsq", j % 2)])
                P.op(PE, lambda e, j=j, p_q=p_q: e.matmul(ps[p_q][:], lhsT=ones_b[:], rhs=sq[j % 2][:], start=(j == 0),
                                                          stop=(j == 7)),
                     reads=[("lsq", j % 2), "ones_b"], writes=[("ps", p_q)])

        def ln_norm(t):
            tb = slice(t * 512, (t + 1) * 512)
            p_s, p_q = lnps[t]
            mean_, rstd_ = mean[t % 2], rstd[t % 2]
            mk, rk = "lmean", "lrstd"
            P.op(DVE, lambda e: e.tensor_scalar(out=mean_[:], in0=ps[p_s][:], scalar1=1.0 / 1024, scalar2=None,
                                                op0=ALU.mult), reads=[("ps", p_s)], writes=[mk])
            P.op(DVE, lambda e: e.tensor_tensor(out=rstd_[:], in0=mean_[:], in1=mean_[:], op=ALU.mult), reads=[mk],
                 writes=[rk])
            P.op(DVE, lambda e: e.scalar_tensor_tensor(out=rstd_[:], in0=ps[p_q][:], scalar=1.0 / 1024,
                                                       in1=rstd_[:], op0=ALU.mult, op1=ALU.subtract),
                 reads=[("ps", p_q), rk], writes=[rk])
            P.op(ACT, lambda e: e.activation(out=rstd_[:], in_=rstd_[:], func=AF.Sqrt, bias=epsc[:, 0:1]),
                 reads=[rk, "epsc"], writes=[rk])
            P.op(DVE, lambda e: e.reciprocal(out=rstd_[:], in_=rstd_[:]), reads=[rk], writes=[rk])
            for j in range(8):
                tt = t1[j % 3]
                tk = ("lt", j % 3)
                P.op(DVE, lambda e, j=j, tt=tt: e.tensor_tensor(out=tt[:], in0=uconv[:, j, tb], in1=mean_[:],
                                                                op=ALU.subtract),
                     reads=[("uconv", t), mk], writes=[tk])
                P.op(DVE, lambda e, tt=tt: e.tensor_tensor(out=tt[:], in0=tt[:], in1=rstd_[:], op=ALU.mult),
                     reads=[tk, rk], writes=[tk])
                P.op(ACT, lambda e, j=j, tt=tt: e.activation(out=uconv[:, j, tb], in_=tt[:], func=AF.Silu,
                                                             scale=vecs[:, V_LW, j:j + 1],
                                                             bias=vecs[:, V_LB, j:j + 1]),
                     reads=[tk, "vecs"], writes=[("uconv", t)])
                if t >= 1:
                    out_proj_t(wo, uconv, "uconv", 8, t - 1, banks=[0, 1, 2, 3], o0=j, o1=j + 1)

        ln_stats(0)
        for t in range(4):
            if t + 1 < 4:
                ln_stats(t + 1)
            ln_norm(t)
        out_proj_t(wo, uconv, "uconv", 8, 3, banks=[0, 1, 2, 3])
        dump("uconv", uconv[:, 0, :], ("uconv", 3))
        A.release(m, fence)

    def ssd(s, after_xt=None):
        m = A.mark()
        OH = A.alloc("OH", [128, 32, 128], BF16)
        NEG = A.alloc("NEG", [128, 2, 256], BF16)
        P.dma(POOL, lambda e: e.dma_start(out=OH[:], in_=oh_d), writes=["OH"])
        P.dma(POOL, lambda e: e.dma_start(out=NEG[:], in_=neg_d), writes=["NEG"])
        R = ws_dt_out(L, 128, True)
        w_tok2, phiHL2, npsiHL2, dbc2 = R["w_tok"], R["phiHL"], R["npsiHL"], R["dbc"]
        m_ov = A.mark()
        yg = A.alloc("yg", [128, 4, L], BF16)
        hprev = A.alloc("hprev", [128, 2, 16, 128], BF16)
        S2 = [A.alloc(f"S{i}", [128, 2, 128], F32) for i in range(2)]
        xwb = [A.alloc(f"xwb{i}", [128, 8, 128], BF16)[:] for i in range(2)]
        eb = [A.alloc(f"eb{i}", [128, 512], BF16) for i in range(2)]
        GE = [A.alloc(f"GE{i}", [128, 512], BF16) for i in range(4)]
        szb = A.alloc("szb", [128, L], BF16)
        ytmp = A.alloc("ytmp", [128, 512], F32)
        m_ov_end = A.mark()
        wz = A.alloc("wz", [128, 8, 128], BF16)
        MB = A.alloc("MB", [128, 8, 512], BF16)
        B_tok = A.alloc("B_tok", [128, 16, 128], BF16)
        sv_cur, sv_peak = A.cur, A.peak
        A.cur = m_ov
        ops_dt = P.capture(lambda: dt_prep(R, hT, "hT", L, 128, True, barrier=False), [0, 1, 2, 3])
        assert A.peak <= max(sv_peak, m_ov_end), "dt temporaries exceed overlay region"
        A.cur = sv_cur
        wp = ws_proj(L, L)
        wp["rezero"] = True
        xpv = wp["xpad"]
        xwb = xwb + [xpv[:, 0, 0:1024].rearrange("p (c f) -> p c f", c=8),
                     xpv[:, 0, 1024:2048].rearrange("p (c f) -> p c f", c=8)]
        YB = [4, 5, 6, 7]
        EBANKS = [0, 1, 2, 3]
        ecnt = [0]
        for g in range(2):
            m2 = A.mark()
            BT = A.alloc("BT", [128, L], BF16)
            CT = A.alloc("CT", [128, L], BF16)

            def bc_stage(g=g, BT=BT, CT=CT):
                proj_conv_silu(wp, 16 + g, 8 + g, L, L, BT[:], "BT", hT, "hT")
                proj_conv_silu(wp, 18 + g, 10 + g, L, L, CT[:], "CT", hT, "hT")
                to_tok(BT[:], "BT", L, B_tok, "B_tok")
                for q in range(4):
                    pb = nps()

                    def bc(e, q=q, pb=pb):
                        ins = None
                        for j in range(4):
                            c = q * 4 + j
                            ins = e.matmul(ps[pb][:, j * 128:(j + 1) * 128], lhsT=BT[:, c * 128:(c + 1) * 128],
                                           rhs=CT[:, c * 128:(c + 1) * 128], start=True, stop=True)
                        return ins
                    P.op(PE, bc, reads=[("BT", q), ("CT", q)], writes=[("ps", pb)])
                    P.op(ACT, lambda e, q=q, pb=pb: e.activation(out=MB[:, 2 * q:2 * q + 2, 0:256],
                                                                 in_=ps[pb][:].rearrange("p (a x) -> p a x", a=2),
                                                                 func=AF.Copy),
                         reads=[("ps", pb)], writes=[("MB", q)])
                    P.op(DVE, lambda e, q=q: e.tensor_copy(out=MB[:, 2 * q:2 * q + 2, 256:512],
                                                           in_=CT[:, q * 512:(q + 1) * 512].rearrange("p (a x) -> p a x", a=2)),
                         reads=[("CT", q)], writes=[("MB", q)])
            if g == 0:
                ops_bc = P.capture(bc_stage, [4, 5, 6, 7])
                P.merge(ops_dt, ops_bc)
            else:
                bc_stage()
            if g == 0:
                dump("MB", MB[:].rearrange("p a b -> p (a b)"), ("MB", 3))
            A.release(m2, fence)
            m2 = A.mark()
            xhb = [A.alloc(f"xh{i}", [128, L], BF16) for i in range(2)]
            xtb = [A.alloc(f"xh_tok{i}", [128, 16, 128], BF16) for i in range(2)]

            def prologue(hq_):
                hp_ = 4 * g + hq_
                bi = hq_ % 2
                proj_conv_silu(wp, 8 + hp_, hp_, L, L, xhb[bi][:], ("xh", bi), hT, "hT")
                to_tok(xhb[bi][:], ("xh", bi), L, xtb[bi], ("xh_tok", bi))
            def rec_phase(hq, mid=None):
                hp = 4 * g + hq
                xbi = hq % 2
                xh, xh_tok = xhb[xbi], xtb[xbi]
                P.dma(SP, lambda e, hp=hp: e.dma_start(out=wz[:], in_=win_s[hp]), reads=[("win_s", hp)], writes=["wz"])
                for dr in range(2):
                    P.dma(SP, lambda e, dr=dr, hp=hp: e.dma_start(out=S2[0][:, dr, :], in_=h0_s[s, dr, :, hp * 128:(hp + 1) * 128]),
                          reads=[("h0_s", s)], writes=[("S", 0, dr, 0), ("S", 0, dr, 1)])
                orders = [list(range(16)), list(range(15, -1, -1))]

                def emit_cs(k):
                    for dr in range(2):
                        c = orders[dr][k]
                        i_ = 2 * k + dr
                        pb = i_ // 4
                        j = i_ % 4
                        buf = (0 if c < 8 else 2) if dr == 0 else (1 if c >= 8 else 3)
                        xv = xwb[buf]
                        rds = [("B_tok", c // 4), ("xwb", buf)] + (["xpad"] if buf >= 2 else [])
                        P.op(PE, lambda e, c=c, xv=xv, j=j, pb=pb: e.matmul(ps[pb][:, j * 128:(j + 1) * 128],
                                                                             lhsT=B_tok[:, c, :], rhs=xv[:, c % 8, :],
                                                                             start=True, stop=True),
                             reads=rds, writes=[("ps", pb)])

                for k in range(4):
                    emit_cs(k)
                if mid is not None:
                    mid()
                for k in range(4, 16):
                    emit_cs(k)
                for k in range(16):
                    Sc, Sn = S2[k % 2], S2[(k + 1) % 2]
                    P.op(ACT, lambda e, k=k, Sc=Sc: e.activation(out=hprev[:, :, k, :], in_=Sc[:, :, :], func=AF.Copy),
                         reads=[("S", k % 2, d_, h_) for d_ in range(2) for h_ in range(2)],
                         writes=[("hprev", 0, k), ("hprev", 1, 15 - k)])
                    for dr in range(2):
                        c = orders[dr][k]
                        i_ = 2 * k + dr
                        pb = i_ // 4
                        j = i_ % 4
                        r0 = dr * 32 + 2 * hp
                        if k < 15:
                            for hh in range(2):
                                hs = slice(hh * 64, (hh + 1) * 64)
                                P.op(DVE, lambda e, dr=dr, c=c, r0=r0, hh=hh, hs=hs, Sc=Sc, Sn=Sn, pb=pb, j=j:
                                     e.scalar_tensor_tensor(out=Sn[:, dr, hs], in0=Sc[:, dr, hs],
                                                            scalar=dbc2[:, r0 + hh, c:c + 1],
                                                            in1=ps[pb][:, j * 128 + hh * 64:j * 128 + (hh + 1) * 64],
                                                            op0=ALU.mult, op1=ALU.add),
                                     reads=[("S", k % 2, dr, hh), ("ps", pb), "dbc"], writes=[("S", (k + 1) % 2, dr, hh)])

            def main_loop(hq):
                hp = 4 * g + hq
                xbi = hq % 2
                xh_tok = xtb[xbi]
                def emit_ex_pair(b8, hh):
                    h = 2 * hp + hh
                    gs = []
                    for dr in range(2):
                        ridx = dr * 16 + h
                        pb = EBANKS[ecnt[0] % 4]
                        ebuf = ecnt[0] % 2
                        gbuf = ecnt[0] % 4
                        ecnt[0] += 1
                        cs = slice(b8 * 256, (b8 + 1) * 256)

                        def ex(e, ridx=ridx, dr=dr, pb=pb, cs=cs, b8=b8):
                            e.matmul(ps[pb][:, 0:256], lhsT=OH[:, ridx, :], rhs=phiHL2[:, cs], start=True, stop=False)
                            for j in range(2):
                                c = 2 * b8 + j
                                e.matmul(ps[pb][:, j * 128:(j + 1) * 128], lhsT=npsiHL2[:, c * 128:(c + 1) * 128],
                                         rhs=OH[:, ridx, :], start=False, stop=False)
                            e.matmul(ps[pb][:, 0:256], lhsT=ident_b[:], rhs=NEG[:, dr, :], start=False, stop=True)
                            return e.matmul(ps[pb][:, 256:512], lhsT=OH[:, ridx, :], rhs=phiHL2[:, cs], start=True,
                                            stop=True)
                        P.op(PE, ex, reads=["OH", "phiHL", "npsiHL", "NEG", "ident_b"], writes=[("ps", pb)])
                        P.op(ACT, lambda e, pb=pb, ebuf=ebuf: e.activation(out=eb[ebuf][:], in_=ps[pb][:], func=AF.Exp),
                             reads=[("ps", pb)], writes=[("eb", ebuf)])
                        P.op(DVE, lambda e, ebuf=ebuf, gbuf=gbuf, b8=b8: e.tensor_tensor(out=GE[gbuf][:], in0=eb[ebuf][:],
                                                                                          in1=MB[:, b8, :], op=ALU.mult),
                             reads=[("eb", ebuf), ("MB", b8 // 2)], writes=[("GE", gbuf)])
                        gs.append(gbuf)
                    return gs

                def emit_ym(b8, hh, gs):
                    ybank = YB[b8 // 2]
                    gf, gb = gs

                    def ym(e, hh=hh, b8=b8, ybank=ybank, gf=gf, gb=gb, xt_=xh_tok):
                        ins = None
                        for j in range(2):
                            c = 2 * b8 + j
                            col = ((b8 % 2) * 2 + j) * 128
                            o = ps[ybank][hh * 64:(hh + 1) * 64, col:col + 128]
                            xl = xt_[:, c, hh * 64:(hh + 1) * 64]
                            e.matmul(o, lhsT=xl, rhs=GE[gf][:, j * 128:(j + 1) * 128], start=True, stop=False)
                            e.matmul(o, lhsT=hprev[:, 0, c, hh * 64:(hh + 1) * 64],
                                     rhs=GE[gf][:, 256 + j * 128:256 + (j + 1) * 128], start=False, stop=False)
                            e.matmul(o, lhsT=xl, rhs=GE[gb][:, j * 128:(j + 1) * 128], start=False, stop=False)
                            ins = e.matmul(o, lhsT=hprev[:, 1, 15 - c, hh * 64:(hh + 1) * 64],
                                           rhs=GE[gb][:, 256 + j * 128:256 + (j + 1) * 128], start=False, stop=True)
                        return ins
                    P.op(PE, ym, reads=[("GE", gf), ("GE", gb), (("xh_tok", xbi), b8 // 2)] +
                         [("hprev", d_, 2 * b8 + j_) for d_ in range(2) for j_ in range(2)],
                         writes=[("ps", ybank)])

                units = [(b8, hh) for b8 in range(8) for hh in range(2)]
                cur = emit_ex_pair(*units[0])
                for ui, (b8, hh) in enumerate(units):
                    nxt_ = emit_ex_pair(*units[ui + 1]) if ui + 1 < len(units) else None
                    emit_ym(b8, hh, cur)
                    cur = nxt_

            def zproj(hq):
                for t in range(4):
                    tb = slice(t * 512, (t + 1) * 512)
                    pb = EBANKS[t % 4]

                    def mm(e, tb=tb, pb=pb):
                        ins = None
                        for kc in range(8):
                            ins = e.matmul(ps[pb][:], lhsT=wz[:, kc, :], rhs=hT[:, kc, tb], start=(kc == 0), stop=(kc == 7))
                        return ins
                    P.op(PE, mm, reads=["wz", ("hT", t)], writes=[("ps", pb)])
                    P.op(ACT, lambda e, tb=tb, pb=pb: e.activation(out=szb[:, tb], in_=ps[pb][:], func=AF.Silu),
                         reads=[("ps", pb)], writes=[("szb", t)])

            def gating(hq):
                hp = 4 * g + hq
                xbi = hq % 2
                xh = xhb[xbi]
                for t in range(4):
                    tb = slice(t * 512, (t + 1) * 512)
                    P.op(DVE, lambda e, t=t, tb=tb, hp=hp, xh_=xh: e.scalar_tensor_tensor(
                        out=ytmp[:], in0=xh_[:, tb], scalar=vecs[:, V_D, hp:hp + 1], in1=ps[YB[t]][:], op0=ALU.mult,
                        op1=ALU.add), reads=[("ps", YB[t]), (("xh", xbi), t), "vecs"], writes=["ytmp"])
                    P.op(DVE, lambda e, t=t, tb=tb, hq=hq: e.tensor_tensor(out=yg[:, hq, tb], in0=ytmp[:],
                                                                           in1=szb[:, tb], op=ALU.mult),
                         reads=["ytmp", ("szb", t)], writes=[("yg", t)])

            def xw_all(hq):
                hp = 4 * g + hq
                xbi = hq % 2
                xt_ = xtb[xbi]
                for buf, (dr, half) in enumerate([(0, 0), (1, 1), (0, 1), (1, 0)]):
                    c0 = half * 8
                    r0 = dr * 32 + 2 * hp
                    xv = xwb[buf]
                    P.op(DVE, lambda e, c0=c0, r0=r0, xv=xv, xt_=xt_: e.tensor_tensor(
                        out=xv.rearrange("p c (h q) -> p c h q", h=2),
                        in0=xt_[:, c0:c0 + 8, :].rearrange("p c (h q) -> p c h q", h=2),
                        in1=w_tok2[:, c0:c0 + 8, r0:r0 + 2].unsqueeze(3).to_broadcast([128, 8, 2, 64]), op=ALU.mult),
                        reads=[(("xh_tok", xbi), c0 // 4), (("xh_tok", xbi), c0 // 4 + 1), "w_tok"] +
                        (["xpad"] if buf >= 2 else []),
                        writes=[("xwb", buf)] + (["xpad"] if buf >= 2 else []))

            prologue(0)
            xw_all(0)
            rec_phase(0)
            prologue(1)
            zproj(0)
            for hq in range(4):
                if hq + 1 < 4:
                    xw_all(hq + 1)
                main_loop(hq)
                if hq + 1 < 4:
                    rec_phase(hq + 1, mid=lambda hq=hq: gating(hq))
                    if hq + 2 < 4:
                        prologue(hq + 2)
                    zproj(hq + 1)
                else:
                    gating(hq)
            if g == 0:
                dump("yg", yg[:, 0, :], ("yg", 3))
            A.release(m2, fence)
            m3 = A.mark()
            wn = ws_norm(2)
            wo = ws_oproj(4)
            out_proj_load(wo, 4 * g, 4)
            for t in range(4):
                def ccb(t_, c_, wo=wo):
                    if t_ >= 1:
                        out_proj_t(wo, yg, "yg", 4, t_ - 1, o0=2 * c_, o1=2 * c_ + 2)
                norm_mod(wn, yg, "yg", L, yg, "yg", nfc=4, denom=512.0, affine=False,
                         vec_scale=vecs[:, V_SN, 4 * g:4 * g + 4], t0=t, t1=t + 1, chunk_cb=ccb)
                if g == 1 and after_xt is not None and t >= 2:
                    after_xt(t - 2, wn)
            out_proj_t(wo, yg, "yg", 4, 3)
            if g == 1 and after_xt is not None:
                after_xt(2, wn)
                after_xt(3, wn)
            A.release(m3, fence)
        A.release(m, fence)

    def ws_final(top=False):
        al = A.alloc_top if top else A.alloc
        w = dict(fin=al("fin", [128, D], F32), ot=[al(f"ot{i}", [128, D], F32) for i in range(2)],
                 junk=al("junk", [128, 512], BF16), ss=[al(f"fss{i}", [128, 2], F32) for i in range(2)])
        P.dma(SP, lambda e: e.dma_start(out=w["fin"][:], in_=fin_d), writes=["fin"])
        return w

    def final_store(ws, s, i0=0, i1=L // 128):
        fin, ot, junk, ss = ws["fin"], ws["ot"], ws["junk"], ws["ss"]
        for i in range(i0, i1):
            o_ = ot[i % 2]
            okey = ("ot", i % 2)
            sk = ("fss", i % 2)
            s_ = ss[i % 2]
            for half in range(2):
                pb = nps()

                def tr(e, i=i, half=half, pb=pb):
                    ins = None
                    for c in range(4):
                        cc = half * 4 + c
                        ins = e.transpose(out=ps[pb][:, c * 128:(c + 1) * 128], in_=xT[:, cc, i * 128:(i + 1) * 128],
                                          identity=ident_f[:])
                    return ins
                P.op(PE, tr, reads=[("xT", i // 4), "ident_f"], writes=[("ps", pb)])
                if half == 0:
                    P.op(ACT, lambda e, o_=o_, pb=pb: e.activation(out=o_[:, 0:512], in_=ps[pb][:], func=AF.Copy),
                         reads=[("ps", pb)], writes=[okey])
                else:
                    P.op(DVE, lambda e, o_=o_, pb=pb: e.tensor_copy(out=o_[:, 512:1024], in_=ps[pb][:]),
                         reads=[("ps", pb)], writes=[okey])
            for hf in range(2):
                P.op(ACT, lambda e, o_=o_, s_=s_, hf=hf: e.activation(out=junk[:], in_=o_[:, hf * 512:(hf + 1) * 512],
                                                                      func=AF.Square, accum_out=s_[:, hf:hf + 1]),
                     reads=[okey], writes=["junk", sk])
            P.op(DVE, lambda e, s_=s_: e.tensor_tensor(out=s_[:, 0:1], in0=s_[:, 0:1], in1=s_[:, 1:2], op=ALU.add),
                 reads=[sk], writes=[sk])
            P.op(ACT, lambda e, s_=s_: e.activation(out=s_[:, 0:1], in_=s_[:, 0:1], func=AF.Sqrt, bias=epsc[:, 0:1],
                                                    scale=1.0 / 1024), reads=[sk, "epsc"], writes=[sk])
            P.op(DVE, lambda e, s_=s_: e.reciprocal(out=s_[:, 0:1], in_=s_[:, 0:1]), reads=[sk], writes=[sk])
            P.op(DVE, lambda e, o_=o_, s_=s_: e.scalar_tensor_tensor(out=o_[:], in0=o_[:], scalar=s_[:, 0:1], in1=fin[:],
                                                                     op0=ALU.mult, op1=ALU.mult),
                 reads=[okey, sk, "fin"], writes=[okey])
            P.dma(POOL, lambda e, i=i, o_=o_: e.dma_start(out=out_d[s, i * 128:(i + 1) * 128, :], in_=o_[:]),
                  reads=[okey], writes=[("out", s, i)])

    stages = build.stages
    if "ctx" in stages:
        ctx_phase()

    def alloc_ffn_phase():
        A.reset_top()
        return dict(wl=ws_load(top=True), wfin=ws_final(top=True), wn=ws_norm(), wf=ws_ffn(1024))

    m = A.mark()
    W = alloc_ffn_phase()
    if "ctx" not in stages:
        load_tokens(W["wl"], x_d[0], L, xT, "xT")
        set_AS(V_N1, 0, 1, 0)
        norm_mod(W["wn"], xT, "xT", L, hT, "hT")
    else:
        norm_mod(W["wn"], xT, "xT", L, hT, "hT", t0=0, t1=2)
    for s in range(nseq):
        wn, wf, wl, wfin = W["wn"], W["wf"], W["wl"], W["wfin"]
        set_gate(2, s, 0.5)
        hk = {}
        if s > 0:
            for i in range(8):
                def f(i=i, s=s, wl=wl, wfin=wfin):
                    final_store(wfin, s - 1, 8 + i, 9 + i)
                    if i == 0:
                        load_tokens(wl, x_d[s], L, xT, "xT", 8, 9, do_tr=False)
                    if i < 7:
                        load_tokens(wl, x_d[s], L, xT, "xT", 9 + i, 10 + i, do_tr=False)
                    load_tokens(wl, x_d[s], L, xT, "xT", 8 + i, 9 + i, do_dma=False)
                hk[("g", i)] = f
            hk[("g", 8)] = lambda wn=wn: norm_mod(wn, xT, "xT", L, hT, "hT", t0=2, t1=3)
            hk[("g", 9)] = lambda wn=wn: norm_mod(wn, xT, "xT", L, hT, "hT", t0=3, t1=4)
        ffn_block(wf, 0, 0, 1024, xT, "xT", hooks=hk)

        def hook_mix0(s=s, wn=wn):
            set_AS(V_NM, 3, 4, s)
            norm_mod(wn, xT, "xT", L, hT, "hT", t0=0, t1=1)
        hk = {("g", 2): hook_mix0, ("g", 3): lambda wn=wn: norm_mod(wn, xT, "xT", L, hT, "hT", t0=1, t1=2)}
        ffn_block(wf, 0, 1024, 1024, xT, "xT", hooks=hk)
        norm_mod(wn, xT, "xT", L, hT, "hT", t0=2, t1=4)
        dump("x1", xT[:, 0, :], ("xT", 3))
        set_gate(5, s, 1.0)
        A.release(m, fence)
        A.reset_top()
        if "conf" in stages:
            conformer()
            dump("x2a", xT[:, 0, :], ("xT", 3))
        set_AS(V_N2, 6, 7, s)

        def after_xt(t, wn_):
            norm_mod(wn_, xT, "xT", L, hT, "hT", t0=t, t1=t + 1)
        if "ssd" in stages:
            ssd(s, after_xt)
        else:
            mt = A.mark()
            wn_ = ws_norm()
            norm_mod(wn_, xT, "xT", L, hT, "hT")
            A.release(mt, fence)
        dump("x2", xT[:, 0, :], ("xT", 3))
        m = A.mark()
        W = alloc_ffn_phase()
        wn, wf, wl, wfin = W["wn"], W["wf"], W["wl"], W["wfin"]
        set_gate(8, s, 0.5)
        ffn_block(wf, 1, 0, 1024, xT, "xT")
        nxt = s + 1 < nseq
        hk = {}
        for i in range(8):
            def f(i=i, s=s, wl=wl, wfin=wfin, nxt=nxt):
                final_store(wfin, s, i, i + 1)
                if nxt:
                    if i == 0:
                        load_tokens(wl, x_d[s + 1], L, xT, "xT", 0, 1, do_tr=False)
                    if i < 7:
                        load_tokens(wl, x_d[s + 1], L, xT, "xT", i + 1, i + 2, do_tr=False)
                    load_tokens(wl, x_d[s + 1], L, xT, "xT", i, i + 1, do_dma=False)
            hk[("g", i)] = f
        if nxt:
            def hook_n1(s=s, wn=wn):
                set_AS(V_N1, 0, 1, s + 1)
                norm_mod(wn, xT, "xT", L, hT, "hT", t0=0, t1=1)
            hk[("g", 8)] = hook_n1
            hk[("g", 9)] = lambda wn=wn: norm_mod(wn, xT, "xT", L, hT, "hT", t0=1, t1=2)
        ffn_block(wf, 1, 1024, 1024, xT, "xT", hooks=hk)
        if not nxt:
            final_store(wfin, s, 8, 16)
    A.release(m, fence)
    P.finalize()
    build.info = dict(nops=len(P.ops), nwaits=P.nwaits, sig=P.sig_totals, peak=A.peak - A.base, cap=A.top - A.base)
    return nc


build.stages = ("ctx", "ffn1", "mix", "conf", "ssd", "ffn2")
build.info = {}


def host_consts():
    ident = np.eye(128, dtype=np.float32)
    oh = np.zeros((128, 32, 128), np.float32)
    for ridx in range(32):
        r = (ridx // 16) * 32 + ridx % 16
        oh[r, ridx, :] = 1.0
        oh[r + 64, ridx, :] = 1.0
    s_ = np.arange(128)[:, None]
    l_ = np.arange(128)[None, :]
    neg = np.zeros((128, 2, 256), np.float32)
    mf = np.where(s_ > l_, NEGV, 0.0).astype(np.float32)
    mb = np.where(s_ < l_, NEGV, 0.0).astype(np.float32)
    neg[:, 0, :] = np.concatenate([mf, mf], axis=1)
    neg[:, 1, :] = np.concatenate([mb, mb], axis=1)
    return ident, oh, neg


def fm(v, n):
    return np.ascontiguousarray(np.asarray(v, np.float32).reshape(n, 128).T)


def prep_inputs(inp, nseq=NSEQ, ncores=8):
    f32 = lambda a: np.ascontiguousarray(np.asarray(a, np.float32))
    ident, oh, neg = host_consts()
    vecs = np.zeros((128, 8, 8), np.float32)
    for i, k in enumerate(["norm_ffn1", "norm_mix", "norm_ffn2", "ssm_norm_w", "cconv_b", "cconv_ln_w", "cconv_ln_b"]):
        vecs[:, i, :] = fm(inp[k][0], 8)
    convwT = np.ascontiguousarray(np.asarray(inp["ssm_conv_w"][0], np.float32).T.reshape(12, 128, 5).transpose(1, 0, 2))
    convbT = fm(inp["ssm_conv_b"][0], 12)
    cconvwT = np.ascontiguousarray(np.asarray(inp["cconv_w"][0], np.float32).T.reshape(8, 128, 31).transpose(1, 0, 2))
    cols = np.zeros((128, 4), np.float32)
    for rep in range(2):
        for dr, (kb, ka) in enumerate([("dt_bias_fwd", "a_log_fwd"), ("dt_bias_bwd", "a_log_bwd")]):
            r0 = rep * 64 + dr * 32
            cols[r0:r0 + 16, 0] = np.asarray(inp[kb][0], np.float32)
            cols[r0:r0 + 16, 1] = np.asarray(inp[ka][0], np.float32)
            cols[r0:r0 + 32, 2] = 1.0 if dr == 0 else -1.0
            cols[r0:r0 + 32, 3] = 0.0 if dr == 0 else 1.0
    vecs[:, 7, :] = fm(np.repeat(np.asarray(inp["ssm_d"][0], np.float32), 64), 8)
    fin = np.ascontiguousarray(np.broadcast_to(np.asarray(inp["final_norm"], np.float32)[None, :], (128, D)))
    shared = {
        "w_mod": f32(inp["w_mod"][0]), "bmodT": fm(inp["b_mod"][0], 72), "vecs": vecs, "convwT": convwT,
        "convbT": convbT, "cconvwT": cconvwT, "cols": cols, "final_bc": fin, "ident": ident, "oh": oh,
        "neg": neg, "ffn1_gate": f32(inp["ffn1_gate"][0]), "ffn1_up": f32(inp["ffn1_up"][0]),
        "ffn1_down": f32(inp["ffn1_down"][0]), "ffn2_gate": f32(inp["ffn2_gate"][0]), "ffn2_up": f32(inp["ffn2_up"][0]),
        "ffn2_down": f32(inp["ffn2_down"][0]), "w_in": f32(inp["w_in"][0]), "w_out": f32(inp["w_out"][0]),
    }
    x = np.asarray(inp["x"], np.float32)
    ctx = np.asarray(inp["ctx"], np.float32)
    c = np.asarray(inp["c"], np.float32)
    c_ctx = np.asarray(inp["c_ctx"], np.float32)
    maps = []
    for k in range(ncores):
        b0 = k * nseq
        cin = np.zeros((128, 8, 5), np.float32)
        for j in range(nseq):
            cin[:, :, j] = fm(c[b0 + j], 8)
        cin[:, :, 4] = fm(c_ctx, 8)
        mp = dict(shared)
        mp["x"] = np.ascontiguousarray(x[b0:b0 + nseq])
        mp["ctx"] = np.ascontiguousarray(ctx[b0:b0 + nseq].reshape(nseq * CTXL, D))
        mp["cin"] = cin
        maps.append(mp)
    return maps


def kernel(**inputs):
    nc = build(NSEQ)
    maps = prep_inputs(inputs, NSEQ, 8)
    res = run_bass_kernel_spmd(nc, maps, core_ids=list(range(8)))
    out = np.concatenate([np.asarray(r["out"], np.float32) for r in res.results], axis=0)
    return out
```
